# Optimizing a Trainium2 kernel written in Bass

```python
import math
import jax
import jax.numpy as jnp
from jax import lax
import numpy as np


D_MODEL = 2048
BATCH = 2
SEQ = 8192
DEPTH = 4
DEC_BATCH = 2
DEC_SEQ = 16384
PAST_LEN = 128

HEAD_DIM = 128
N_MIXERS = 4
HEADS_PER_MIXER = D_MODEL // HEAD_DIM // N_MIXERS
GROUP_WIDTH = HEADS_PER_MIXER * HEAD_DIM
MIX_WIDTH = N_MIXERS * GROUP_WIDTH
IN_WIDTH = 15 * GROUP_WIDTH + 4 * HEADS_PER_MIXER
D_FF = 4 * D_MODEL
NORM_EPS = 1e-6
ROPE_THETA = 500000.0
ROPE_DIM = HEAD_DIM // 4
DILATED_CONFIGS = ((128, 1), (512, 4), (2048, 16))
GRID_W = 64
NA_KH = 8
NA_KW = 16
HGRN_CHUNK = 64
DN_CHUNK = 64
DN_CONV = 5
MASK_VALUE = -1e30
LOG_FLOOR = 1e-30

kernel_name = 'hybrid_bidir_encoder_parallel_heads'

F32 = jnp.float32


def rms_norm(x, g):
    xf = x.astype(F32)
    y = xf * lax.rsqrt(jnp.mean(xf * xf, axis=-1, keepdims=True) + NORM_EPS)
    return (y * g.astype(F32)).astype(x.dtype)


def l2_normalize(x):
    xf = x.astype(F32)
    return xf * lax.rsqrt(jnp.sum(xf * xf, axis=-1, keepdims=True) + NORM_EPS)


def masked_exp(mask, logits):
    return jnp.where(mask, jnp.exp(jnp.where(mask, logits, 0.0)), 0.0)


def partial_rope(x, positions):
    half = ROPE_DIM // 2
    inv_freq = jnp.power(ROPE_THETA, -jnp.arange(half, dtype=F32) / half)
    ang = positions.astype(F32)[:, None] * inv_freq[None, :]
    cos = jnp.cos(ang)[None, :, None, :]
    sin = jnp.sin(ang)[None, :, None, :]
    xr = x[..., :ROPE_DIM].astype(F32)
    x1, x2 = xr[..., :half], xr[..., half:]
    rot = jnp.concatenate([x1 * cos - x2 * sin, x2 * cos + x1 * sin], axis=-1)
    return jnp.concatenate([rot.astype(x.dtype), x[..., ROPE_DIM:]], axis=-1)


def banded_window_attention(q, k, v, radius):
    lead, n, dh = q.shape[:-2], q.shape[-2], q.shape[-1]
    nd = len(lead)
    blk = radius
    nb = -(-n // blk)
    extra = nb * blk - n
    qb = jnp.pad(q, [(0, 0)] * nd + [(0, extra), (0, 0)]).reshape(*lead, nb, blk, dh)

    def key_blocks(t):
        tp = jnp.pad(t, [(0, 0)] * nd + [(blk, extra + blk), (0, 0)]).reshape(*lead, nb + 2, blk, dh)
        return jnp.concatenate([tp[..., j:j + nb, :, :] for j in range(3)], axis=-2)

    kb, vb = key_blocks(k), key_blocks(v)
    s = jnp.einsum('...qd,...kd->...qk', qb, kb, preferred_element_type=F32) * (dh ** -0.5)
    qi = (jnp.arange(nb) * blk)[:, None] + jnp.arange(blk)[None, :]
    ki = (jnp.arange(nb) * blk - blk)[:, None] + jnp.arange(3 * blk)[None, :]
    valid = ((jnp.abs(qi[:, :, None] - ki[:, None, :]) <= radius)
             & (ki[:, None, :] >= 0) & (ki[:, None, :] < n))
    s = jnp.where(valid, s, MASK_VALUE)
    m = jnp.max(s, axis=-1, keepdims=True)
    e = masked_exp(valid, s - m)
    den = jnp.sum(e, axis=-1, keepdims=True)
    o = jnp.einsum('...qk,...kd->...qd', (e / den).astype(v.dtype), vb)
    lse = (m + jnp.log(den))[..., 0]
    o = o.reshape(*lead, nb * blk, dh)[..., :n, :]
    lse = lse.reshape(*lead, nb * blk)[..., :n]
    return o, lse


def dilated_attention(q, k, v):
    B, L, H, dh = q.shape
    outs, lses = [], []
    for window, dil in DILATED_CONFIGS:
        radius = window // (2 * dil)

        def to_strided(t):
            return t.reshape(B, L // dil, dil, H, dh).transpose(0, 3, 2, 1, 4)

        o, lse = banded_window_attention(to_strided(q), to_strided(k), to_strided(v), radius)
        outs.append(o.transpose(0, 3, 2, 1, 4).reshape(B, L, H, dh).astype(F32))
        lses.append(lse.transpose(0, 3, 2, 1).reshape(B, L, H))
    w = jax.nn.softmax(jnp.stack(lses, axis=-1), axis=-1)
    return jnp.einsum('blhn,nblhd->blhd', w, jnp.stack(outs, axis=0))


def swa_mixer(q, k, v, q_norm, k_norm):
    B, L, _ = q.shape
    shp = (B, L, HEADS_PER_MIXER, HEAD_DIM)
    pos = jnp.arange(L)
    qh = partial_rope(rms_norm(q.reshape(shp), q_norm), pos)
    kh = partial_rope(rms_norm(k.reshape(shp), k_norm), pos)
    o = dilated_attention(qh, kh, v.reshape(shp))
    return o.reshape(B, L, GROUP_WIDTH)


def gla_chunk_scan(q, k, v, log_f):
    B, H, L, dk = q.shape
    dv = v.shape[-1]
    C = HGRN_CHUNK
    n = L // C

    def chunks(t):
        return jnp.moveaxis(t.reshape(B, H, n, C, t.shape[-1]), 2, 0)

    incl = jnp.tril(jnp.ones((C, C), bool))[:, :, None]

    def step(S, inp):
        q_c, k_c, v_c, g_c = inp
        b = jnp.cumsum(g_c, axis=-2)
        decay = masked_exp(incl, b[..., :, None, :] - b[..., None, :, :])
        a = jnp.einsum('bhik,bhjk,bhijk->bhij', q_c, k_c, decay)
        o = (jnp.einsum('bhik,bhkv->bhiv', q_c * jnp.exp(b), S)
             + jnp.einsum('bhij,bhjv->bhiv', a, v_c))
        b_last = b[..., -1:, :]
        S = (jnp.exp(b_last[..., 0, :])[..., None] * S
             + jnp.einsum('bhjk,bhjv->bhkv', k_c * jnp.exp(b_last - b), v_c))
        return S, o

    S0 = jnp.zeros((B, H, dk, dv), F32)
    _, o = lax.scan(step, S0, (chunks(q), chunks(k), chunks(v), chunks(log_f)))
    return jnp.moveaxis(o, 0, 2).reshape(B, H, L, dv)


def hgrn2_mixer(q, f_fwd, f_bwd, i, g, lb, norm_g):
    B, L, _ = q.shape
    H = HEADS_PER_MIXER

    def heads(t):
        return t.astype(F32).reshape(B, L, H, HEAD_DIM).transpose(0, 2, 1, 3)

    def flip(t):
        return jnp.flip(t, axis=2)

    qh = heads(jax.nn.silu(q))
    vh = heads(i)
    gates = []
    for d, f_in in enumerate((f_fwd, f_bwd)):
        lbd = lb[d]
        xf = f_in.astype(F32)
        f = lbd + (1.0 - lbd) * jax.nn.sigmoid(xf)
        one_minus_f = (1.0 - lbd) * jax.nn.sigmoid(-xf)
        log_f = jnp.log(jnp.maximum(f, LOG_FLOOR))
        gates.append((heads(one_minus_f), heads(log_f)))
    (k_f, g_f), (k_b, g_b) = gates
    o_fwd = gla_chunk_scan(qh, k_f, vh, g_f)
    o_bwd = flip(gla_chunk_scan(flip(qh), flip(k_b), flip(vh), flip(g_b)))
    o = (o_fwd + o_bwd).transpose(0, 2, 1, 3)
    o = rms_norm(o, norm_g) * jax.nn.silu(g.astype(F32).reshape(B, L, H, HEAD_DIM))
    return o.reshape(B, L, GROUP_WIDTH)


def neighbourhood_attention(q, k, v, rpb):
    B, L, H, dh = q.shape
    rows = L // GRID_W
    kh = min(NA_KH, rows)
    kw = NA_KW
    r = jnp.arange(rows)
    r0 = jnp.clip(r - kh // 2, 0, rows - kh)
    key_rows = r0[:, None] + jnp.arange(kh)[None, :]
    cq = jnp.arange(GRID_W)
    c0 = jnp.clip(cq - kw // 2, 0, GRID_W - kw)
    ck = jnp.arange(GRID_W)
    col_ok = (ck[None, :] >= c0[:, None]) & (ck[None, :] < c0[:, None] + kw)
    qg = q.reshape(B, rows, GRID_W, H, dh)
    kg = k.reshape(B, rows, GRID_W, H, dh)[:, key_rows]
    vg = v.reshape(B, rows, GRID_W, H, dh)[:, key_rows]
    s = jnp.einsum('brqhd,brawhd->bhrqaw', qg, kg, preferred_element_type=F32) * (dh ** -0.5)
    row_off = key_rows - r[:, None] + (NA_KH - 1)
    col_off = jnp.clip(ck[None, :] - cq[:, None], -(kw - 1), kw - 1) + (NA_KW - 1)
    bias = rpb.astype(F32)[:, row_off]
    bias = bias[..., col_off].transpose(0, 1, 3, 2, 4)
    s = s + bias[None]
    s = jnp.where(col_ok[None, None, None, :, None, :], s, MASK_VALUE)
    p = jax.nn.softmax(s.reshape(B, H, rows, GRID_W, kh * GRID_W), axis=-1)
    p = p.reshape(B, H, rows, GRID_W, kh, GRID_W).astype(v.dtype)
    o = jnp.einsum('bhrqaw,brawhd->brqhd', p, vg)
    return o.reshape(B, L, H, dh)


def na_mixer(q, k, v, q_norm, k_norm, rpb):
    B, L, _ = q.shape
    shp = (B, L, HEADS_PER_MIXER, HEAD_DIM)
    o = neighbourhood_attention(rms_norm(q.reshape(shp), q_norm), rms_norm(k.reshape(shp), k_norm),
                                v.reshape(shp), rpb)
    return o.reshape(B, L, GROUP_WIDTH)


def centred_depthwise_conv(x, w):
    K = w.shape[0]
    return lax.conv_general_dilated(x, w[:, None, :].astype(x.dtype), window_strides=(1,),
                                    padding=[(K // 2, K // 2)], dimension_numbers=('NWC', 'WIO', 'NWC'),
                                    feature_group_count=x.shape[-1])


def gated_delta_chunk_scan(q, k, v, g, beta):
    B, H, L, dk = q.shape
    dv = v.shape[-1]
    C = DN_CHUNK
    n = L // C
    q, k, v = (t.reshape(B, H, n, C, t.shape[-1]) for t in (q, k, v))
    g = g.reshape(B, H, n, C)
    beta = beta.reshape(B, H, n, C)
    G = jnp.cumsum(g, axis=-1)
    incl = jnp.tril(jnp.ones((C, C), bool))
    strict = jnp.tril(jnp.ones((C, C), bool), -1)
    gamma = masked_exp(incl, G[..., :, None] - G[..., None, :])
    kb = k * beta[..., None]
    n_mat = jnp.where(strict, jnp.einsum('bhnid,bhnjd->bhnij', kb, k) * gamma, 0.0)
    t_mat = n_mat + jnp.eye(C, dtype=F32)
    rhs = jnp.concatenate([kb * jnp.exp(G)[..., None], v * beta[..., None]], axis=-1)
    sol = lax.linalg.triangular_solve(t_mat, rhs, left_side=True, lower=True, unit_diagonal=True)
    w_c, u_c = sol[..., :dk], sol[..., dk:]
    a_qk = jnp.einsum('bhnid,bhnjd->bhnij', q, k) * gamma
    q_dec = q * jnp.exp(G)[..., None]
    k_dec = k * jnp.exp(G[..., -1:] - G)[..., None]
    last = jnp.exp(G[..., -1])

    def step(S, inp):
        w_i, u_i, a_i, q_i, k_i, l_i = inp
        v_new = u_i - jnp.einsum('bhik,bhkv->bhiv', w_i, S)
        o = jnp.einsum('bhik,bhkv->bhiv', q_i, S) + jnp.einsum('bhij,bhjv->bhiv', a_i, v_new)
        S = l_i[..., None, None] * S + jnp.einsum('bhjk,bhjv->bhkv', k_i, v_new)
        return S, o

    def mv(t):
        return jnp.moveaxis(t, 2, 0)

    S0 = jnp.zeros((B, H, dk, dv), F32)
    _, o = lax.scan(step, S0, (mv(w_c), mv(u_c), mv(a_qk), mv(q_dec), mv(k_dec), mv(last)))
    return jnp.moveaxis(o, 0, 2).reshape(B, H, L, dv)


def gated_deltanet_mixer(qkv, z, a, b, conv_w, a_log, dt_bias, norm_g):
    B, L, _ = qkv.shape
    H = HEADS_PER_MIXER
    qkv = jax.nn.silu(centred_depthwise_conv(qkv, conv_w))
    q, k, v = jnp.split(qkv, 3, axis=-1)

    def heads(t):
        return t.astype(F32).reshape(B, L, H, HEAD_DIM).transpose(0, 2, 1, 3)

    def flip(t):
        return jnp.flip(t, axis=2)

    qh = l2_normalize(heads(q)) * (HEAD_DIM ** -0.5)
    kh = l2_normalize(heads(k))
    vh = heads(v)
    a = a.astype(F32).reshape(B, L, 2, H)
    b = b.astype(F32).reshape(B, L, 2, H)
    g = (-jnp.exp(a_log.astype(F32)) * jax.nn.softplus(a + dt_bias.astype(F32))).transpose(2, 0, 3, 1)
    beta = jax.nn.sigmoid(b).transpose(2, 0, 3, 1)
    o_fwd = gated_delta_chunk_scan(qh, kh, vh, g[0], beta[0])
    o_bwd = flip(gated_delta_chunk_scan(flip(qh), flip(kh), flip(vh), flip(g[1]), flip(beta[1])))
    o = (o_fwd + o_bwd).transpose(0, 2, 1, 3)
    o = rms_norm(o, norm_g) * jax.nn.silu(z.astype(F32).reshape(B, L, H, HEAD_DIM))
    return o.reshape(B, L, GROUP_WIDTH)


def encoder_layer(x, c, norm1_g, norm2_g, ada_w, ada_b, w_in, w_out, swa_q_norm, swa_k_norm,
                  hgrn_lb, hgrn_norm_g, na_q_norm, na_k_norm, na_rpb, dn_conv_w, dn_a_log,
                  dn_dt_bias, dn_norm_g, w_mlp_in, w_mlp_out):
    mod = jnp.einsum('bd,de->be', jax.nn.silu(c), ada_w) + ada_b
    shift1, scale1, gate1, shift2, scale2, gate2 = (t[:, None, :] for t in jnp.split(mod, 6, axis=-1))
    h = rms_norm(x, norm1_g) * (1 + scale1) + shift1
    u = jnp.einsum('bld,de->ble', h, w_in)
    G, H2 = GROUP_WIDTH, 2 * HEADS_PER_MIXER
    sizes = (G, G, G, G, G, G, G, G, G, G, G, 3 * G, G, H2, H2)
    parts, start = [], 0
    for s in sizes:
        parts.append(u[..., start:start + s])
        start += s
    (swa_q, swa_k, swa_v, hg_q, hg_ff, hg_fb, hg_i, hg_g,
     na_q, na_k, na_v, dn_qkv, dn_z, dn_a, dn_b) = parts
    y = jnp.concatenate([
        swa_mixer(swa_q, swa_k, swa_v, swa_q_norm, swa_k_norm),
        hgrn2_mixer(hg_q, hg_ff, hg_fb, hg_i, hg_g, hgrn_lb, hgrn_norm_g),
        na_mixer(na_q, na_k, na_v, na_q_norm, na_k_norm, na_rpb),
        gated_deltanet_mixer(dn_qkv, dn_z, dn_a, dn_b, dn_conv_w, dn_a_log, dn_dt_bias, dn_norm_g),
    ], axis=-1).astype(x.dtype)
    x = x + gate1 * jnp.einsum('ble,ed->bld', y, w_out)
    h = rms_norm(x, norm2_g) * (1 + scale2) + shift2
    hid = jnp.square(jax.nn.relu(jnp.einsum('bld,df->blf', h, w_mlp_in)))
    return x + gate2 * jnp.einsum('blf,fd->bld', hid, w_mlp_out)


def run_trunk(x, c, norm1_g, norm2_g, ada_w, ada_b, w_in, w_out, swa_q_norm, swa_k_norm,
              hgrn_lb_logits, hgrn_norm_g, na_q_norm, na_k_norm, na_rpb, dn_conv_w, dn_a_log,
              dn_dt_bias, dn_norm_g, w_mlp_in, w_mlp_out):
    p = jax.nn.softmax(hgrn_lb_logits.astype(F32), axis=0)
    lower_bounds = jnp.cumsum(p, axis=0) - p[0:1]
    for l in range(DEPTH):
        x = encoder_layer(x, c, norm1_g[l], norm2_g[l], ada_w[l], ada_b[l], w_in[l], w_out[l],
                          swa_q_norm[l], swa_k_norm[l], lower_bounds[l], hgrn_norm_g[l],
                          na_q_norm[l], na_k_norm[l], na_rpb[l], dn_conv_w[l], dn_a_log[l],
                          dn_dt_bias[l], dn_norm_g[l], w_mlp_in[l], w_mlp_out[l])
    return x


def setup_inputs(seed: int = 0) -> dict:
    key = jax.random.key(seed)
    ks = jax.random.split(key, 26)
    D = D_MODEL
    H = HEADS_PER_MIXER

    def nrm(k, shape, scale):
        return jax.random.normal(k, shape, F32) * scale

    dt = jnp.exp(jax.random.uniform(ks[19], (DEPTH, 2, H), F32, math.log(1e-3), math.log(1e-1)))
    return {
        'x_prompt': nrm(ks[0], (BATCH, SEQ, D), 1.0),
        'x_sample': nrm(ks[1], (DEC_BATCH, DEC_SEQ, D), 1.0),
        'c_prompt': nrm(ks[2], (BATCH, D), 1.0),
        'c_sample': nrm(ks[3], (DEC_BATCH, D), 1.0),
        'norm1_g': 1.0 + nrm(ks[4], (DEPTH, D), 0.02),
        'norm2_g': 1.0 + nrm(ks[5], (DEPTH, D), 0.02),
        'ada_w': nrm(ks[6], (DEPTH, D, 6 * D), 0.5 * D ** -0.5),
        'ada_b': nrm(ks[7], (DEPTH, 6 * D), 0.02),
        'w_in': nrm(ks[8], (DEPTH, D, IN_WIDTH), D ** -0.5),
        'w_out': nrm(ks[9], (DEPTH, MIX_WIDTH, D), MIX_WIDTH ** -0.5),
        'swa_q_norm': 1.0 + nrm(ks[10], (DEPTH, HEAD_DIM), 0.02),
        'swa_k_norm': 1.0 + nrm(ks[11], (DEPTH, HEAD_DIM), 0.02),
        'hgrn_lb_logits': nrm(ks[12], (DEPTH, 2, GROUP_WIDTH), 1.0),
        'hgrn_norm_g': 1.0 + nrm(ks[13], (DEPTH, HEAD_DIM), 0.02),
        'na_q_norm': 1.0 + nrm(ks[14], (DEPTH, HEAD_DIM), 0.02),
        'na_k_norm': 1.0 + nrm(ks[15], (DEPTH, HEAD_DIM), 0.02),
        'na_rpb': nrm(ks[16], (DEPTH, H, 2 * NA_KH - 1, 2 * NA_KW - 1), 0.5),
        'dn_conv_w': nrm(ks[17], (DEPTH, DN_CONV, 3 * GROUP_WIDTH), DN_CONV ** -0.5),
        'dn_a_log': jnp.log(jax.random.uniform(ks[18], (DEPTH, 2, H), F32, 1.0, 16.0)),
        'dn_dt_bias': dt + jnp.log(-jnp.expm1(-dt)),
        'dn_norm_g': 1.0 + nrm(ks[20], (DEPTH, HEAD_DIM), 0.02),
        'w_mlp_in': nrm(ks[21], (DEPTH, D, D_FF), D ** -0.5),
        'w_mlp_out': nrm(ks[22], (DEPTH, D_FF, D), D_FF ** -0.5),
    }


def reference(x_prompt, x_sample, c_prompt, c_sample, norm1_g, norm2_g, ada_w, ada_b, w_in, w_out,
              swa_q_norm, swa_k_norm, hgrn_lb_logits, hgrn_norm_g, na_q_norm, na_k_norm, na_rpb,
              dn_conv_w, dn_a_log, dn_dt_bias, dn_norm_g, w_mlp_in, w_mlp_out):
    y_prompt = run_trunk(x_prompt, c_prompt, norm1_g, norm2_g, ada_w, ada_b, w_in, w_out,
                         swa_q_norm, swa_k_norm, hgrn_lb_logits, hgrn_norm_g, na_q_norm, na_k_norm,
                         na_rpb, dn_conv_w, dn_a_log, dn_dt_bias, dn_norm_g, w_mlp_in, w_mlp_out)
    y_sample = run_trunk(x_sample, c_sample, norm1_g, norm2_g, ada_w, ada_b, w_in, w_out,
                         swa_q_norm, swa_k_norm, hgrn_lb_logits, hgrn_norm_g, na_q_norm, na_k_norm,
                         na_rpb, dn_conv_w, dn_a_log, dn_dt_bias, dn_norm_g, w_mlp_in, w_mlp_out)
    return (y_prompt, y_sample)
```

```python
from contextlib import ExitStack
import numpy as np
import ml_dtypes
import concourse.bass as bass
import concourse.mybir as mybir
from concourse.bass_utils import run_bass_kernel_spmd

F32 = mybir.dt.float32
BF16 = mybir.dt.bfloat16
AF = mybir.ActivationFunctionType
ALU = mybir.AluOpType
AX = mybir.AxisListType

D = 2048
NCH = 16
DFF = 8192
INW = 7696
DEPTH = 4
EPS = 1e-6
NCONST = 24

C_SWA_Q, C_SWA_K, C_SWA_V = 0, 512, 1024
C_HG_Q, C_HG_F1, C_HG_F2, C_HG_I, C_HG_G = 1536, 2048, 2560, 3072, 3584
C_NA_Q, C_NA_K, C_NA_V = 4096, 4608, 5120
C_DN_QKV, C_DN_Z, C_DN_AB = 5632, 7168, 7680

ENG_ATTR = {'pe': 'tensor', 'act': 'scalar', 'dve': 'vector', 'pool': 'gpsimd', 'sp': 'sync'}
SEM_ROT = 30000


class Buf:
    __slots__ = ('name', 'w', 'r')

    def __init__(self, name=''):
        self.name = name
        self.w = None
        self.r = []


class Op:
    __slots__ = ('eng', 'fn', 'deps', 'signal', 'sem', 'val', 'inc', 'dma', 'idx')

    def __init__(self, eng, fn, dma=False):
        self.eng = eng
        self.fn = fn
        self.deps = []
        self.signal = dma
        self.sem = None
        self.val = 0
        self.inc = 16 if dma else 1
        self.dma = dma


class Prog:
    def __init__(self, nc, n_dma_sems=20):
        self.nc = nc
        self.es = ExitStack()
        self.ops = {e: [] for e in ENG_ATTR}
        self.dma_ops = {e: [] for e in ENG_ATTR}
        self.n_dma_sems = n_dma_sems
        self.dma_sems = {}
        self.nops = 0
        self.phase_es = None

    def sem(self, name):
        return self.es.enter_context(self.nc.semaphore(name))

    def sbuf(self, name, shape, dt, perm=False):
        es = self.es if (perm or self.phase_es is None) else self.phase_es
        self.nsb = getattr(self, 'nsb', 0) + 1
        return es.enter_context(self.nc.sbuf_tensor(f"{name}_{self.nsb}", list(shape), dt))

    def psum(self, name, shape, dt):
        return self.es.enter_context(self.nc.psum_tensor(name, list(shape), dt))

    def begin_phase(self):
        self.barrier()
        if self.phase_es is not None:
            self.phase_es.close()
        self.phase_es = ExitStack()

    def add(self, eng, fn, reads=(), writes=(), dma=False, extra_deps=()):
        op = Op(eng, fn, dma)
        op.idx = self.nops
        self.nops += 1
        deps = []
        for b in reads:
            if b.w is not None:
                deps.append(b.w)
        for b in writes:
            if b.w is not None:
                deps.append(b.w)
            deps.extend(b.r)
        deps.extend(extra_deps)
        for b in reads:
            if not dma:
                b.r = [q for q in b.r if q.dma or q.eng != eng]
            b.r.append(op)
        for b in writes:
            b.w = op
            b.r = []
        seen = set()
        for p in deps:
            if p is op or id(p) in seen:
                continue
            seen.add(id(p))
            if (not p.dma) and (not dma) and p.eng == eng and eng == 'pe':
                continue
            op.deps.append(p)
            p.signal = True
        if dma:
            lst = self.dma_ops[eng]
            k = len(lst)
            if eng not in self.dma_sems:
                self.dma_sems[eng] = [self.sem(f"dq_{eng}_{i}") for i in range(self.n_dma_sems)]
            n = self.n_dma_sems
            op.sem = self.dma_sems[eng][k % n]
            op.val = 16 * (k // n + 1)
            if k >= n:
                op.deps.append(lst[k - n])
            lst.append(op)
        self.ops[eng].append(op)
        return op

    def dma(self, out, in_, reads=(), writes=(), eng='sp', **kw):
        return self.add(eng, lambda e: e.dma_start(out=out, in_=in_, **kw), reads, writes, dma=True)

    def _lasts(self):
        lasts = []
        for e, ops in self.ops.items():
            for op in reversed(ops):
                if not op.dma and op.fn is not None:
                    lasts.append(op)
                    break
        for e, lst in self.dma_ops.items():
            last = {}
            for op in lst:
                last[op.sem.num] = op
            lasts.extend(last.values())
        return lasts

    def barrier(self):
        lasts = self._lasts()
        for p in lasts:
            p.signal = True
        for e in ENG_ATTR:
            op = Op(e, None)
            op.deps = list(lasts)
            self.ops[e].append(op)

    def finalize(self):
        nc = self.nc
        self.barrier()
        for e, ops in self.ops.items():
            cnt = 0
            cur = None
            k = 0
            for op in ops:
                if op.dma or not op.signal:
                    continue
                if cur is None or cnt >= SEM_ROT:
                    cur = self.sem(f"cs_{e}_{k}")
                    k += 1
                    cnt = 0
                cnt += 1
                op.sem = cur
                op.val = cnt
        self.stats = {e: (sum(1 for o in ops if o.signal and not o.dma), len(ops), len(self.dma_ops[e])) for e, ops in self.ops.items()}
        with nc.Block() as block:
            for e, attr in ENG_ATTR.items():
                ops = self.ops[e]
                if not ops:
                    continue

                def body(eng, ops=ops):
                    waited = {}
                    for op in ops:
                        for p in op.deps:
                            s, v = p.sem, p.val
                            if waited.get(s.num, 0) >= v:
                                continue
                            eng.wait_ge(s, v)
                            waited[s.num] = v
                        if op.fn is None:
                            continue
                        ins = op.fn(eng)
                        if op.signal:
                            ins.then_inc(op.sem, op.inc)

                getattr(block, attr)(body)
        if self.phase_es is not None:
            self.phase_es.close()
        self.es.close()


def sl(start, n, step):
    return slice(start, start + (n - 1) * step + 1, step)


class Pool:
    def __init__(self, items):
        self.items = items
        self.i = 0

    def next(self):
        it = self.items[self.i % len(self.items)]
        self.i += 1
        return it


class Builder:
    def __init__(self, T, depth=DEPTH, debug=None, mixers=('swa', 'hg', 'na', 'dn'), couple=True, n_cores=8):
        self.T = T
        self.depth = depth
        self.debug = debug or ()
        self.mixers = mixers
        self.couple = couple
        nc = bass.Bass("TRN2", target_bir_lowering=False)
        nc.allow_low_precision("bf16 matmul operands, fp32 accumulation (reference tolerance is bf16-matmul)")
        self.nc = nc
        self.P = Prog(nc)
        L = depth
        dt = nc.dram_tensor
        self.xT_in = dt("xT", [D, T], F32, kind="ExternalInput").ap()
        self.cvec = dt("cvec", [128, NCH], F32, kind="ExternalInput").ap()
        self.ada_w = dt("ada_w", [L, D, 6 * D], F32, kind="ExternalInput").ap()
        self.ada_b = dt("ada_b", [L, 128, 96], F32, kind="ExternalInput").ap()
        self.ng = dt("ng", [L, 2, 128, NCH], F32, kind="ExternalInput").ap()
        self.w_in = dt("w_in", [L, D, INW], F32, kind="ExternalInput").ap()
        self.w_out = dt("w_out", [L, D, D], F32, kind="ExternalInput").ap()
        self.w1 = dt("w1", [L, D, DFF], F32, kind="ExternalInput").ap()
        self.w2 = dt("w2", [L, DFF, D], F32, kind="ExternalInput").ap()
        self.consts_in = dt("consts", [128, NCONST, 128], F32, kind="ExternalInput").ap()
        self.yT_out = dt("yT", [D, T], F32, kind="ExternalOutput").ap()
        self.xs = dt("xs", [D, T], F32).ap()
        self.w_in_b = dt("w_in_b", [L, D, INW], BF16).ap()
        self.w_out_b = dt("w_out_b", [L, D, D], BF16).ap()
        self.w1_b = dt("w1_b", [L, D, DFF], BF16).ap()
        self.w2_b = dt("w2_b", [L, NCH, 128, 64, 128], BF16).ap()
        self.Utok = dt("Utok", [T, INW], F32).ap()
        self.UT = dt("UT", [INW, T], F32).ap()
        self.yT = dt("yTs", [D, T], BF16).ap()
        self.dbufs = {}
        NT = T // 128
        self.NT = NT
        self.gains_in = dt("gains", [L, 4, 128, 512], F32, kind="ExternalInput").ap()
        self.rope_in = dt("rope", [128, NT, 2, 16], F32, kind="ExternalInput").ap()
        self.swa_mask_in = dt("swa_mask", [128, 4, 128], F32, kind="ExternalInput").ap()
        self.na_bias_in = dt("na_bias", [L, 4, 128, 24, 128], F32, kind="ExternalInput").ap()
        self.swa_qT = dt("swa_qT", [4, 128, T], BF16).ap()
        self.swa_kT = dt("swa_kT", [4, 128, T], BF16).ap()
        self.swa_V = dt("swa_V", [T, 512], BF16).ap()
        self.na_qT = dt("na_qT", [4, 128, T], BF16).ap()
        self.na_kT = dt("na_kT", [4, 128, T], BF16).ap()
        self.na_V = dt("na_V", [T, 512], BF16).ap()
        self.hg_lb_in = dt("hg_lb", [128, L, 2, 512], F32, kind="ExternalInput").ap()
        self.hg_lbT_in = dt("hg_lbT", [128, L, 2, 4], F32, kind="ExternalInput").ap()
        self.hg_gain_in = dt("hg_gain", [L, 128, 512], F32, kind="ExternalInput").ap()
        self.flag_in = dt("flag", [128, 2], F32, kind="ExternalInput").ap()
        self.lb_d = dt("lb_d", [L, 2, 2, 128, 512], F32).ap()
        self.lbT_d = dt("lbT_d", [L, 128, 2, 2, 4], F32).ap()
        self.o1_d = dt("o1_d", [T, 512], F32).ap()
        self.hg_state_d = dt("hg_state_d", [128, 4, 128], F32).ap()
        self.dn_conv_in = dt("dn_conv", [L, 128, 12, 5], F32, kind="ExternalInput").ap()
        self.dn_par_in = dt("dn_par", [L, 128, 2, 2, 4], F32, kind="ExternalInput").ap()
        self.dn_gain_in = dt("dn_gain", [L, 128, 512], F32, kind="ExternalInput").ap()
        self.dnc_d = dt("dnc_d", [T, 1536], BF16).ap()
        self.o1dn_d = dt("o1dn_d", [T, 512], F32).ap()
        self.dn_state_d = dt("dn_state_d", [128, 4, 128], F32).ap()
        self.h_dn_x = dt("h_dn_x", [1536, 2], F32).ap()
        self.HS = min(1024, T)
        self.XW = 8 * self.HS + 2048
        self.XWs = [4 * self.HS, 4 * self.HS, 2048]
        self.pubA = [dt(f"pubA{i}", [128, w], BF16).ap() for i, w in enumerate(self.XWs)]
        self.gathA = [dt(f"gathA{i}", [256, w], BF16).ap() for i, w in enumerate(self.XWs)]
        self.pubB = dt("pubB", [128, 24], F32).ap()
        self.gathB = dt("gathB", [256, 24], F32).ap()
        self.pubS = dt("pubS", [128, 512], F32).ap()
        self.gathS = dt("gathS", [256, 512], F32).ap()
        self.RG = [[2 * i, 2 * i + 1] for i in range(n_cores // 2)]
        self.h_swa_kT = dt("h_swa_kT", [4, 128, self.HS], BF16).ap()
        self.h_swa_V = dt("h_swa_V", [self.HS, 512], BF16).ap()
        self.h_na_kT = dt("h_na_kT", [4, 128, 256], BF16).ap()
        self.h_na_V = dt("h_na_V", [256, 512], BF16).ap()
        if 'UT' in self.debug:
            self.dbg_UT = dt("dbg_UT", [INW, T], F32, kind="ExternalOutput").ap()
            self.dbg_Utok = dt("dbg_Utok", [T, INW], F32, kind="ExternalOutput").ap()

    def dbuf(self, key):
        if key not in self.dbufs:
            self.dbufs[key] = Buf(str(key))
        return self.dbufs[key]

    def setup(self):
        P = self.P
        nc = self.nc
        self.psb = [P.psum(f"ps{i}", [128, 512], F32) for i in range(8)]
        self.psbuf = [Buf(f"ps{i}") for i in range(8)]
        self.ps_main = Pool([(self.psb[i], self.psbuf[i]) for i in range(0, 6)])
        self.ps_aux = Pool([(self.psb[i], self.psbuf[i]) for i in range(6, 8)])
        self.cf = P.sbuf("cf", [128, NCONST, 128], F32, perm=True)
        self.cb = P.sbuf("cb", [128, NCONST, 128], BF16, perm=True)
        self.b_cf = Buf("cf")
        self.b_cb = Buf("cb")
        P.dma(self.cf[:], self.consts_in, writes=[self.b_cf])
        P.add('dve', lambda e: e.tensor_copy(self.cb[:], self.cf[:]), reads=[self.b_cf], writes=[self.b_cb])
        self.ident_f = self.cf[:, 0, :]
        self.ident_b = self.cb[:, 0, :]
        self.ones_b = self.cb[:, 1, :]
        self.ones_f = self.cf[:, 1, :]
        L = self.depth
        self.modT = P.sbuf("modT", [128, L, 96], F32, perm=True)
        self.amod = P.sbuf("amod", [128, L, 2, NCH], F32, perm=True)
        self.b_mod = Buf("mod")

    def compute_mod(self):
        P = self.P
        L = self.depth
        P.begin_phase()
        cv = P.sbuf("cv", [128, NCH], F32)
        sc = P.sbuf("sc", [128, NCH], F32)
        adb = P.sbuf("adb", [128, L, 96], F32)
        ngt = P.sbuf("ngt", [128, L, 2, NCH], F32)
        b_cv, b_sc, b_adb, b_ng = Buf(), Buf(), Buf(), Buf()
        P.dma(cv[:], self.cvec, writes=[b_cv])
        P.dma(adb[:], self.ada_b.rearrange("l p j -> p l j"), writes=[b_adb])
        P.dma(ngt[:], self.ng.rearrange("l a p c -> p l a c"), writes=[b_ng])
        P.add('act', lambda e: e.activation(out=sc[:], in_=cv[:], func=AF.Silu), reads=[b_cv], writes=[b_sc])
        wts = [(P.sbuf(f"adw{i}", [128, NCH, 512], F32), Buf()) for i in range(2)]
        wp = Pool(wts)
        for l in range(L):
            ps, pb = self.ps_main.next()
            for blk in range(24):
                wt, wb = wp.next()
                src = self.ada_w[l, :, blk * 512:(blk + 1) * 512].rearrange("(c p) n -> p c n", p=128)
                P.dma(wt[:], src, writes=[wb])
                for j in range(4):
                    col = blk * 4 + j
                    for c in range(NCH):
                        P.add('pe', lambda e, wt=wt, c=c, j=j, col=col, ps=ps: e.matmul(
                            ps[:, col:col + 1], wt[:, c, j * 128:(j + 1) * 128], sc[:, c:c + 1],
                            start=(c == 0), stop=(c == NCH - 1)),
                            reads=[wb, b_sc], writes=[pb])
            P.add('dve', lambda e, l=l, ps=ps: e.tensor_tensor(out=self.modT[:, l, :], in0=ps[:, 0:96], in1=adb[:, l, :], op=ALU.add),
                  reads=[pb, b_adb], writes=[self.b_mod])
            for k, so in ((0, 16), (1, 64)):
                P.add('dve', lambda e, l=l, k=k, so=so: e.scalar_tensor_tensor(
                    out=self.amod[:, l, k, :], in0=self.modT[:, l, so:so + 16], scalar=1.0, in1=ngt[:, l, k, :],
                    op0=ALU.add, op1=ALU.mult), reads=[self.b_mod, b_ng], writes=[self.b_mod])

    def mod(self, l, which, c):
        if which == 'a1':
            return self.amod[:, l, 0, c:c + 1]
        if which == 'a2':
            return self.amod[:, l, 1, c:c + 1]
        off = {'b1': 0, 'g1': 32, 'b2': 48, 'g2': 80}[which]
        return self.modT[:, l, off + c:off + c + 1]

    def convert_weights(self):
        P = self.P
        P.begin_phase()
        L = self.depth
        st = [(P.sbuf(f"wcf{i}", [128, 8192], F32), Buf()) for i in range(2)]
        sb = [(P.sbuf(f"wcb{i}", [128, 8192], BF16), Buf()) for i in range(2)]
        fp, bp = Pool(st), Pool(sb)
        k = 0
        engs = ['dve', 'act', 'pool']
        for l in range(L):
            jobs = []
            for c in range(NCH):
                jobs.append((self.w_in[l, c * 128:(c + 1) * 128, :], self.w_in_b[l, c * 128:(c + 1) * 128, :], INW, None))
            for c in range(NCH):
                jobs.append((self.w_out[l, c * 128:(c + 1) * 128, :], self.w_out_b[l, c * 128:(c + 1) * 128, :], D, None))
            for c in range(NCH):
                jobs.append((self.w1[l, c * 128:(c + 1) * 128, :], self.w1_b[l, c * 128:(c + 1) * 128, :], DFF, None))
            for f in range(0, 64, 4):
                jobs.append((self.w2[l, f * 128:(f + 4) * 128, :], None, 4 * D, f))
            for (src, dst, n, f) in jobs:
                ft, fb = fp.next()
                bt, bb = bp.next()
                if f is None:
                    P.dma(ft[:, 0:n], src, writes=[fb])
                else:
                    P.dma(ft[:, 0:n].rearrange("p (a d) -> p a d", a=4), src.rearrange("(a p) d -> p a d", p=128), writes=[fb])
                eng = engs[k % 2]
                k += 1
                if eng == 'act':
                    P.add('act', lambda e, ft=ft, bt=bt, n=n: e.copy(bt[:, 0:n], ft[:, 0:n]), reads=[fb], writes=[bb])
                else:
                    P.add(eng, lambda e, ft=ft, bt=bt, n=n: e.tensor_copy(bt[:, 0:n], ft[:, 0:n]), reads=[fb], writes=[bb])
                if f is None:
                    P.dma(dst, bt[:, 0:n], reads=[bb], writes=[self.dbuf(('w', l))])
                else:
                    for a in range(4):
                        dstv = self.w2_b[l, :, :, f + a, :].rearrange("j p d -> p j d")
                        P.dma(dstv, bt[:, a * D:(a + 1) * D].rearrange("p (j d) -> p j d", j=NCH), reads=[bb], writes=[self.dbuf(('w', l))])

    def norm_mod(self, l, which, xg, b_xg, hT, b_hT, G, sq, b_sq, rstd, b_rstd, tmp, b_tmp):
        P = self.P
        a_key, b_key = ('a1', 'b1') if which == 1 else ('a2', 'b2')
        for c in range(NCH):
            eng = 'act' if c % 2 == 0 else 'pool'
            if eng == 'act':
                P.add('act', lambda e, c=c: e.activation(out=sq[:, c, :], in_=xg[:, c, :], func=AF.Square),
                      reads=[b_xg], writes=[b_sq[c]])
            else:
                P.add('pool', lambda e, c=c: e.tensor_tensor(out=sq[:, c, :], in0=xg[:, c, :], in1=xg[:, c, :], op=ALU.mult),
                      reads=[b_xg], writes=[b_sq[c]])
        for h in range(G // 512):
            ps, pb = self.ps_aux.next()
            for c in range(NCH):
                P.add('pe', lambda e, c=c, h=h, ps=ps: e.matmul(ps[:, :], self.ones_b, sq[:, c, h * 512:(h + 1) * 512],
                                                               start=(c == 0), stop=(c == NCH - 1)),
                      reads=[b_sq[c], self.b_cb], writes=[pb])
            P.add('dve', lambda e, h=h, ps=ps: e.tensor_scalar(out=rstd[:, h * 512:(h + 1) * 512], in0=ps[:, :], scalar1=1.0 / D, scalar2=EPS,
                                                              op0=ALU.mult, op1=ALU.add), reads=[pb], writes=[b_rstd])
        P.add('act', lambda e: e.activation(out=rstd[:, :], in_=rstd[:, :], func=AF.Sqrt), reads=[b_rstd], writes=[b_rstd])
        P.add('dve', lambda e: e.reciprocal(rstd[:, :], rstd[:, :]), reads=[b_rstd], writes=[b_rstd])
        for c in range(NCH):
            t, tb = tmp[c % len(tmp)], b_tmp[c % len(tmp)]
            P.add('dve', lambda e, c=c, t=t: e.tensor_tensor(out=t[:, :], in0=xg[:, c, :], in1=rstd[:, :], op=ALU.mult),
                  reads=[b_xg, b_rstd], writes=[tb])
            P.add('act', lambda e, c=c, t=t: e.activation(out=hT[:, c, :], in_=t[:, :], func=AF.Identity,
                                                          scale=self.mod(l, a_key, c), bias=self.mod(l, b_key, c)),
                  reads=[tb, self.b_mod], writes=[b_hT[c]])

    def tok_blocks(self):
        blocks = [(C_SWA_Q, 512), (C_SWA_K, 512), (C_SWA_V, 512), (C_HG_F1, 512), (C_HG_F2, 512), (C_HG_I, 512),
                  (C_HG_G, 512), (C_NA_Q, 512), (C_NA_K, 512), (C_NA_V, 512), (C_DN_Z, 512), (C_DN_AB, 16)]
        return blocks

    def fm_blocks(self):
        return [(C_HG_Q, 512), (C_HG_F1, 512), (C_HG_F2, 512), (C_DN_QKV, 512), (C_DN_QKV + 512, 512), (C_DN_QKV + 1024, 512)]

    def phase_A(self, l, x_src):
        P = self.P
        T = self.T
        G = 1024 if T % 1024 == 0 else 512
        P.begin_phase()
        xg = P.sbuf("A_xg", [128, NCH, G], F32)
        b_xg = Buf()
        hT = P.sbuf("A_hT", [128, NCH, G], BF16)
        b_hT = [Buf() for _ in range(NCH)]
        sq = P.sbuf("A_sq", [128, NCH, G], BF16)
        b_sq = [Buf() for _ in range(NCH)]
        rstd = P.sbuf("A_rstd", [128, G], F32)
        b_rstd = Buf()
        tmp = [P.sbuf(f"A_tmp{i}", [128, G], F32) for i in range(2)]
        b_tmp = [Buf() for _ in range(2)]
        wts = Pool([(P.sbuf(f"A_w{i}", [128, NCH, 512], BF16), Buf()) for i in range(2)])
        stg = Pool([(P.sbuf(f"A_st{i}", [128, 512], F32), Buf()) for i in range(4)])
        xv = x_src.rearrange("(c p) t -> p c t", p=128)
        evk = 0
        for g in range(T // G):
            t0 = g * G
            P.dma(xg[:], xv[:, :, t0:t0 + G], reads=[self.dbuf(('x', g * G // 512)), self.dbuf(('x', (g * G + G - 1) // 512))], writes=[b_xg])
            self.norm_mod(l, 1, xg, b_xg, hT, b_hT, G, sq, b_sq, rstd, b_rstd, tmp, b_tmp)
            for (c0, ncol) in self.fm_blocks():
                wt, wb = wts.next()
                P.dma(wt[:, :, 0:ncol], self.w_in_b[l, :, c0:c0 + ncol].rearrange("(c p) n -> p c n", p=128),
                      reads=[self.dbuf(('w', l))], writes=[wb])
                for j in range(ncol // 128):
                    for h in range(G // 512):
                        ps, pb = self.ps_main.next()
                        for c in range(NCH):
                            P.add('pe', lambda e, wt=wt, c=c, j=j, h=h, ps=ps: e.matmul(
                                ps[:, :], wt[:, c, j * 128:(j + 1) * 128], hT[:, c, h * 512:(h + 1) * 512],
                                start=(c == 0), stop=(c == NCH - 1)), reads=[wb, b_hT[c]], writes=[pb])
                        st, sb_ = stg.next()
                        evk += 1
                        if evk % 2:
                            P.add('act', lambda e, st=st, ps=ps: e.copy(st[:, :], ps[:, :]), reads=[pb], writes=[sb_])
                        else:
                            P.add('dve', lambda e, st=st, ps=ps: e.tensor_copy(st[:, :], ps[:, :]), reads=[pb], writes=[sb_])
                        r0 = c0 + j * 128
                        P.dma(self.UT[r0:r0 + 128, t0 + h * 512:t0 + (h + 1) * 512], st[:, :], reads=[sb_],
                              writes=[self.dbuf(('UT', l))])
            for (c0, ncol) in self.tok_blocks():
                wt, wb = wts.next()
                P.dma(wt[:, :, 0:ncol], self.w_in_b[l, :, c0:c0 + ncol].rearrange("(c p) n -> p c n", p=128),
                      reads=[self.dbuf(('w', l))], writes=[wb])
                for tt in range(G // 128):
                    ps, pb = self.ps_main.next()
                    for c in range(NCH):
                        P.add('pe', lambda e, wt=wt, c=c, tt=tt, ps=ps, ncol=ncol: e.matmul(
                            ps[:, 0:ncol], hT[:, c, tt * 128:(tt + 1) * 128], wt[:, c, 0:ncol],
                            start=(c == 0), stop=(c == NCH - 1)), reads=[wb, b_hT[c]], writes=[pb])
                    st, sb_ = stg.next()
                    evk += 1
                    if evk % 2:
                        P.add('act', lambda e, st=st, ps=ps, ncol=ncol: e.copy(st[:, 0:ncol], ps[:, 0:ncol]), reads=[pb], writes=[sb_])
                    else:
                        P.add('dve', lambda e, st=st, ps=ps, ncol=ncol: e.tensor_copy(st[:, 0:ncol], ps[:, 0:ncol]), reads=[pb], writes=[sb_])
                    P.dma(self.Utok[t0 + tt * 128:t0 + (tt + 1) * 128, c0:c0 + ncol], st[:, 0:ncol], reads=[sb_],
                          writes=[self.dbuf(('Utok', l))])

    def phase_C(self, l, x_src, x_dst):
        P = self.P
        T = self.T
        G = 512
        P.begin_phase()
        xg = P.sbuf("C_xg", [128, NCH, G], F32)
        b_xg = Buf()
        hT = P.sbuf("C_hT", [128, NCH, G], BF16)
        b_hT = [Buf() for _ in range(NCH)]
        yg = hT
        rstd = P.sbuf("C_rstd", [128, G], F32)
        b_rstd = Buf()
        tmp = [P.sbuf(f"C_tmp{i}", [128, G], F32) for i in range(2)]
        b_tmp = [Buf() for _ in range(2)]
        hid = P.sbuf("C_hid", [128, 64, G], BF16)
        b_hid = [Buf() for _ in range(64)]
        sq = hid
        b_sq = b_hid[0:NCH]
        rl = Pool([(P.sbuf(f"C_rl{i}", [128, G], BF16), Buf()) for i in range(3)])
        wts = Pool([(P.sbuf(f"C_w{i}", [128, NCH, 512], BF16), Buf()) for i in range(2)])
        w2s = Pool([(P.sbuf(f"C_w2{i}", [128, 64, 128], BF16), Buf()) for i in range(2)])
        xv = x_src.rearrange("(c p) t -> p c t", p=128)
        xo = x_dst.rearrange("(c p) t -> p c t", p=128)
        yv = self.yT.rearrange("(c p) t -> p c t", p=128)
        for g in range(T // G):
            t0 = g * G
            P.dma(xg[:], xv[:, :, t0:t0 + G], reads=[self.dbuf(('x', g))], writes=[b_xg])
            P.dma(yg[:], yv[:, :, t0:t0 + G], reads=[self.dbuf(('yT', l))], writes=b_hT)
            for blk in range(4):
                wt, wb = wts.next()
                P.dma(wt[:], self.w_out_b[l, :, blk * 512:(blk + 1) * 512].rearrange("(c p) n -> p c n", p=128),
                      reads=[self.dbuf(('w', l))], writes=[wb])
                for j4 in range(4):
                    j = blk * 4 + j4
                    ps, pb = self.ps_main.next()
                    for c in range(NCH):
                        P.add('pe', lambda e, wt=wt, c=c, j4=j4, ps=ps: e.matmul(
                            ps[:, :], wt[:, c, j4 * 128:(j4 + 1) * 128], yg[:, c, :], start=(c == 0), stop=(c == NCH - 1)),
                            reads=[wb, b_hT[c]], writes=[pb])
                    P.add('dve', lambda e, j=j, ps=ps: e.scalar_tensor_tensor(
                        out=xg[:, j, :], in0=ps[:, :], scalar=self.mod(l, 'g1', j), in1=xg[:, j, :], op0=ALU.mult, op1=ALU.add),
                        reads=[pb, self.b_mod, b_xg], writes=[b_xg])
            self.norm_mod(l, 2, xg, b_xg, hT, b_hT, G, sq, b_sq, rstd, b_rstd, tmp, b_tmp)
            for blk in range(16):
                wt, wb = wts.next()
                P.dma(wt[:], self.w1_b[l, :, blk * 512:(blk + 1) * 512].rearrange("(c p) n -> p c n", p=128),
                      reads=[self.dbuf(('w', l))], writes=[wb])
                for j4 in range(4):
                    f = blk * 4 + j4
                    ps, pb = self.ps_main.next()
                    for c in range(NCH):
                        P.add('pe', lambda e, wt=wt, c=c, j4=j4, ps=ps: e.matmul(
                            ps[:, :], wt[:, c, j4 * 128:(j4 + 1) * 128], hT[:, c, :], start=(c == 0), stop=(c == NCH - 1)),
                            reads=[wb, b_hT[c]], writes=[pb])
                    r, rb = rl.next()
                    P.add('act', lambda e, r=r, ps=ps: e.activation(out=r[:, :], in_=ps[:, :], func=AF.Relu), reads=[pb], writes=[rb])
                    eng = 'pool' if f % 3 == 0 else 'dve'
                    P.add(eng, lambda e, r=r, f=f: e.tensor_tensor(out=hid[:, f, :], in0=r[:, :], in1=r[:, :], op=ALU.mult),
                          reads=[rb], writes=[b_hid[f]])
            for j in range(NCH):
                wt, wb = w2s.next()
                P.dma(wt[:], self.w2_b[l, j], reads=[self.dbuf(('w', l))], writes=[wb])
                ps, pb = self.ps_main.next()
                for f in range(64):
                    P.add('pe', lambda e, wt=wt, f=f, ps=ps: e.matmul(ps[:, :], wt[:, f, :], hid[:, f, :], start=(f == 0), stop=(f == 63)),
                          reads=[wb, b_hid[f]], writes=[pb])
                P.add('dve', lambda e, j=j, ps=ps: e.scalar_tensor_tensor(
                    out=xg[:, j, :], in0=ps[:, :], scalar=self.mod(l, 'g2', j), in1=xg[:, j, :], op0=ALU.mult, op1=ALU.add),
                    reads=[pb, self.b_mod, b_xg], writes=[b_xg])
            P.dma(xo[:, :, t0:t0 + G], xg[:], reads=[b_xg], writes=[self.dbuf(('x', g))])


    def zero_halos(self):
        P = self.P
        P.begin_phase()
        z = P.sbuf("zt", [128, 4096], BF16)
        bz = Buf()
        P.add('pool', lambda e: e.memset(z[:], 0.0), writes=[bz])
        HS = self.HS
        P.dma(self.h_swa_kT.rearrange("h p t -> p h t"), z[:, 0:4 * HS].rearrange("p (h t) -> p h t", h=4), reads=[bz], writes=[self.dbuf('halo')])
        P.dma(self.h_swa_V.rearrange("(a p) n -> p a n", p=128), z[:, 0:(HS // 128) * 512].rearrange("p (a n) -> p a n", n=512), reads=[bz], writes=[self.dbuf('halo')])
        P.dma(self.h_na_kT.rearrange("h p t -> p h t"), z[:, 0:1024].rearrange("p (h t) -> p h t", h=4), reads=[bz], writes=[self.dbuf('halo')])
        P.dma(self.h_na_V.rearrange("(a p) n -> p a n", p=128), z[:, 0:1024].rearrange("p (a n) -> p a n", n=512), reads=[bz], writes=[self.dbuf('halo')])
        zf = P.sbuf("ztf", [128, 12, 2], F32)
        bzf = Buf()
        P.add('pool', lambda e: e.memset(zf[:], 0.0), writes=[bzf])
        P.dma(self.h_dn_x.rearrange("(c p) t -> p c t", p=128), zf[:], reads=[bzf], writes=[self.dbuf('halo')])

    def phase_B1(self, l):
        P = self.P
        T, NT = self.T, self.NT
        P.begin_phase()
        gains = P.sbuf("B_gains", [128, 4, 512], F32)
        b_g = Buf()
        rope = P.sbuf("B_rope", [128, NT, 2, 16], F32)
        b_rope = Buf()
        P.dma(gains[:], self.gains_in[l].rearrange("a p n -> p a n"), writes=[b_g])
        P.dma(rope[:], self.rope_in, writes=[b_rope])
        for a in (0, 2):
            P.add('dve', lambda e, a=a: e.tensor_scalar(out=gains[:, a, :], in0=gains[:, a, :], scalar1=128.0 ** -0.5, scalar2=None, op0=ALU.mult),
                  reads=[b_g], writes=[b_g])
        upool = Pool([(P.sbuf(f"B_u{i}", [128, 3072], F32), Buf()) for i in range(2)])
        sq = P.sbuf("B_sq", [128, 4, 512], F32)
        b_sq = [Buf() for _ in range(4)]
        ss = P.sbuf("B_ss", [128, 16], F32)
        b_ss = Buf()
        tq = [P.sbuf(f"B_t{i}", [128, 512], F32) for i in range(2)]
        b_tq = [Buf() for _ in range(2)]
        t2 = [P.sbuf(f"B_t2{i}", [128, 512], F32) for i in range(2)]
        b_t2 = [Buf() for _ in range(2)]
        rt = P.sbuf("B_rt", [128, 4, 4, 16], F32)
        b_rt = [Buf() for _ in range(4)]
        obp = Pool([(P.sbuf(f"B_ob{i}", [128, 512], BF16), Buf()) for i in range(3)])
        stp = Pool([(P.sbuf(f"B_st{i}", [128, 4, 128], BF16), Buf()) for i in range(3)])
        vbp = Pool([(P.sbuf(f"B_vb{i}", [128, 512], BF16), Buf()) for i in range(3)])
        pieces = [(0, 0, True, self.swa_qT), (1, 512, True, self.swa_kT), (2, 1536, False, self.na_qT), (3, 2048, False, self.na_kT)]
        for tt in range(NT):
            r0 = tt * 128
            u, ub = upool.next()
            P.dma(u[:, 0:1536], self.Utok[r0:r0 + 128, 0:1536], reads=[self.dbuf(('Utok', l))], writes=[ub])
            P.dma(u[:, 1536:3072], self.Utok[r0:r0 + 128, 4096:5632], reads=[self.dbuf(('Utok', l))], writes=[ub])
            for (a, off, _, _) in pieces:
                eng = 'pool' if a % 2 else 'dve'
                P.add(eng, lambda e, a=a, off=off, u=u: e.tensor_tensor(out=sq[:, a, :], in0=u[:, off:off + 512], in1=u[:, off:off + 512], op=ALU.mult),
                      reads=[ub], writes=[b_sq[a]])
                P.add('dve', lambda e, a=a: e.tensor_reduce(out=ss[:, 4 * a:4 * a + 4], in_=sq[:, a, :].rearrange("p (h d) -> p h d", h=4),
                                                            axis=AX.X, op=ALU.add), reads=[b_sq[a]], writes=[b_ss])
            P.add('dve', lambda e: e.tensor_scalar(out=ss[:, :], in0=ss[:, :], scalar1=1.0 / 128, scalar2=EPS, op0=ALU.mult, op1=ALU.add),
                  reads=[b_ss], writes=[b_ss])
            P.add('act', lambda e: e.activation(out=ss[:, :], in_=ss[:, :], func=AF.Sqrt), reads=[b_ss], writes=[b_ss])
            P.add('dve', lambda e: e.reciprocal(ss[:, :], ss[:, :]), reads=[b_ss], writes=[b_ss])
            for (a, off, is_swa, dst) in pieces:
                k = a % 2
                ob, obb = obp.next()
                P.add('dve', lambda e, a=a, off=off, u=u, k=k: e.tensor_tensor(
                    out=tq[k][:, :].rearrange("p (h d) -> p h d", h=4), in0=u[:, off:off + 512].rearrange("p (h d) -> p h d", h=4),
                    in1=ss[:, 4 * a:4 * a + 4].unsqueeze(2).to_broadcast([128, 4, 128]), op=ALU.mult),
                    reads=[ub, b_ss], writes=[b_tq[k]])
                if not is_swa:
                    P.add('pool', lambda e, a=a, k=k, ob=ob: e.tensor_tensor(out=ob[:, :], in0=tq[k][:, :], in1=gains[:, a, :], op=ALU.mult),
                          reads=[b_tq[k], b_g], writes=[obb])
                else:
                    P.add('pool', lambda e, a=a, k=k: e.tensor_tensor(out=t2[k][:, :], in0=tq[k][:, :], in1=gains[:, a, :], op=ALU.mult),
                          reads=[b_tq[k], b_g], writes=[b_t2[k]])
                    tv = t2[k][:, :].rearrange("p (h d) -> p h d", h=4)
                    obv = ob[:, :].rearrange("p (h d) -> p h d", h=4)
                    cosb = rope[:, tt, 0, :].unsqueeze(1).to_broadcast([128, 4, 16])
                    sinb = rope[:, tt, 1, :].unsqueeze(1).to_broadcast([128, 4, 16])
                    x1, x2 = tv[:, :, 0:16], tv[:, :, 16:32]
                    P.add('dve', lambda e, x1=x1, cosb=cosb: e.tensor_tensor(out=rt[:, 0], in0=x1, in1=cosb, op=ALU.mult), reads=[b_t2[k], b_rope], writes=[b_rt[0]])
                    P.add('pool', lambda e, x2=x2, sinb=sinb: e.tensor_tensor(out=rt[:, 1], in0=x2, in1=sinb, op=ALU.mult), reads=[b_t2[k], b_rope], writes=[b_rt[1]])
                    P.add('dve', lambda e, x2=x2, cosb=cosb: e.tensor_tensor(out=rt[:, 2], in0=x2, in1=cosb, op=ALU.mult), reads=[b_t2[k], b_rope], writes=[b_rt[2]])
                    P.add('pool', lambda e, x1=x1, sinb=sinb: e.tensor_tensor(out=rt[:, 3], in0=x1, in1=sinb, op=ALU.mult), reads=[b_t2[k], b_rope], writes=[b_rt[3]])
                    P.add('act', lambda e, tv=tv, obv=obv: e.copy(obv[:, :, 32:128], tv[:, :, 32:128]), reads=[b_t2[k]], writes=[obb])
                    P.add('dve', lambda e, obv=obv: e.tensor_tensor(out=obv[:, :, 0:16], in0=rt[:, 0], in1=rt[:, 1], op=ALU.subtract),
                          reads=[b_rt[0], b_rt[1]], writes=[obb])
                    P.add('dve', lambda e, obv=obv: e.tensor_tensor(out=obv[:, :, 16:32], in0=rt[:, 2], in1=rt[:, 3], op=ALU.add),
                          reads=[b_rt[2], b_rt[3]], writes=[obb])
                ps, pb = self.ps_main.next()
                psv = ps[:, 0:256].bitcast(BF16)
                for h in range(4):
                    P.add('pe', lambda e, h=h, ob=ob, psv=psv: e.transpose(psv[:, h * 128:(h + 1) * 128], ob[:, h * 128:(h + 1) * 128], self.ident_b),
                          reads=[obb, self.b_cb], writes=[pb])
                st, stb = stp.next()
                P.add('act', lambda e, st=st, psv=psv: e.copy(st[:, :, :].rearrange("p h t -> p (h t)"), psv[:, :]), reads=[pb], writes=[stb])
                P.dma(dst[:, :, r0:r0 + 128].rearrange("h p t -> p h t"), st[:, :, :], reads=[stb], writes=[self.dbuf(('qk', l))])
            for (off, dstv) in ((1024, self.swa_V), (2560, self.na_V)):
                vb, vbb = vbp.next()
                P.add('act', lambda e, vb=vb, u=u, off=off: e.copy(vb[:, :], u[:, off:off + 512]), reads=[ub], writes=[vbb])
                P.dma(dstv[r0:r0 + 128, :], vb[:, :], reads=[vbb], writes=[self.dbuf(('qk', l))])

    def phase_SWA(self, l):
        P = self.P
        T = self.T
        HS = self.HS
        P.begin_phase()
        PADL = HS
        mk = P.sbuf("S_mk", [128, 4, 128], BF16)
        mkf = P.sbuf("S_mkf", [128, 4, 128], F32)
        b_mk = Buf()
        P.dma(mkf[:], self.swa_mask_in, writes=[b_mk])
        P.add('dve', lambda e: e.tensor_copy(mk[:], mkf[:]), reads=[b_mk], writes=[b_mk])
        qT = P.sbuf("S_qT", [128, T], BF16)
        kT = P.sbuf("S_kT", [128, PADL + T], BF16)
        kh = P.sbuf("S_kh", [128, HS], BF16)
        b_q, b_k, b_kh = Buf(), Buf(), Buf()
        accn = P.sbuf("S_accn", [128, T], F32)
        accd = P.sbuf("S_accd", [128, T], F32)
        b_acc = Buf()
        kbl = Pool([(P.sbuf(f"S_kbl{i}", [128, 128], BF16), Buf()) for i in range(2)])
        vt = Pool([(P.sbuf(f"S_vt{i}", [128, 128], BF16), Buf()) for i in range(4)])
        pts = Pool([(P.sbuf(f"S_pt{i}", [128, 2, 128], BF16), Buf()) for i in range(3)])
        yst = Pool([(P.sbuf(f"S_y{i}", [128, 2048 if T >= 2048 else T], BF16), Buf()) for i in range(2)])
        P.add('pool', lambda e: e.memset(kT[:, 0:PADL], 0.0), writes=[b_k])
        psS = Pool([(self.psb[i], self.psbuf[i]) for i in (0, 1, 2)])
        psO = Pool([(self.psb[i], self.psbuf[i]) for i in (3, 4, 5)])
        for h in range(4):
            P.dma(qT[:], self.swa_qT[h], reads=[self.dbuf(('qk', l))], writes=[b_q])
            P.dma(kT[:, PADL:PADL + T], self.swa_kT[h], reads=[self.dbuf(('qk', l))], writes=[b_k])
            P.dma(kh[:], self.h_swa_kT[h], reads=[self.dbuf('halo')], writes=[b_kh])
            first = True
            import os
            for dil in tuple(int(v) for v in os.environ.get('SWA_DILS', '1,4,16').split(',')):
                Tn = T // dil
                nq = Tn // 128
                for r in range(dil):
                    prevV = None
                    for i in range(nq):
                        qv = qT[:, sl(r + dil * 128 * i, 128, dil)]
                        lastq = (i == nq - 1)
                        a0 = PADL + r + dil * (128 * i - 64)
                        kA = kT[:, sl(a0, 128, dil)]
                        mA = mk[:, 0 if i == 0 else 1, :]
                        if not lastq:
                            b0 = PADL + r + dil * (128 * i + 64)
                            kB = kT[:, sl(b0, 128, dil)]
                            kB_reads = [b_k]
                            mB = mk[:, 2, :]
                        else:
                            kbt, kbb = kbl.next()
                            b0 = PADL + r + dil * (128 * i + 64)
                            P.add('pool', lambda e, kbt=kbt, b0=b0, dil=dil: e.tensor_copy(kbt[:, 0:64], kT[:, sl(b0, 64, dil)]), reads=[b_k], writes=[kbb])
                            rr = dil - 1 - r
                            h0 = rr + dil * (HS // dil - 64)
                            P.add('pool', lambda e, kbt=kbt, h0=h0, dil=dil: e.tensor_copy(kbt[:, 64:128], kh[:, sl(h0, 64, dil)]), reads=[b_kh], writes=[kbb])
                            kB = kbt[:, :]
                            kB_reads = [kbb]
                            mB = mk[:, 3, :]
                        if prevV is None:
                            vA, vAb = vt.next()
                            if i == 0:
                                P.add('pool', lambda e, vA=vA: e.memset(vA[0:64, :], 0.0), writes=[vAb])
                                src = self.swa_V[sl(r, 64, dil), h * 128:(h + 1) * 128]
                                P.dma(vA[64:128, :], src, reads=[self.dbuf(('qk', l))], writes=[vAb])
                            else:
                                raise AssertionError
                        else:
                            vA, vAb = prevV
                        vB, vBb = vt.next()
                        if not lastq:
                            t0 = r + dil * (128 * i + 64)
                            P.dma(vB[:, :], self.swa_V[sl(t0, 128, dil), h * 128:(h + 1) * 128], reads=[self.dbuf(('qk', l))], writes=[vBb])
                        else:
                            t0 = r + dil * (128 * i + 64)
                            P.dma(vB[0:64, :], self.swa_V[sl(t0, 64, dil), h * 128:(h + 1) * 128], reads=[self.dbuf(('qk', l))], writes=[vBb])
                            rr = dil - 1 - r
                            h0 = rr + dil * (HS // dil - 64)
                            P.dma(vB[64:128, :], self.h_swa_V[sl(h0, 64, dil), h * 128:(h + 1) * 128], reads=[self.dbuf('halo')], writes=[vBb])
                        prevV = (vB, vBb)
                        ps, pb = psS.next()
                        for (kk, kx, mx, rds) in ((0, kA, mA, [b_k]), (1, kB, mB, kB_reads)):
                            P.add('pe', lambda e, ps=ps, kk=kk, kx=kx, qv=qv: e.matmul(ps[:, kk * 128:(kk + 1) * 128], kx, qv, start=True, stop=False),
                                  reads=rds + [b_q], writes=[pb])
                            P.add('pe', lambda e, ps=ps, kk=kk, mx=mx: e.matmul(ps[:, kk * 128:(kk + 1) * 128], self.ident_b, mx, start=False, stop=True),
                                  reads=[b_mk, self.b_cb], writes=[pb])
                        pt, ptb = pts.next()
                        P.add('act', lambda e, pt=pt, ps=ps: e.activation(out=pt[:, :, :].rearrange("p a n -> p (a n)"), in_=ps[:, 0:256], func=AF.Exp),
                              reads=[pb], writes=[ptb])
                        po, pob = psO.next()
                        for (kk, vx, vxb) in ((0, vA, vAb), (1, vB, vBb)):
                            P.add('pe', lambda e, po=po, kk=kk, vx=vx, pt=pt: e.matmul(po[:, 0:128], vx[:, :], pt[:, kk, :], start=(kk == 0), stop=(kk == 1)),
                                  reads=[vxb, ptb], writes=[pob])
                        for kk in (0, 1):
                            P.add('pe', lambda e, po=po, kk=kk, pt=pt: e.matmul(po[:, 128:256], self.ones_b, pt[:, kk, :], start=(kk == 0), stop=(kk == 1)),
                                  reads=[self.b_cb, ptb], writes=[pob])
                        q0 = r + dil * 128 * i
                        an = accn[:, sl(q0, 128, dil)]
                        ad = accd[:, sl(q0, 128, dil)]
                        if first:
                            P.add('dve', lambda e, an=an, po=po: e.tensor_copy(an, po[:, 0:128]), reads=[pob], writes=[b_acc])
                            P.add('act', lambda e, ad=ad, po=po: e.copy(ad, po[:, 128:256]), reads=[pob], writes=[b_acc])
                        else:
                            P.add('dve', lambda e, an=an, po=po: e.tensor_tensor(out=an, in0=an, in1=po[:, 0:128], op=ALU.add), reads=[pob, b_acc], writes=[b_acc])
                            P.add('dve', lambda e, ad=ad, po=po: e.tensor_tensor(out=ad, in0=ad, in1=po[:, 128:256], op=ALU.add), reads=[pob, b_acc], writes=[b_acc])
                first = False
            CW = 2048 if T >= 2048 else T
            for c0 in range(0, T, CW):
                P.add('dve', lambda e, c0=c0, CW=CW: e.reciprocal(accd[:, c0:c0 + CW], accd[:, c0:c0 + CW]), reads=[b_acc], writes=[b_acc])
                y, yb = yst.next()
                P.add('pool', lambda e, c0=c0, CW=CW, y=y: e.tensor_tensor(out=y[:, 0:CW], in0=accn[:, c0:c0 + CW], in1=accd[:, c0:c0 + CW], op=ALU.mult),
                      reads=[b_acc], writes=[yb])
                P.dma(self.yT[h * 128:(h + 1) * 128, c0:c0 + CW], y[:, 0:CW], reads=[yb], writes=[self.dbuf(('yT', l))])

    NA_TYPES = {'F0': (0, [0, 1, 2, 3]), 'F1': (4, [-1, 0, 1, 2]), 'INT': (8, [-2, -1, 0, 1, 2]), 'L1': (13, [-2, -1, 0, 1, 2]),
                'L0': (18, [-3, -2, -1, 0, 1, 2])}

    def phase_NA(self, l):
        P = self.P
        T, NT = self.T, self.NT
        J = NT
        P.begin_phase()
        qT = P.sbuf("N_qT", [128, T], BF16)
        kT = P.sbuf("N_kT", [128, T + 256], BF16)
        V = P.sbuf("N_V", [128, J + 2, 128], BF16)
        bias_f = P.sbuf("N_bf", [128, 24, 128], F32)
        bias = P.sbuf("N_b", [128, 24, 128], BF16)
        yb_ = P.sbuf("N_y", [128, T], BF16)
        b_q, b_k, b_v, b_bf, b_b, b_y = Buf(), Buf(), Buf(), Buf(), Buf(), Buf()
        pts = Pool([(P.sbuf(f"N_pt{i}", [128, 6, 128], BF16), Buf()) for i in range(3)])
        rd = Pool([(P.sbuf(f"N_rd{i}", [128, 128], F32), Buf()) for i in range(3)])
        psS = Pool([((self.psb[i], self.psb[i + 1]), (self.psbuf[i], self.psbuf[i + 1])) for i in (0, 2)])
        psO = Pool([(self.psb[i], self.psbuf[i]) for i in (4, 5, 6)])
        for h in range(4):
            P.dma(qT[:], self.na_qT[h], reads=[self.dbuf(('qk', l))], writes=[b_q])
            P.dma(kT[:, 0:T], self.na_kT[h], reads=[self.dbuf(('qk', l))], writes=[b_k])
            P.dma(kT[:, T:T + 128], self.h_na_kT[h, :, 128:256], reads=[self.dbuf('halo')], writes=[b_k])
            P.dma(kT[:, T + 128:T + 256], self.h_na_kT[h, :, 0:128], reads=[self.dbuf('halo')], writes=[b_k])
            P.dma(V[:, 0:J, :], self.na_V[:, h * 128:(h + 1) * 128].rearrange("(j p) d -> p j d", p=128), reads=[self.dbuf(('qk', l))], writes=[b_v])
            P.dma(V[:, J, :], self.h_na_V[128:256, h * 128:(h + 1) * 128], reads=[self.dbuf('halo')], writes=[b_v])
            P.dma(V[:, J + 1, :], self.h_na_V[0:128, h * 128:(h + 1) * 128], reads=[self.dbuf('halo')], writes=[b_v])
            P.dma(bias_f[:], self.na_bias_in[l, h], writes=[b_bf])
            P.add('dve', lambda e: e.tensor_copy(bias[:], bias_f[:]), reads=[b_bf], writes=[b_b])
            for j in range(J):
                ty = 'F0' if j == 0 else 'F1' if j == 1 else 'L0' if j == J - 1 else 'L1' if j == J - 2 else 'INT'
                base, offs = self.NA_TYPES[ty]
                (psA, psB), (pbA, pbB) = psS.next()
                qv = qT[:, j * 128:(j + 1) * 128]
                for oi, o in enumerate(offs):
                    kt = j + o
                    ps, pb = (psA, pbA) if oi < 4 else (psB, pbB)
                    col = (oi % 4) * 128
                    P.add('pe', lambda e, ps=ps, col=col, kt=kt, qv=qv: e.matmul(ps[:, col:col + 128], kT[:, kt * 128:(kt + 1) * 128], qv, start=True, stop=False),
                          reads=[b_k, b_q], writes=[pb])
                    P.add('pe', lambda e, ps=ps, col=col, bi=base + oi: e.matmul(ps[:, col:col + 128], self.ident_b, bias[:, bi, :], start=False, stop=True),
                          reads=[b_b, self.b_cb], writes=[pb])
                pt, ptb = pts.next()
                n0 = min(4, len(offs))
                P.add('act', lambda e, pt=pt, psA=psA, n0=n0: e.activation(out=pt[:, 0:n0, :].rearrange("p a n -> p (a n)"), in_=psA[:, 0:n0 * 128], func=AF.Exp),
                      reads=[pbA], writes=[ptb])
                if len(offs) > 4:
                    n1 = len(offs) - 4
                    P.add('act', lambda e, pt=pt, psB=psB, n1=n1: e.activation(out=pt[:, 4:4 + n1, :].rearrange("p a n -> p (a n)"), in_=psB[:, 0:n1 * 128], func=AF.Exp),
                          reads=[pbB], writes=[ptb])
                po, pob = psO.next()
                no = len(offs)
                for oi, o in enumerate(offs):
                    kt = j + o
                    P.add('pe', lambda e, po=po, oi=oi, kt=kt, pt=pt, no=no: e.matmul(po[:, 0:128], V[:, kt, :], pt[:, oi, :], start=(oi == 0), stop=(oi == no - 1)),
                          reads=[b_v, ptb], writes=[pob])
                for oi, o in enumerate(offs):
                    P.add('pe', lambda e, po=po, oi=oi, pt=pt, no=no: e.matmul(po[:, 128:256], self.ones_b, pt[:, oi, :], start=(oi == 0), stop=(oi == no - 1)),
                          reads=[self.b_cb, ptb], writes=[pob])
                r, rb = rd.next()
                P.add('dve', lambda e, r=r, po=po: e.reciprocal(r[:, :], po[:, 128:256]), reads=[pob], writes=[rb])
                P.add('dve', lambda e, r=r, po=po, j=j: e.tensor_tensor(out=yb_[:, j * 128:(j + 1) * 128], in0=po[:, 0:128], in1=r[:, :], op=ALU.mult),
                      reads=[pob, rb], writes=[b_y])
            P.dma(self.yT[1024 + h * 128:1024 + (h + 1) * 128, :], yb_[:], reads=[b_y], writes=[self.dbuf(('yT', l))])

    def zero_y(self, l, rows):
        P = self.P
        P.begin_phase()
        z = P.sbuf("zy", [128, self.T], BF16)
        bz = Buf()
        P.add('pool', lambda e: e.memset(z[:], 0.0), writes=[bz])
        for r0 in rows:
            P.dma(self.yT[r0:r0 + 128, :], z[:], reads=[bz], writes=[self.dbuf(('yT', l))])


    def compute_lb(self):
        P = self.P
        L = self.depth
        P.begin_phase()
        for (src, dstkind, W) in ((self.hg_lb_in, 'tok', 512), (self.hg_lbT_in, 'fm', 4)):
            lg = P.sbuf(f"lb_lg{W}", [128, L, 2, W], F32)
            sm = P.sbuf(f"lb_sm{W}", [128, 2, W], F32)
            out = P.sbuf(f"lb_out{W}", [128, L, 2, 2, W], F32)
            b = Buf()
            P.dma(lg[:], src, writes=[b])
            P.add('act', lambda e, lg=lg: e.activation(out=lg[:], in_=lg[:], func=AF.Exp), reads=[b], writes=[b])
            P.add('dve', lambda e, lg=lg, sm=sm: e.tensor_copy(sm[:], lg[:, 0]), reads=[b], writes=[b])
            for l in range(1, L):
                P.add('dve', lambda e, lg=lg, sm=sm, l=l: e.tensor_tensor(out=sm[:], in0=sm[:], in1=lg[:, l], op=ALU.add), reads=[b], writes=[b])
            P.add('dve', lambda e, sm=sm: e.reciprocal(sm[:], sm[:]), reads=[b], writes=[b])
            P.add('pool', lambda e, out=out: e.memset(out[:, 0, :, 0, :], 0.0), writes=[b])
            for l in range(1, L):
                P.add('dve', lambda e, lg=lg, sm=sm, l=l: e.tensor_tensor(out=lg[:, l], in0=lg[:, l], in1=sm[:], op=ALU.mult), reads=[b], writes=[b])
                P.add('dve', lambda e, lg=lg, out=out, l=l: e.tensor_tensor(out=out[:, l, :, 0, :], in0=out[:, l - 1, :, 0, :], in1=lg[:, l], op=ALU.add),
                      reads=[b], writes=[b])
            for l in range(L):
                P.add('dve', lambda e, out=out, l=l: e.tensor_scalar(out=out[:, l, :, 1, :], in0=out[:, l, :, 0, :], scalar1=-1.0, scalar2=1.0,
                                                                     op0=ALU.mult, op1=ALU.add), reads=[b], writes=[b])
            if dstkind == 'tok':
                for l in range(L):
                    P.dma(self.lb_d[l].rearrange("d a p w -> p d a w"), out[:, l], reads=[b], writes=[self.dbuf('lb')])
            else:
                for l in range(L):
                    P.dma(self.lbT_d[l], out[:, l], reads=[b], writes=[self.dbuf('lb')])

    def phase_HG(self, l):
        P = self.P
        T, NT = self.T, self.NT
        P.begin_phase()
        lbt = P.sbuf("H_lbt", [128, 2, 2, 512], F32)
        lbf = P.sbuf("H_lbf", [128, 2, 2, 4], F32)
        gain = P.sbuf("H_gain", [128, 512], F32)
        b_lb = Buf()
        P.dma(lbt[:], self.lb_d[l].rearrange("d a p w -> p d a w"), reads=[self.dbuf('lb')], writes=[b_lb])
        P.dma(lbf[:], self.lbT_d[l], reads=[self.dbuf('lb')], writes=[b_lb])
        P.dma(gain[:], self.hg_gain_in[l], writes=[b_lb])
        S = P.sbuf("H_S", [128, 4, 128], F32)
        Sb = P.sbuf("H_Sb", [128, 4, 128], BF16)
        b_S = [Buf() for _ in range(4)]
        b_Sb = [Buf() for _ in range(4)]
        up = Pool([(P.sbuf(f"H_u{i}", [128, 3, 512], F32), Buf()) for i in range(2)])
        fp_ = Pool([(P.sbuf(f"H_f{i}", [128, 8, 128], F32), Buf()) for i in range(2)])
        sc1 = P.sbuf("H_sc1", [128, 512], F32)
        ff = P.sbuf("H_ff", [128, 512], F32)
        b_sc1, b_ff = Buf(), Buf()
        ktok = Pool([(P.sbuf(f"H_kt{i}", [128, 512], F32), Buf()) for i in range(2)])
        logf = Pool([(P.sbuf(f"H_lf{i}", [128, 512], F32), Buf()) for i in range(2)])
        vbp = Pool([(P.sbuf(f"H_vb{i}", [128, 512], BF16), Buf()) for i in range(2)])
        qkp = Pool([(P.sbuf(f"H_qk{i}", [128, 8, 128], F32), Buf()) for i in range(2)])
        exq = Pool([(P.sbuf(f"H_ex{i}", [128, 4, 128], F32), Buf()) for i in range(3)])
        Qb2 = [P.sbuf(f"H_Qb{i}", [128, 4, 128], BF16) for i in range(3)]
        b_Qb2 = [Buf() for _ in range(3)]
        QK = Pool([(P.sbuf(f"H_QK{i}", [128, 2, 128], BF16), Buf()) for i in range(3)])
        Kd2 = [P.sbuf(f"H_Kd{i}", [128, 4, 128], BF16) for i in range(3)]
        kdf = P.sbuf("H_kdf", [128, 128], F32)
        b_kdf = Buf()
        b_Kd2 = [Buf() for _ in range(3)]
        aTm = Pool([(P.sbuf(f"H_aT{i}", [128, 128], BF16), Buf()) for i in range(3)])
        ost = Pool([(P.sbuf(f"H_o{i}", [128, 512], F32), Buf()) for i in range(2)])
        o1t = Pool([(P.sbuf(f"H_o1{i}", [128, 512], F32), Buf()) for i in range(2)])
        sq = P.sbuf("H_sq", [128, 512], F32)
        ss = P.sbuf("H_ss", [128, 4], F32)
        b_sq, b_ss = Buf(), Buf()
        gg = P.sbuf("H_gg", [128, 512], F32)
        b_gg = Buf()
        yb = Pool([(P.sbuf(f"H_y{i}", [128, 512], BF16), Buf()) for i in range(2)])
        yst = Pool([(P.sbuf(f"H_ys{i}", [128, 4, 128], BF16), Buf()) for i in range(2)])
        for i in range(3):
            P.add('pool', lambda e, i=i: e.memset(Qb2[i][:], 0.0), writes=[b_Qb2[i]])
        psE = Pool([(self.psb[i], self.psbuf[i]) for i in (0, 1, 2)])
        psO = Pool([(self.psb[i], self.psbuf[i]) for i in (3, 4)])
        psS = Pool([(self.psb[i], self.psbuf[i]) for i in (5, 6)])
        UTv = self.UT
        kq = 0
        for d in (0, 1):
            cbase = 2 if d == 0 else 6
            Mrem, M1, M2, msk = (self.cf[:, cbase + k, :] for k in range(4))
            fcol = C_HG_F1 if d == 0 else C_HG_F2
            if d == 0:
                P.add('pool', lambda e: e.memset(S[:], 0.0), writes=b_S)
                P.add('pool', lambda e: e.memset(Sb[:], 0.0), writes=b_Sb)
            else:
                self.exchange_state(l, 'hg', S, b_S, Sb, b_Sb)
            tiles = range(NT) if d == 0 else range(NT - 1, -1, -1)
            for tt in tiles:
                r0 = tt * 128
                u, ub = up.next()
                P.dma(u[:, 0, :], self.Utok[r0:r0 + 128, fcol:fcol + 512], reads=[self.dbuf(('Utok', l))], writes=[ub])
                P.dma(u[:, 1, :], self.Utok[r0:r0 + 128, C_HG_I:C_HG_I + 512], reads=[self.dbuf(('Utok', l))], writes=[ub])
                if d == 1:
                    P.dma(u[:, 2, :], self.Utok[r0:r0 + 128, C_HG_G:C_HG_G + 512], reads=[self.dbuf(('Utok', l))], writes=[ub])
                f, fb = fp_.next()
                P.dma(f[:, 0:4, :], UTv[C_HG_Q:C_HG_Q + 512, r0:r0 + 128].rearrange("(h p) t -> p h t", p=128), reads=[self.dbuf(('UT', l))], writes=[fb])
                P.dma(f[:, 4:8, :], UTv[fcol:fcol + 512, r0:r0 + 128].rearrange("(h p) t -> p h t", p=128), reads=[self.dbuf(('UT', l))], writes=[fb])
                kt, ktb = ktok.next()
                lf, lfb = logf.next()
                vb, vbb = vbp.next()
                P.add('act', lambda e, u=u: e.activation(out=sc1[:], in_=u[:, 0, :], func=AF.Sigmoid), reads=[ub], writes=[b_sc1])
                P.add('pool', lambda e, d=d: e.tensor_tensor(out=sc1[:], in0=sc1[:], in1=lbt[:, d, 1, :], op=ALU.mult), reads=[b_sc1, b_lb], writes=[b_sc1])
                P.add('dve', lambda e, d=d: e.tensor_tensor(out=ff[:], in0=sc1[:], in1=lbt[:, d, 0, :], op=ALU.add), reads=[b_sc1, b_lb], writes=[b_ff])
                P.add('pool', lambda e, d=d, kt=kt: e.tensor_tensor(out=kt[:], in0=lbt[:, d, 1, :], in1=sc1[:], op=ALU.subtract), reads=[b_sc1, b_lb], writes=[ktb])
                P.add('act', lambda e, lf=lf: e.activation(out=lf[:], in_=ff[:], func=AF.Ln), reads=[b_ff], writes=[lfb])
                P.add('act', lambda e, vb=vb, u=u: e.copy(vb[:], u[:, 1, :]), reads=[ub], writes=[vbb])
                qk, qkb = qkp.next()
                P.add('act', lambda e, qk=qk, f=f: e.activation(out=qk[:, 0:4, :], in_=f[:, 0:4, :], func=AF.Silu), reads=[fb], writes=[qkb])
                P.add('act', lambda e, qk=qk, f=f: e.activation(out=qk[:, 4:8, :], in_=f[:, 4:8, :], func=AF.Sigmoid, scale=-1.0), reads=[fb], writes=[qkb])
                for h in range(4):
                    P.add('pool', lambda e, qk=qk, h=h, d=d: e.tensor_scalar(out=qk[:, 4 + h, :], in0=qk[:, 4 + h, :], scalar1=lbf[:, d, 1, h:h + 1], scalar2=None,
                                                                              op0=ALU.mult), reads=[qkb, b_lb], writes=[qkb])
                po, pob = psO.next()
                for h in range(4):
                    hs = slice(h * 128, (h + 1) * 128)
                    pe_, peb = psE.next()
                    P.add('pe', lambda e, pe_=pe_, lf=lf, hs=hs, Mrem=Mrem: e.matmul(pe_[:, 0:128], Mrem, lf[:, hs], start=True, stop=True),
                          reads=[lfb, self.b_cf], writes=[peb])
                    P.add('pe', lambda e, pe_=pe_, lf=lf, hs=hs, M1=M1: e.matmul(pe_[:, 128:256], lf[:, hs], M1, start=True, stop=True),
                          reads=[lfb, self.b_cf], writes=[peb])
                    ex, exb = exq.next()
                    P.add('act', lambda e, ex=ex, pe_=pe_: e.activation(out=ex[:, 0:2, :].rearrange("p a n -> p (a n)"), in_=pe_[:, 0:256], func=AF.Exp),
                          reads=[peb], writes=[exb])
                    P.add('act', lambda e, ex=ex, pe_=pe_: e.activation(out=ex[:, 2, :], in_=pe_[:, 128:256], func=AF.Exp, scale=-1.0),
                          reads=[peb], writes=[exb])
                    k3 = kq % 3
                    kq += 1
                    qb = Qb2[k3]
                    qbf = qb[:]
                    qb_out = bass.AP(qbf.tensor, qbf.offset, [list(qbf.ap[0]), [160, 4], [1, 32]])
                    qT_h = qk[:, h, :]
                    P.add('dve', lambda e, qb_out=qb_out, qT_h=qT_h, ex=ex: e.tensor_tensor(
                        out=qb_out, in0=qT_h.rearrange("p (c n) -> p c n", c=4), in1=ex[:, 1, :].rearrange("p (c n) -> p c n", c=4), op=ALU.mult),
                        reads=[qkb, exb], writes=[b_Qb2[k3]])
                    QKt, QKb = QK.next()
                    P.add('pool', lambda e, QKt=QKt, qT_h=qT_h, ex=ex: e.tensor_tensor(out=QKt[:, 0, :], in0=qT_h, in1=ex[:, 1, :], op=ALU.mult),
                          reads=[qkb, exb], writes=[QKb])
                    P.add('dve', lambda e, QKt=QKt, qk=qk, h=h, ex=ex: e.tensor_tensor(out=QKt[:, 1, :], in0=qk[:, 4 + h, :], in1=ex[:, 2, :], op=ALU.mult),
                          reads=[qkb, exb], writes=[QKb])
                    kd = Kd2[k3]
                    P.add('pool', lambda e, kt=kt, hs=hs, ex=ex: e.tensor_tensor(out=kdf[:], in0=kt[:, hs], in1=ex[:, 0, :], op=ALU.mult),
                          reads=[ktb, exb], writes=[b_kdf])
                    P.add('dve', lambda e, kd=kd: e.tensor_tensor(out=kd[:], in0=kdf[:].unsqueeze(1).to_broadcast([128, 4, 128]), in1=self.cf[:, 16:20, :], op=ALU.mult),
                          reads=[b_kdf, self.b_cf], writes=[b_Kd2[k3]])
                    P.add('pe', lambda e, pe_=pe_, QKt=QKt: e.matmul(pe_[:, 384:512], QKt[:, 1, :], QKt[:, 0, :], start=True, stop=True),
                          reads=[QKb], writes=[peb])
                    at, atb = aTm.next()
                    P.add('dve', lambda e, at=at, pe_=pe_, msk=msk: e.tensor_tensor(out=at[:], in0=pe_[:, 384:512], in1=msk, op=ALU.mult),
                          reads=[peb, self.b_cf], writes=[atb])
                    corder = (0, 1, 2, 3) if d == 0 else (3, 2, 1, 0)
                    for ci, c in enumerate(corder):
                        P.add('pe', lambda e, po=po, hs=hs, qb=qb, c=c, h=h, ci=ci: e.matmul(po[:, hs], qb[:, c, :], Sb[:, h, :], start=(ci == 0), stop=False),
                              reads=[b_Qb2[k3], b_Sb[h]], writes=[pob])
                        if ci == 3:
                            P.add('pe', lambda e, po=po, hs=hs, at=at, vb=vb: e.matmul(po[:, hs], at[:], vb[:, hs], start=False, stop=True),
                                  reads=[atb, vbb], writes=[pob])
                        pS, pSb = psS.next()
                        P.add('pe', lambda e, pS=pS, kd=kd, c=c, vb=vb, hs=hs: e.matmul(pS[:, 0:128], kd[:, c, :], vb[:, hs], start=True, stop=True),
                              reads=[b_Kd2[k3], vbb], writes=[pSb])
                        deccol = (c * 32 + 31) if d == 0 else (c * 32)
                        P.add('dve', lambda e, pS=pS, h=h, ex=ex, deccol=deccol: e.scalar_tensor_tensor(
                            out=S[:, h, :], in0=S[:, h, :], scalar=ex[:, 1, deccol:deccol + 1], in1=pS[:, 0:128], op0=ALU.mult, op1=ALU.add),
                            reads=[pSb, exb, b_S[h]], writes=[b_S[h]])
                        P.add('act', lambda e, h=h: e.copy(Sb[:, h, :], S[:, h, :]), reads=[b_S[h]], writes=[b_Sb[h]])
                if d == 0:
                    o, ob = ost.next()
                    P.add('act', lambda e, o=o, po=po: e.copy(o[:], po[:]), reads=[pob], writes=[ob])
                    P.dma(self.o1_d[r0:r0 + 128, :], o[:], reads=[ob], writes=[self.dbuf(('o1', l))])
                else:
                    o1, o1b = o1t.next()
                    P.dma(o1[:], self.o1_d[r0:r0 + 128, :], reads=[self.dbuf(('o1', l))], writes=[o1b])
                    o, ob = ost.next()
                    P.add('dve', lambda e, o=o, po=po, o1=o1: e.tensor_tensor(out=o[:], in0=po[:], in1=o1[:], op=ALU.add), reads=[pob, o1b], writes=[ob])
                    self.norm_gate_emit(l, o, ob, u[:, 2, :], ub, gain, b_lb, sq, b_sq, ss, b_ss, gg, b_gg, yb, yst, 512, r0)

    def norm_gate_emit(self, l, o, ob, gt, gtb, gain, b_gain, sq, b_sq, ss, b_ss, gg, b_gg, yb, yst, yrow0, r0):
        P = self.P
        P.add('pool', lambda e: e.tensor_tensor(out=sq[:], in0=o[:], in1=o[:], op=ALU.mult), reads=[ob], writes=[b_sq])
        P.add('dve', lambda e: e.tensor_reduce(out=ss[:, 0:4], in_=sq[:].rearrange("p (h d) -> p h d", h=4), axis=AX.X, op=ALU.add),
              reads=[b_sq], writes=[b_ss])
        P.add('dve', lambda e: e.tensor_scalar(out=ss[:, 0:4], in0=ss[:, 0:4], scalar1=1.0 / 128, scalar2=EPS, op0=ALU.mult, op1=ALU.add),
              reads=[b_ss], writes=[b_ss])
        P.add('act', lambda e: e.activation(out=ss[:, 0:4], in_=ss[:, 0:4], func=AF.Sqrt), reads=[b_ss], writes=[b_ss])
        P.add('dve', lambda e: e.reciprocal(ss[:, 0:4], ss[:, 0:4]), reads=[b_ss], writes=[b_ss])
        P.add('act', lambda e: e.activation(out=gg[:], in_=gt, func=AF.Silu), reads=[gtb], writes=[b_gg])
        P.add('pool', lambda e: e.tensor_tensor(out=gg[:], in0=gg[:], in1=gain[:], op=ALU.mult), reads=[b_gg, b_gain], writes=[b_gg])
        P.add('dve', lambda e: e.tensor_tensor(out=o[:].rearrange("p (h d) -> p h d", h=4), in0=o[:].rearrange("p (h d) -> p h d", h=4),
                                               in1=ss[:, 0:4].unsqueeze(2).to_broadcast([128, 4, 128]), op=ALU.mult), reads=[ob, b_ss], writes=[ob])
        y, ybb = yb.next()
        P.add('dve', lambda e, y=y: e.tensor_tensor(out=y[:], in0=o[:], in1=gg[:], op=ALU.mult), reads=[ob, b_gg], writes=[ybb])
        ps, pb = self.psb[7], self.psbuf[7]
        psv = ps[:, 0:256].bitcast(BF16)
        for h in range(4):
            P.add('pe', lambda e, h=h, y=y, psv=psv: e.transpose(psv[:, h * 128:(h + 1) * 128], y[:, h * 128:(h + 1) * 128], self.ident_b),
                  reads=[ybb, self.b_cb], writes=[pb])
        st, stb = yst.next()
        P.add('act', lambda e, st=st, psv=psv: e.copy(st[:, :, :].rearrange("p h t -> p (h t)"), psv[:, :]), reads=[pb], writes=[stb])
        P.dma(self.yT[yrow0:yrow0 + 512, r0:r0 + 128].rearrange("(h p) t -> p h t", p=128), st[:, :, :], reads=[stb], writes=[self.dbuf(('yT', l))])


    def exchange_halos(self, l):
        P = self.P
        T, HS = self.T, self.HS
        P.begin_phase()
        bpub, bg = Buf(), Buf()
        src = [self.dbuf(('qk', l)), self.dbuf(('UT', l))]
        P.dma(self.pubA[0].rearrange("p (h t) -> p h t", h=4), self.swa_kT[:, :, T - HS:T].rearrange("h p t -> p h t"), reads=src, writes=[bpub])
        P.dma(self.pubA[1].rearrange("p (a n) -> p a n", n=512), self.swa_V[T - HS:T, :].rearrange("(a p) n -> p a n", p=128), reads=src, writes=[bpub])
        P.dma(self.pubA[2][:, 0:1024].rearrange("p (h t) -> p h t", h=4), self.na_kT[:, :, T - 256:T].rearrange("h p t -> p h t"), reads=src, writes=[bpub])
        P.dma(self.pubA[2][:, 1024:2048].rearrange("p (a n) -> p a n", n=512), self.na_V[T - 256:T, :].rearrange("(a p) n -> p a n", p=128), reads=src, writes=[bpub])
        P.dma(self.pubB.rearrange("p (c t) -> p c t", t=2), self.UT[C_DN_QKV:C_DN_QKV + 1536, T - 2:T].rearrange("(c p) t -> p c t", p=128), reads=src, writes=[bpub])
        for i in range(3):
            P.add('pool', lambda e, i=i: e.collective_compute("AllGather", ALU.bypass, replica_groups=self.RG, ins=[self.pubA[i]], outs=[self.gathA[i]]),
                  reads=[bpub], writes=[bg])
        P.add('pool', lambda e: e.collective_compute("AllGather", ALU.bypass, replica_groups=self.RG, ins=[self.pubB], outs=[self.gathB]),
              reads=[bpub], writes=[bg])
        fl = P.sbuf("X_fl", [128, 2], F32)
        b_fl = Buf()
        P.dma(fl[:], self.flag_in, writes=[b_fl])
        hb = self.dbuf('halo')
        WM = max(self.XWs)
        p0 = P.sbuf("X_p0", [128, WM], BF16)
        p1 = P.sbuf("X_p1", [128, WM], BF16)
        b_p = Buf()
        for i, w in enumerate(self.XWs):
            P.dma(p0[:, 0:w], self.gathA[i][0:128, :], reads=[bg], writes=[b_p])
            P.dma(p1[:, 0:w], self.gathA[i][128:256, :], reads=[bg], writes=[b_p])
            P.add('dve', lambda e, w=w: e.tensor_scalar(out=p0[:, 0:w], in0=p0[:, 0:w], scalar1=fl[:, 0:1], scalar2=None, op0=ALU.mult), reads=[b_p, b_fl], writes=[b_p])
            P.add('dve', lambda e, w=w: e.scalar_tensor_tensor(out=p0[:, 0:w], in0=p1[:, 0:w], scalar=fl[:, 1:2], in1=p0[:, 0:w], op0=ALU.mult, op1=ALU.add),
                  reads=[b_p, b_fl], writes=[b_p])
            if i == 0:
                P.dma(self.h_swa_kT.rearrange("h p t -> p h t"), p0[:, 0:w].rearrange("p (h t) -> p h t", h=4), reads=[b_p], writes=[hb])
            elif i == 1:
                P.dma(self.h_swa_V.rearrange("(a p) n -> p a n", p=128), p0[:, 0:w].rearrange("p (a n) -> p a n", n=512), reads=[b_p], writes=[hb])
            else:
                P.dma(self.h_na_kT.rearrange("h p t -> p h t"), p0[:, 0:1024].rearrange("p (h t) -> p h t", h=4), reads=[b_p], writes=[hb])
                P.dma(self.h_na_V.rearrange("(a p) n -> p a n", p=128), p0[:, 1024:2048].rearrange("p (a n) -> p a n", n=512), reads=[b_p], writes=[hb])
        q0 = P.sbuf("X_q0", [128, 12, 2], F32)
        q1 = P.sbuf("X_q1", [128, 12, 2], F32)
        q2 = P.sbuf("X_q2", [128, 12, 2], F32)
        b_q = Buf()
        P.dma(q0[:], self.gathB[0:128, :].rearrange("p (c t) -> p c t", t=2), reads=[bg], writes=[b_q])
        P.dma(q1[:], self.gathB[128:256, :].rearrange("p (c t) -> p c t", t=2), reads=[bg], writes=[b_q])
        P.add('dve', lambda e: e.tensor_scalar(out=q0[:], in0=q0[:], scalar1=fl[:, 0:1], scalar2=None, op0=ALU.mult), reads=[b_q, b_fl], writes=[b_q])
        P.add('dve', lambda e: e.scalar_tensor_tensor(out=q0[:], in0=q1[:], scalar=fl[:, 1:2], in1=q0[:], op0=ALU.mult, op1=ALU.add), reads=[b_q, b_fl], writes=[b_q])
        P.add('dve', lambda e: e.tensor_copy(q2[:, :, 0:1], q0[:, :, 1:2]), reads=[b_q], writes=[b_q])
        P.add('dve', lambda e: e.tensor_copy(q2[:, :, 1:2], q0[:, :, 0:1]), reads=[b_q], writes=[b_q])
        P.dma(self.h_dn_x.rearrange("(c p) t -> p c t", p=128), q2[:], reads=[b_q], writes=[hb])

    def exchange_state(self, l, kind, S, b_S, Sb, b_Sb):
        P = self.P
        if not self.couple:
            P.add('pool', lambda e: e.memset(S[:], 0.0), writes=b_S)
            P.add('pool', lambda e: e.memset(Sb[:], 0.0), writes=b_Sb)
            return
        bpub, bg = self.dbuf(('pubS', kind, l)), self.dbuf(('gathS', kind, l))
        P.dma(self.pubS.rearrange("p (h n) -> p h n", h=4), S[:], reads=b_S, writes=[self.dbuf('pubS_any')])
        P.add('pool', lambda e: e.collective_compute("AllGather", ALU.bypass, replica_groups=self.RG, ins=[self.pubS], outs=[self.gathS]),
              reads=[self.dbuf('pubS_any')], writes=[self.dbuf('gathS_any')])
        t1 = P.sbuf(f"XS_{kind}", [128, 4, 128], F32)
        fl = P.sbuf(f"XSf_{kind}", [128, 2], F32)
        bt, bf = Buf(), Buf()
        P.dma(fl[:], self.flag_in, writes=[bf])
        P.dma(S[:], self.gathS[0:128, :].rearrange("p (h n) -> p h n", h=4), reads=[self.dbuf('gathS_any')], writes=b_S)
        P.dma(t1[:], self.gathS[128:256, :].rearrange("p (h n) -> p h n", h=4), reads=[self.dbuf('gathS_any')], writes=[bt])
        P.add('dve', lambda e: e.tensor_scalar(out=S[:], in0=S[:], scalar1=fl[:, 0:1], scalar2=None, op0=ALU.mult), reads=b_S + [bf], writes=b_S)
        P.add('dve', lambda e: e.scalar_tensor_tensor(out=S[:], in0=t1[:], scalar=fl[:, 1:2], in1=S[:], op0=ALU.mult, op1=ALU.add), reads=b_S + [bf, bt], writes=b_S)
        P.add('act', lambda e: e.copy(Sb[:], S[:]), reads=b_S, writes=b_Sb)

    def phase_DNconv(self, l):
        P = self.P
        T, NT = self.T, self.NT
        P.begin_phase()
        cw = P.sbuf("DC_w", [128, 12, 5], F32)
        b_cw = Buf()
        P.dma(cw[:], self.dn_conv_in[l], writes=[b_cw])
        xp = Pool([(P.sbuf(f"DC_x{i}", [128, T + 4], F32), Buf()) for i in range(2)])
        acc = Pool([(P.sbuf(f"DC_a{i}", [128, T], F32), Buf()) for i in range(2)])
        sb = Pool([(P.sbuf(f"DC_s{i}", [128, T], BF16), Buf()) for i in range(2)])
        st = Pool([(P.sbuf(f"DC_t{i}", [128, 512], BF16), Buf()) for i in range(3)])
        pss = Pool([(self.psb[i], self.psbuf[i]) for i in range(4)])
        for ct in range(12):
            x, xb = xp.next()
            r0 = C_DN_QKV + ct * 128
            P.add('pool', lambda e, x=x: e.memset(x[:, 0:2], 0.0), writes=[xb])
            P.dma(x[:, 2:2 + T], self.UT[r0:r0 + 128, :], reads=[self.dbuf(('UT', l))], writes=[xb])
            P.dma(x[:, 2 + T:4 + T], self.h_dn_x[ct * 128:(ct + 1) * 128, :], reads=[self.dbuf('halo')], writes=[xb])
            a, ab = acc.next()
            eng = 'dve'
            P.add(eng, lambda e, a=a, x=x, ct=ct: e.tensor_scalar(out=a[:], in0=x[:, 0:T], scalar1=cw[:, ct, 0:1], scalar2=None, op0=ALU.mult),
                  reads=[xb, b_cw], writes=[ab])
            for j in range(1, 5):
                P.add(eng, lambda e, a=a, x=x, ct=ct, j=j: e.scalar_tensor_tensor(out=a[:], in0=x[:, j:j + T], scalar=cw[:, ct, j:j + 1], in1=a[:],
                                                                                  op0=ALU.mult, op1=ALU.add), reads=[xb, b_cw, ab], writes=[ab])
            sbt, sbb = sb.next()
            P.add('act', lambda e, sbt=sbt, a=a: e.activation(out=sbt[:], in_=a[:], func=AF.Silu), reads=[ab], writes=[sbb])
            for tg in range(NT // 4):
                ps, pb = pss.next()
                psv = ps[:, 0:256].bitcast(BF16)
                for k in range(4):
                    tt = tg * 4 + k
                    P.add('pe', lambda e, psv=psv, k=k, tt=tt, sbt=sbt: e.transpose(psv[:, k * 128:(k + 1) * 128], sbt[:, tt * 128:(tt + 1) * 128], self.ident_b),
                          reads=[sbb, self.b_cb], writes=[pb])
                s_, s_b = st.next()
                P.add('act' if tg % 2 else 'dve', (lambda e, s_=s_, psv=psv: e.copy(s_[:], psv[:, :])) if tg % 2 else (lambda e, s_=s_, psv=psv: e.tensor_copy(s_[:], psv[:, :])),
                      reads=[pb], writes=[s_b])
                P.dma(self.dnc_d[tg * 512:(tg + 1) * 512, ct * 128:(ct + 1) * 128].rearrange("(k p) c -> p k c", p=128),
                      s_[:].rearrange("p (k c) -> p k c", k=4), reads=[s_b], writes=[self.dbuf(('dnc', l))])

    def phase_DN(self, l):
        P = self.P
        T, NT = self.T, self.NT
        self.phase_DNconv(l)
        P.begin_phase()
        par = P.sbuf("D_par", [128, 2, 2, 4], F32)
        gain = P.sbuf("D_gain", [128, 512], F32)
        b_par = Buf()
        P.dma(par[:], self.dn_par_in[l], writes=[b_par])
        P.dma(gain[:], self.dn_gain_in[l], writes=[b_par])
        P.add('act', lambda e: e.activation(out=par[:, 0], in_=par[:, 0], func=AF.Exp), reads=[b_par], writes=[b_par])
        P.add('dve', lambda e: e.tensor_scalar(out=par[:, 0], in0=par[:, 0], scalar1=-1.0, scalar2=None, op0=ALU.mult), reads=[b_par], writes=[b_par])
        S = P.sbuf("D_S", [128, 4, 128], F32)
        Sb = P.sbuf("D_Sb", [128, 4, 128], BF16)
        b_S = [Buf() for _ in range(4)]
        b_Sb = [Buf() for _ in range(4)]
        cin = Pool([(P.sbuf(f"D_c{i}", [128, 1536], BF16), Buf()) for i in range(2)])
        zin = Pool([(P.sbuf(f"D_z{i}", [128, 512], F32), Buf()) for i in range(2)])
        abin = Pool([(P.sbuf(f"D_ab{i}", [128, 16], F32), Buf()) for i in range(2)])
        sm = Pool([(P.sbuf(f"D_sm{i}", [128, 12, 4], F32), Buf()) for i in range(2)])
        sqk = P.sbuf("D_sqk", [128, 1024], F32)
        b_sqk = Buf()
        tokb = Pool([(P.sbuf(f"D_tb{i}", [128, 3, 512], BF16), Buf()) for i in range(2)])
        kdec = Pool([(P.sbuf(f"D_kd{i}", [128, 512], BF16), Buf()) for i in range(2)])
        vb4 = Pool([(P.sbuf(f"D_vb4{i}", [128, 4, 128], F32), Buf()) for i in range(3)])
        vbe = Pool([(P.sbuf(f"D_vbe{i}", [128, 512], F32), Buf()) for i in range(2)])
        fm = [P.sbuf(f"D_fm{i}", [128, 3, 4, 128], BF16) for i in range(3)]
        b_fm = [Buf() for _ in range(3)]
        qTp = Pool([(P.sbuf(f"D_qT{i}", [128, 128], BF16), Buf()) for i in range(3)])
        gbc = Pool([(P.sbuf(f"D_gbc{i}", [128, 2, 128], F32), Buf()) for i in range(3)])
        Wm = Pool([(P.sbuf(f"D_W{i}", [128, 3, 128], F32), Buf()) for i in range(3)])
        Nm = Pool([(P.sbuf(f"D_N{i}", [128, 2, 128], F32), Buf()) for i in range(3)])
        FT = Pool([(P.sbuf(f"D_FT{i}", [128, 128], F32), Buf()) for i in range(3)])
        Yb = Pool([(P.sbuf(f"D_Yb{i}", [128, 128], BF16), Buf()) for i in range(3)])
        aTb = Pool([(P.sbuf(f"D_aT{i}", [128, 128], BF16), Buf()) for i in range(3)])
        last4 = Pool([(P.sbuf(f"D_l4{i}", [128, 4], F32), Buf()) for i in range(3)])
        rhsc = Pool([(P.sbuf(f"D_rc{i}", [128, 128], BF16), Buf()) for i in range(4)])
        vnew = Pool([(P.sbuf(f"D_vn{i}", [128, 128], BF16), Buf()) for i in range(4)])
        ost = Pool([(P.sbuf(f"D_o{i}", [128, 512], F32), Buf()) for i in range(2)])
        o1t = Pool([(P.sbuf(f"D_o1{i}", [128, 512], F32), Buf()) for i in range(2)])
        sq = P.sbuf("D_sq", [128, 512], F32)
        ss = P.sbuf("D_ss", [128, 4], F32)
        b_sq, b_ss = Buf(), Buf()
        gg = P.sbuf("D_gg", [128, 512], F32)
        b_gg = Buf()
        yb = Pool([(P.sbuf(f"D_y{i}", [128, 512], BF16), Buf()) for i in range(2)])
        yst = Pool([(P.sbuf(f"D_ys{i}", [128, 4, 128], BF16), Buf()) for i in range(2)])
        for i in range(3):
            P.add('pool', lambda e, i=i: e.memset(fm[i][:], 0.0), writes=[b_fm[i]])
        psA = Pool([(self.psb[i], self.psbuf[i]) for i in (0, 1)])
        psW = Pool([(self.psb[i], self.psbuf[i]) for i in (2, 3)])
        psN = Pool([(self.psb[i], self.psbuf[i]) for i in (4,)])
        psO = Pool([(self.psb[i], self.psbuf[i]) for i in (5,)])
        psS = Pool([(self.psb[i], self.psbuf[i]) for i in (6, 7)])
        ind4 = self.cf[:, 16:20, :]
        kq = 0
        for d in (0, 1):
            cb = 2 if d == 0 else 6
            Mrem, M1 = self.cf[:, cb, :], self.cf[:, cb + 1, :]
            mk_ij_s = self.cf[:, 12 if d == 0 else 10, :]
            mk_ji_s = self.cf[:, 10 if d == 0 else 12, :]
            mk_ji_i = self.cf[:, 11 if d == 0 else 13, :]
            if d == 0:
                P.add('pool', lambda e: e.memset(S[:], 0.0), writes=b_S)
                P.add('pool', lambda e: e.memset(Sb[:], 0.0), writes=b_Sb)
            else:
                self.exchange_state(l, 'dn', S, b_S, Sb, b_Sb)
            tiles = range(NT) if d == 0 else range(NT - 1, -1, -1)
            for tt in tiles:
                r0 = tt * 128
                c_, cb_ = cin.next()
                P.dma(c_[:], self.dnc_d[r0:r0 + 128, :], reads=[self.dbuf(('dnc', l))], writes=[cb_])
                ab, abb = abin.next()
                P.dma(ab[:], self.Utok[r0:r0 + 128, C_DN_AB:C_DN_AB + 16], reads=[self.dbuf(('Utok', l))], writes=[abb])
                if d == 1:
                    z, zb = zin.next()
                    P.dma(z[:], self.Utok[r0:r0 + 128, C_DN_Z:C_DN_Z + 512], reads=[self.dbuf(('Utok', l))], writes=[zb])
                m, mb = sm.next()
                P.add('pool', lambda e, c_=c_: e.tensor_tensor(out=sqk[:], in0=c_[:, 0:1024], in1=c_[:, 0:1024], op=ALU.mult), reads=[cb_], writes=[b_sqk])
                P.add('dve', lambda e, m=m: e.tensor_reduce(out=m[:, 0:2, :].rearrange("p a h -> p (a h)"), in_=sqk[:].rearrange("p (a d) -> p a d", a=8),
                                                            axis=AX.X, op=ALU.add), reads=[b_sqk], writes=[mb])
                P.add('dve', lambda e, m=m: e.tensor_scalar(out=m[:, 0:2, :], in0=m[:, 0:2, :], scalar1=EPS, scalar2=None, op0=ALU.add), reads=[mb], writes=[mb])
                P.add('act', lambda e, m=m: e.activation(out=m[:, 0:2, :], in_=m[:, 0:2, :], func=AF.Sqrt), reads=[mb], writes=[mb])
                P.add('dve', lambda e, m=m: e.reciprocal(m[:, 0:2, :], m[:, 0:2, :]), reads=[mb], writes=[mb])
                P.add('dve', lambda e, m=m: e.tensor_scalar(out=m[:, 0, :], in0=m[:, 0, :], scalar1=128.0 ** -0.5, scalar2=None, op0=ALU.mult), reads=[mb], writes=[mb])
                P.add('dve', lambda e, m=m, ab=ab, d=d: e.tensor_tensor(out=m[:, 2, :], in0=ab[:, 4 * d:4 * d + 4], in1=par[:, 1, d, :], op=ALU.add),
                      reads=[abb, b_par], writes=[mb])
                P.add('act', lambda e, m=m: e.activation(out=m[:, 2, :], in_=m[:, 2, :], func=AF.Exp), reads=[mb], writes=[mb])
                P.add('act', lambda e, m=m: e.activation(out=m[:, 2, :], in_=m[:, 2, :], func=AF.Ln, bias=1.0), reads=[mb], writes=[mb])
                P.add('dve', lambda e, m=m, d=d: e.tensor_tensor(out=m[:, 2, :], in0=m[:, 2, :], in1=par[:, 0, d, :], op=ALU.mult), reads=[mb, b_par], writes=[mb])
                P.add('act', lambda e, m=m, ab=ab, d=d: e.activation(out=m[:, 3, :], in_=ab[:, 8 + 4 * d:12 + 4 * d], func=AF.Sigmoid), reads=[abb], writes=[mb])
                P.add('act', lambda e, m=m: e.activation(out=m[:, 4, :], in_=m[:, 3, :], func=AF.Ln), reads=[mb], writes=[mb])
                pa, pab = psA.next()
                P.add('pe', lambda e, pa=pa, m=m, M1=M1: e.matmul(pa[:, 0:4], M1, m[:, 2, :], start=True, stop=True), reads=[mb, self.b_cf], writes=[pab])
                P.add('pe', lambda e, pa=pa, m=m, Mrem=Mrem: e.matmul(pa[:, 4:8], Mrem, m[:, 2, :], start=True, stop=True), reads=[mb, self.b_cf], writes=[pab])
                P.add('dve', lambda e, pa=pa, m=m: e.tensor_copy(m[:, 5:7, :].rearrange("p a h -> p (a h)"), pa[:, 0:8]), reads=[pab], writes=[mb])
                P.add('act', lambda e, m=m: e.activation(out=m[:, 7:9, :], in_=m[:, 5:7, :], func=AF.Exp), reads=[mb], writes=[mb])
                P.add('dve', lambda e, m=m: e.tensor_tensor(out=m[:, 9, :], in0=m[:, 5, :], in1=m[:, 4, :], op=ALU.add), reads=[mb], writes=[mb])
                P.add('dve', lambda e, m=m: e.tensor_scalar(out=m[:, 10, :], in0=m[:, 5, :], scalar1=-1.0, scalar2=None, op0=ALU.mult), reads=[mb], writes=[mb])
                P.add('dve', lambda e, m=m: e.scalar_tensor_tensor(out=m[:, 11, :], in0=m[:, 3, :], scalar=-1.0, in1=m[:, 7, :], op0=ALU.mult, op1=ALU.mult),
                      reads=[mb], writes=[mb])
                tb, tbb = tokb.next()
                kdt, kdb = kdec.next()
                ve, veb = vbe.next()
                c3 = c_[:].rearrange("p (a h d) -> p a h d", a=3, h=4)
                bc = lambda k, m=m: m[:, k, :].unsqueeze(2).to_broadcast([128, 4, 128])
                b0, b1, b3, b7, b8 = bc(0), bc(1), bc(3), bc(7), bc(8)
                v4h = lambda ap: ap.rearrange("p (h d) -> p h d", h=4)
                P.add('dve', lambda e, o_=v4h(tb[:, 0, :]), i0=c3[:, 1], i1=b1: e.tensor_tensor(out=o_, in0=i0, in1=i1, op=ALU.mult), reads=[cb_, mb], writes=[tbb])
                P.add('pool', lambda e, o_=v4h(tb[:, 1, :]), i0=c3[:, 0], i1=b0: e.tensor_tensor(out=o_, in0=i0, in1=i1, op=ALU.mult), reads=[cb_, mb], writes=[tbb])
                P.add('pool', lambda e, o_=v4h(tb[:, 2, :]), i0=v4h(tb[:, 1, :]), i1=b7: e.tensor_tensor(out=o_, in0=i0, in1=i1, op=ALU.mult), reads=[tbb, mb], writes=[tbb])
                P.add('dve', lambda e, o_=v4h(kdt[:]), i0=v4h(tb[:, 0, :]), i1=b8: e.tensor_tensor(out=o_, in0=i0, in1=i1, op=ALU.mult), reads=[tbb, mb], writes=[kdb])
                P.add('pool', lambda e, o_=v4h(ve[:]), i0=c3[:, 2], i1=b3: e.tensor_tensor(out=o_, in0=i0, in1=i1, op=ALU.mult), reads=[cb_, mb], writes=[veb])
                po, pob = psO.next()
                for h in range(4):
                    hs = slice(h * 128, (h + 1) * 128)
                    k3 = kq % 3
                    kq += 1
                    f = fm[k3]
                    fbuf = b_fm[k3]
                    pt, ptb = psA.next()
                    ptv = pt[:, 0:256].bitcast(BF16)
                    for k in range(3):
                        P.add('pe', lambda e, ptv=ptv, k=k, tb=tb, hs=hs: e.transpose(ptv[:, k * 128:(k + 1) * 128], tb[:, k, hs], self.ident_b),
                              reads=[tbb, self.b_cb], writes=[ptb])
                    ff_ = f[:]
                    base = ff_.offset
                    pstr = list(ff_.ap[0])
                    k4_out = bass.AP(ff_.tensor, base + 512, [pstr, [160, 4], [1, 32]])
                    q4_out = bass.AP(ff_.tensor, base + 1024, [pstr, [160, 4], [1, 32]])
                    qt, qtb = qTp.next()
                    P.add('act', lambda e, f=f, ptv=ptv: e.copy(f[:, 0, 0, :], ptv[:, 0:128]), reads=[ptb], writes=[fbuf])
                    P.add('dve', lambda e, k4_out=k4_out, ptv=ptv: e.tensor_copy(k4_out, ptv[:, 0:128].rearrange("p (c n) -> p c n", c=4)), reads=[ptb], writes=[fbuf])
                    P.add('act', lambda e, qt=qt, ptv=ptv: e.copy(qt[:], ptv[:, 128:256]), reads=[ptb], writes=[qtb])
                    P.add('dve', lambda e, q4_out=q4_out, ptv=ptv: e.tensor_copy(q4_out, ptv[:, 256:384].rearrange("p (c n) -> p c n", c=4)), reads=[ptb], writes=[fbuf])
                    kT = f[:, 0, 0, :]
                    gb, gbb = gbc.next()
                    P.add('pool', lambda e, gb=gb, m=m, h=h: e.tensor_scalar(out=gb[:, 0, :], in0=self.ones_f, scalar1=m[:, 2, h:h + 1], scalar2=None, op0=ALU.mult),
                          reads=[mb, self.b_cf], writes=[gbb])
                    P.add('pool', lambda e, gb=gb, m=m, h=h: e.tensor_scalar(out=gb[:, 1, :], in0=self.ones_f, scalar1=m[:, 4, h:h + 1], scalar2=None, op0=ALU.mult),
                          reads=[mb, self.b_cf], writes=[gbb])
                    pw, pwb = psW.next()
                    P.add('pe', lambda e, pw=pw, gb=gb, M1=M1: e.matmul(pw[:, 0:128], gb[:, 0, :], M1, start=True, stop=False), reads=[gbb, self.b_cf], writes=[pwb])
                    P.add('pe', lambda e, pw=pw, mk=mk_ij_s: e.matmul(pw[:, 0:128], self.ident_f, mk, start=False, stop=True), reads=[self.b_cf], writes=[pwb])
                    P.add('pe', lambda e, pw=pw, gb=gb, M1=M1: e.matmul(pw[:, 128:256], gb[:, 0, :], M1, start=True, stop=False), reads=[gbb, self.b_cf], writes=[pwb])
                    P.add('pe', lambda e, pw=pw, gb=gb: e.matmul(pw[:, 128:256], gb[:, 1, :], self.ident_f, start=False, stop=False), reads=[gbb, self.b_cf], writes=[pwb])
                    P.add('pe', lambda e, pw=pw, mk=mk_ji_s: e.matmul(pw[:, 128:256], self.cf[:, 20, :], mk, start=False, stop=True), reads=[self.b_cf], writes=[pwb])
                    P.add('pe', lambda e, pw=pw, gb=gb, M1=M1: e.matmul(pw[:, 256:384], gb[:, 0, :], M1, start=True, stop=False), reads=[gbb, self.b_cf], writes=[pwb])
                    P.add('pe', lambda e, pw=pw, mk=mk_ji_i: e.matmul(pw[:, 256:384], self.cf[:, 20, :], mk, start=False, stop=True), reads=[self.b_cf], writes=[pwb])
                    P.add('pe', lambda e, pw=pw, kT=kT: e.matmul(pw[:, 384:512], kT, kT, start=True, stop=True), reads=[fbuf], writes=[pwb])
                    W, Wb = Wm.next()
                    P.add('act', lambda e, W=W, pw=pw, m=m, h=h: e.activation(out=W[:, 0, :], in_=pw[:, 0:128], func=AF.Exp, scale=-1.0, bias=m[:, 9, h:h + 1]),
                          reads=[pwb, mb], writes=[Wb])
                    P.add('act', lambda e, W=W, pw=pw, m=m, h=h: e.activation(out=W[:, 1:3, :].rearrange("p a n -> p (a n)"), in_=pw[:, 128:384], func=AF.Exp,
                                                                              bias=m[:, 10, h:h + 1]), reads=[pwb, mb], writes=[Wb])
                    Nt, Nb = Nm.next()
                    P.add('dve', lambda e, Nt=Nt, pw=pw, W=W: e.tensor_tensor(out=Nt[:, 0, :], in0=pw[:, 384:512], in1=W[:, 0, :], op=ALU.mult), reads=[pwb, Wb], writes=[Nb])
                    P.add('pool', lambda e, Nt=Nt: e.tensor_tensor(out=Nt[:, 0, :], in0=Nt[:, 0, :], in1=self.ident_f, op=ALU.add), reads=[Nb, self.b_cf], writes=[Nb])
                    P.add('dve', lambda e, Nt=Nt, pw=pw, W=W: e.tensor_tensor(out=Nt[:, 1, :], in0=pw[:, 384:512], in1=W[:, 1, :], op=ALU.mult), reads=[pwb, Wb], writes=[Nb])
                    P.add('pool', lambda e, Nt=Nt: e.tensor_tensor(out=Nt[:, 1, :], in0=self.ident_f, in1=Nt[:, 1, :], op=ALU.subtract), reads=[Nb, self.b_cf], writes=[Nb])
                    pq, pqb = psA.next()
                    P.add('pe', lambda e, pq=pq, kT=kT, qt=qt: e.matmul(pq[:, 0:128], kT, qt[:], start=True, stop=True), reads=[fbuf, qtb], writes=[pqb])
                    at, atb = aTb.next()
                    P.add('dve', lambda e, at=at, pq=pq, W=W: e.tensor_tensor(out=at[:], in0=pq[:, 0:128], in1=W[:, 2, :], op=ALU.mult), reads=[pqb, Wb], writes=[atb])
                    P.add('pe', lambda e, pq=pq, gb=gb: e.matmul(pq[:, 128:132], gb[:, 0, :], self.cf[:, 16:20, 0], start=True, stop=True), reads=[gbb, self.b_cf], writes=[pqb])
                    l4, l4b = last4.next()
                    P.add('act', lambda e, l4=l4, pq=pq: e.activation(out=l4[:], in_=pq[:, 128:132], func=AF.Exp), reads=[pqb], writes=[l4b])
                    for it in range(4):
                        pn, pnb = psN.next()
                        P.add('pe', lambda e, pn=pn, Nt=Nt: e.matmul(pn[:, 0:128], Nt[:, 1, :], Nt[:, 0, :], start=True, stop=True), reads=[Nb], writes=[pnb])
                        ft, ftb = FT.next()
                        P.add('dve', lambda e, ft=ft, pn=pn: e.tensor_tensor(out=ft[:], in0=self.ident_f, in1=pn[:, 0:128], op=ALU.subtract), reads=[pnb, self.b_cf], writes=[ftb])
                        P.add('pe', lambda e, pn=pn, ft=ft, Nt=Nt: e.matmul(pn[:, 128:256], ft[:], Nt[:, 1, :], start=True, stop=True), reads=[ftb, Nb], writes=[pnb])
                        P.add('dve', lambda e, Nt=Nt, pn=pn: e.tensor_tensor(out=Nt[:, 1, :], in0=Nt[:, 1, :], in1=pn[:, 128:256], op=ALU.add), reads=[pnb, Nb], writes=[Nb])
                    ybf, ybb_ = Yb.next()
                    P.add('act', lambda e, ybf=ybf, Nt=Nt: e.copy(ybf[:], Nt[:, 1, :]), reads=[Nb], writes=[ybb_])
                    v4, v4b = vb4.next()
                    P.add('pool', lambda e, v4=v4, ve=ve, hs=hs: e.tensor_tensor(out=v4[:], in0=ve[:, hs].unsqueeze(1).to_broadcast([128, 4, 128]), in1=ind4, op=ALU.mult),
                          reads=[veb, self.b_cf], writes=[v4b])
                    corder = (0, 1, 2, 3) if d == 0 else (3, 2, 1, 0)
                    for ci, c in enumerate(corder):
                        pS, pSb = psS.next()
                        P.add('pe', lambda e, pS=pS, f=f, c=c, h=h: e.matmul(pS[:, 0:128], f[:, 1, c, :], Sb[:, h, :], start=True, stop=True),
                              reads=[fbuf, b_Sb[h]], writes=[pSb])
                        P.add('pe', lambda e, po=po, hs=hs, f=f, c=c, h=h, ci=ci: e.matmul(po[:, hs], f[:, 2, c, :], Sb[:, h, :], start=(ci == 0), stop=False),
                              reads=[fbuf, b_Sb[h]], writes=[pob])
                        rc, rcb = rhsc.next()
                        P.add('dve', lambda e, rc=rc, pS=pS, m=m, h=h, v4=v4, c=c: e.scalar_tensor_tensor(
                            out=rc[:], in0=pS[:, 0:128], scalar=m[:, 11, h:h + 1], in1=v4[:, c, :], op0=ALU.mult, op1=ALU.add), reads=[pSb, mb, v4b], writes=[rcb])
                        P.add('pe', lambda e, pS=pS, ybf=ybf, rc=rc: e.matmul(pS[:, 128:256], ybf[:], rc[:], start=True, stop=True), reads=[ybb_, rcb], writes=[pSb])
                        vn, vnb = vnew.next()
                        P.add('act', lambda e, vn=vn, pS=pS: e.copy(vn[:], pS[:, 128:256]), reads=[pSb], writes=[vnb])
                        P.add('pe', lambda e, po=po, hs=hs, at=at, vn=vn, ci=ci: e.matmul(po[:, hs], at[:], vn[:], start=False, stop=(ci == 3)),
                              reads=[atb, vnb], writes=[pob])
                        P.add('pe', lambda e, pS=pS, kdt=kdt, hs=hs, vn=vn: e.matmul(pS[:, 256:384], kdt[:, hs], vn[:], start=True, stop=True), reads=[kdb, vnb], writes=[pSb])
                        P.add('dve', lambda e, pS=pS, h=h, l4=l4, c=c: e.scalar_tensor_tensor(
                            out=S[:, h, :], in0=S[:, h, :], scalar=l4[:, c:c + 1], in1=pS[:, 256:384], op0=ALU.mult, op1=ALU.add),
                            reads=[pSb, l4b, b_S[h]], writes=[b_S[h]])
                        P.add('act', lambda e, h=h: e.copy(Sb[:, h, :], S[:, h, :]), reads=[b_S[h]], writes=[b_Sb[h]])
                if d == 0:
                    o, ob = ost.next()
                    P.add('act', lambda e, o=o, po=po: e.copy(o[:], po[:]), reads=[pob], writes=[ob])
                    P.dma(self.o1dn_d[r0:r0 + 128, :], o[:], reads=[ob], writes=[self.dbuf(('o1dn', l))])
                else:
                    o1, o1b = o1t.next()
                    P.dma(o1[:], self.o1dn_d[r0:r0 + 128, :], reads=[self.dbuf(('o1dn', l))], writes=[o1b])
                    o, ob = ost.next()
                    P.add('dve', lambda e, o=o, po=po, o1=o1: e.tensor_tensor(out=o[:], in0=po[:], in1=o1[:], op=ALU.add), reads=[pob, o1b], writes=[ob])
                    self.norm_gate_emit(l, o, ob, z[:], zb, gain, b_par, sq, b_sq, ss, b_ss, gg, b_gg, yb, yst, 1536, r0)

    def fake_mixer(self, l):
        P = self.P
        P.begin_phase()
        T = self.T
        st = Pool([(P.sbuf(f"fk{i}", [128, T], F32), Buf()) for i in range(2)])
        sb = Pool([(P.sbuf(f"fkb{i}", [128, T], BF16), Buf()) for i in range(2)])
        rows = [C_HG_Q + i * 128 for i in range(12)] + [C_DN_QKV + i * 128 for i in range(4)]
        for i, r0 in enumerate(rows):
            s, sbf = st.next()
            b, bbf = sb.next()
            P.dma(s[:], self.UT[r0:r0 + 128, :], reads=[self.dbuf(('UT', l))], writes=[sbf])
            P.add('dve', lambda e, s=s, b=b: e.tensor_copy(b[:], s[:]), reads=[sbf], writes=[bbf])
            P.dma(self.yT[i * 128:(i + 1) * 128, :], b[:], reads=[bbf], writes=[self.dbuf(('yT', l))])

    def dump_debug(self):
        P = self.P
        if 'UT' in self.debug:
            P.barrier()
            P.dma(self.dbg_UT, self.UT, reads=[self.dbuf(('UT', 0))])
            P.dma(self.dbg_Utok, self.Utok, reads=[self.dbuf(('Utok', 0))])

    def build(self):
        self.setup()
        self.compute_mod()
        self.convert_weights()
        if not self.couple:
            self.zero_halos()
        if 'hg' in self.mixers:
            self.compute_lb()
        for l in range(self.depth):
            x_src = self.xT_in if l == 0 else self.xs
            x_dst = self.yT_out if l == self.depth - 1 else self.xs
            self.phase_A(l, x_src)
            if 'UT' in self.debug and l == 0:
                self.dump_debug()
            if 'fake' in self.mixers:
                self.fake_mixer(l)
            else:
                self.phase_B1(l)
                if self.couple:
                    self.exchange_halos(l)
                zr = []
                if 'swa' in self.mixers:
                    self.phase_SWA(l)
                else:
                    zr += [i * 128 for i in range(0, 4)]
                if 'hg' in self.mixers:
                    self.phase_HG(l)
                else:
                    zr += [i * 128 for i in range(4, 8)]
                if 'na' in self.mixers:
                    self.phase_NA(l)
                else:
                    zr += [i * 128 for i in range(8, 12)]
                if 'dn' in self.mixers:
                    self.phase_DN(l)
                else:
                    zr += [i * 128 for i in range(12, 16)]
                if zr:
                    self.zero_y(l, zr)
            self.phase_C(l, x_src, x_dst)
        self.P.finalize()
        return self.nc


def make_consts():
    c = np.zeros((128, NCONST, 128), np.float32)
    c[:, 0, :] = np.eye(128)
    c[:, 1, :] = 1.0
    t = np.arange(128)[:, None]
    i = np.arange(128)[None, :]
    same = (t // 32) == (i // 32)
    c[:, 2, :] = same & (t > i)
    c[:, 3, :] = same & (t <= i)
    c[:, 5, :] = same & (t <= i)
    c[:, 6, :] = same & (t < i)
    c[:, 7, :] = same & (t >= i)
    c[:, 9, :] = same & (t >= i)
    for cc in range(4):
        c[:, 16 + cc, :] = ((np.arange(128) // 32) == cc)[:, None]
    c[:, 20, :] = -np.eye(128)
    same = (t // 32) == (i // 32)
    BIG = 30000.0
    c[:, 10, :] = np.where(same & (t < i), 0.0, BIG)
    c[:, 11, :] = np.where(same & (t <= i), 0.0, BIG)
    c[:, 12, :] = np.where(same & (t > i), 0.0, BIG)
    c[:, 13, :] = np.where(same & (t >= i), 0.0, BIG)
    c[:, 14, :] = same
    c[:, 15, 0:64] = (t < 64)
    c[:, 15, :] = 0.0
    c[:, 15, 0] = (np.arange(128) < 64)
    c[:, 15, 1] = (np.arange(128) >= 64)
    return c


def arrange_vec(v):
    n = v.shape[-1] // 128
    return np.ascontiguousarray(np.swapaxes(v.reshape(*v.shape[:-1], n, 128), -1, -2))


NEG = -30000.0


def rope_tables(pos):
    T = len(pos)
    half = 16
    inv_freq = np.power(np.float32(500000.0), -np.arange(half, dtype=np.float32) / np.float32(half)).astype(np.float32)
    ang = pos.astype(np.float32)[:, None] * inv_freq[None, :]
    cs = np.stack([np.cos(ang), np.sin(ang)], axis=1).astype(np.float32)
    return np.ascontiguousarray(cs.reshape(T // 128, 128, 2, 16).transpose(1, 0, 2, 3))


def swa_masks(coupled):
    p = np.arange(128)[:, None]
    c = np.arange(128)[None, :]
    m = np.full((128, 4, 128), NEG, np.float32)
    ai = np.abs(p - 64 - c) <= 64
    m[:, 0, :] = np.where(ai & (p >= 64), 0.0, NEG)
    m[:, 1, :] = np.where(ai, 0.0, NEG)
    bi = np.abs(p + 64 - c) <= 64
    m[:, 2, :] = np.where(bi, 0.0, NEG)
    bl = np.where(p < 64, bi, ((p - 64) + c >= 127) & bool(coupled))
    m[:, 3, :] = np.where(bl, 0.0, NEG)
    return m


def na_bias_mats(rpb, tok_own, tok_partner, Ls, T):
    rows = Ls // 64
    J = T // 128
    jm = J // 2
    specs = [(0, [('o', 0), ('o', 1), ('o', 2), ('o', 3)]),
             (1, [('o', 0), ('o', 1), ('o', 2), ('o', 3)]),
             (jm, [('o', jm - 2), ('o', jm - 1), ('o', jm), ('o', jm + 1), ('o', jm + 2)]),
             (J - 2, [('o', J - 4), ('o', J - 3), ('o', J - 2), ('o', J - 1), ('h', 0)]),
             (J - 1, [('o', J - 4), ('o', J - 3), ('o', J - 2), ('o', J - 1), ('h', 0), ('h', 1)])]
    out = np.full((4, 128, 24, 128), NEG, np.float32)
    idx = 0
    for (j, keys) in specs:
        tq = tok_own[j * 128:(j + 1) * 128]
        rq, cq = tq // 64, tq % 64
        r0 = np.clip(rq - 4, 0, rows - 8)
        c0 = np.clip(cq - 8, 0, 64 - 16)
        for (kind, kt) in keys:
            if kind == 'o':
                tk = tok_own[kt * 128:(kt + 1) * 128]
            elif tok_partner is not None:
                pt = J - 1 - kt
                tk = tok_partner[pt * 128:(pt + 1) * 128]
            else:
                tk = None
            if tk is not None:
                rk, ck = tk // 64, tk % 64
                valid = ((rk[:, None] >= r0[None, :]) & (rk[:, None] < r0[None, :] + 8) &
                         (ck[:, None] >= c0[None, :]) & (ck[:, None] < c0[None, :] + 16))
                ro = np.clip(rk[:, None] - rq[None, :] + 7, 0, 14)
                co = np.clip(ck[:, None] - cq[None, :], -15, 15) + 15
                for h in range(4):
                    g = rpb[h][ro, co]
                    out[h, :, idx, :] = np.where(valid, g, NEG)
            idx += 1
    assert idx == 24
    return out


def rep128(v):
    return np.ascontiguousarray(np.broadcast_to(v[None, :], (128, v.shape[0])))


def extra_inputs(inp, L, T, tok, rev, sel):
    lg = np.asarray(inp['hgrn_lb_logits'])[:L]
    if rev:
        lg = lg[:, ::-1]
    hg_lb = np.ascontiguousarray(np.broadcast_to(lg[None], (128, L, 2, 512))).astype(np.float32)
    hg_lbT = np.ascontiguousarray(lg.reshape(L, 2, 4, 128).transpose(3, 0, 1, 2)).astype(np.float32)
    hg_gain = np.stack([rep128(np.tile(np.asarray(inp['hgrn_norm_g'])[l], 4)) for l in range(L)]).astype(np.float32)
    flag = np.ascontiguousarray(np.broadcast_to(np.asarray(sel, np.float32)[None, :], (128, 2)))
    return dict(hg_lb=hg_lb, hg_lbT=hg_lbT, hg_gain=hg_gain, flag=flag)


def dn_inputs(inp, L, rev):
    cw = np.asarray(inp['dn_conv_w'])[:L]
    if rev:
        cw = cw[:, ::-1]
    dn_conv = np.ascontiguousarray(cw.reshape(L, 5, 12, 128).transpose(0, 3, 2, 1)).astype(np.float32)
    al = np.asarray(inp['dn_a_log'])[:L]
    db = np.asarray(inp['dn_dt_bias'])[:L]
    if rev:
        al, db = al[:, ::-1], db[:, ::-1]
    par = np.stack([al, db], axis=1)
    dn_par = np.ascontiguousarray(np.broadcast_to(par[:, None], (L, 128, 2, 2, 4))).astype(np.float32)
    dn_gain = np.stack([rep128(np.tile(np.asarray(inp['dn_norm_g'])[l], 4)) for l in range(L)]).astype(np.float32)
    return dict(dn_conv=dn_conv, dn_par=dn_par, dn_gain=dn_gain)


W_PERM_CACHE = {}


def permute_w_in(w_in):
    idx = np.arange(INW)
    idx[C_HG_F1:C_HG_F1 + 512], idx[C_HG_F2:C_HG_F2 + 512] = np.arange(C_HG_F2, C_HG_F2 + 512), np.arange(C_HG_F1, C_HG_F1 + 512)
    for base in (C_DN_AB, C_DN_AB + 8):
        idx[base:base + 4], idx[base + 4:base + 8] = np.arange(base + 4, base + 8), np.arange(base, base + 4)
    return np.ascontiguousarray(w_in[:, :, idx])


def run_model(seqs, T, depth, inp, n_cores, mixers=('swa', 'hg', 'na', 'dn'), trace=False):
    L = depth
    roles = []
    for si, (x, c) in enumerate(seqs):
        Ls = x.shape[0]
        if Ls == 2 * T:
            if len(roles) % 2:
                roles.append(None)
            t0 = np.arange(T)
            t1 = 2 * T - 1 - np.arange(T)
            roles.append(dict(si=si, tok=t0, ptok=t1, rev=False, sel=(0.0, 1.0), Ls=Ls))
            roles.append(dict(si=si, tok=t1, ptok=t0, rev=True, sel=(1.0, 0.0), Ls=Ls))
        else:
            assert Ls == T
            roles.append(dict(si=si, tok=np.arange(T), ptok=None, rev=False, sel=(0.0, 0.0), Ls=Ls))
    assert len(roles) <= n_cores
    real = [r for r in roles if r is not None]
    k = 0
    while len(roles) < n_cores:
        roles.append(dict(real[k % len(real)], dup=True) if True else None)
        k += 1
    roles = [r if r is not None else dict(real[0], dup=True) for r in roles]
    import os
    b = Builder(T, depth=L, mixers=mixers, couple=(os.environ.get('COUPLE', '1') == '1'), n_cores=n_cores)
    nc = b.build()
    w_in = np.asarray(inp['w_in'])[:L]
    w_in_rev = permute_w_in(w_in) if any(r['rev'] for r in roles) else None
    shared = dict(ada_w=np.asarray(inp['ada_w'])[:L], ada_b=arrange_vec(np.asarray(inp['ada_b'])[:L]),
                  ng=np.stack([arrange_vec(np.asarray(inp['norm1_g'])[:L]), arrange_vec(np.asarray(inp['norm2_g'])[:L])], axis=1),
                  w_out=np.asarray(inp['w_out'])[:L], w1=np.asarray(inp['w_mlp_in'])[:L], w2=np.asarray(inp['w_mlp_out'])[:L],
                  consts=make_consts(),
                  gains=np.stack([np.stack([rep128(np.tile(np.asarray(inp[kk])[l], 4)) for kk in ('swa_q_norm', 'swa_k_norm', 'na_q_norm', 'na_k_norm')])
                                  for l in range(L)]).astype(np.float32))
    in_maps = []
    cache = {}
    for r in roles:
        key = (r['si'], r['rev'])
        if key in cache:
            in_maps.append(cache[key])
            continue
        x, c = seqs[r['si']]
        m = dict(shared)
        m['xT'] = np.ascontiguousarray(np.asarray(x)[r['tok']].T)
        m['cvec'] = arrange_vec(np.asarray(c))
        m['w_in'] = w_in_rev if r['rev'] else w_in
        m['rope'] = rope_tables(r['tok'])
        m['swa_mask'] = swa_masks(r['ptok'] is not None)
        m['na_bias'] = np.stack([na_bias_mats(np.asarray(inp['na_rpb'])[l], r['tok'], r['ptok'], r['Ls'], T) for l in range(L)])
        m.update(extra_inputs(inp, L, T, r['tok'], r['rev'], r['sel']))
        m.update(dn_inputs(inp, L, r['rev']))
        cache[key] = m
        in_maps.append(m)
    res = run_bass_kernel_spmd(nc, in_maps, core_ids=list(range(n_cores)), **({'trace': True} if trace else {}))
    outs = [np.zeros((x.shape[0], D), np.float32) for (x, c) in seqs]
    for ci, r in enumerate(roles):
        if r.get('dup'):
            continue
        y = np.asarray(res.results[ci]['yT']).T
        outs[r['si']][r['tok']] = y
    return outs, res


def kernel(x_prompt, x_sample, c_prompt, c_sample, **w):
    x_prompt, x_sample = np.asarray(x_prompt), np.asarray(x_sample)
    c_prompt, c_sample = np.asarray(c_prompt), np.asarray(c_sample)
    T = 8192
    seqs = [(x_sample[0], c_sample[0]), (x_sample[1], c_sample[1]), (x_prompt[0], c_prompt[0]), (x_prompt[1], c_prompt[1])]
    outs, _ = run_model(seqs, T, DEPTH, w, 8)
    y_sample = np.stack([outs[0], outs[1]]).astype(np.float32)
    y_prompt = np.stack([outs[2], outs[3]]).astype(np.float32)
    return (y_prompt, y_sample)
```

```python
from contextlib import ExitStack
import numpy as np
import ml_dtypes
import concourse.bass as bass
import concourse.mybir as mybir
from concourse.bass_utils import run_bass_kernel_spmd

F32 = mybir.dt.float32
BF16 = mybir.dt.bfloat16
AF = mybir.ActivationFunctionType
ALU = mybir.AluOpType
AX = mybir.AxisListType

D = 2048
NCH = 16
DFF = 8192
INW = 7696
DEPTH = 4
EPS = 1e-6
NCONST = 24

C_SWA_Q, C_SWA_K, C_SWA_V = 0, 512, 1024
C_HG_Q, C_HG_F1, C_HG_F2, C_HG_I, C_HG_G = 1536, 2048, 2560, 3072, 3584
C_NA_Q, C_NA_K, C_NA_V = 4096, 4608, 5120
C_DN_QKV, C_DN_Z, C_DN_AB = 5632, 7168, 7680

ENG_ATTR = {'pe': 'tensor', 'act': 'scalar', 'dve': 'vector', 'pool': 'gpsimd', 'sp': 'sync'}
SEM_ROT = 30000


class Buf:
    __slots__ = ('name', 'w', 'r')

    def __init__(self, name=''):
        self.name = name
        self.w = None
        self.r = []


class Op:
    __slots__ = ('eng', 'fn', 'deps', 'signal', 'sem', 'val', 'inc', 'dma', 'idx')

    def __init__(self, eng, fn, dma=False):
        self.eng = eng
        self.fn = fn
        self.deps = []
        self.signal = dma
        self.sem = None
        self.val = 0
        self.inc = 16 if dma else 1
        self.dma = dma


class Prog:
    def __init__(self, nc, n_dma_sems=20):
        self.nc = nc
        self.es = ExitStack()
        self.ops = {e: [] for e in ENG_ATTR}
        self.dma_ops = {e: [] for e in ENG_ATTR}
        self.n_dma_sems = n_dma_sems
        self.dma_sems = {}
        self.nops = 0
        self.phase_es = None

    def sem(self, name):
        return self.es.enter_context(self.nc.semaphore(name))

    def sbuf(self, name, shape, dt, perm=False):
        es = self.es if (perm or self.phase_es is None) else self.phase_es
        self.nsb = getattr(self, 'nsb', 0) + 1
        return es.enter_context(self.nc.sbuf_tensor(f"{name}_{self.nsb}", list(shape), dt))

    def psum(self, name, shape, dt):
        return self.es.enter_context(self.nc.psum_tensor(name, list(shape), dt))

    def begin_phase(self):
        self.barrier()
        if self.phase_es is not None:
            self.phase_es.close()
        self.phase_es = ExitStack()

    def add(self, eng, fn, reads=(), writes=(), dma=False, extra_deps=()):
        op = Op(eng, fn, dma)
        op.idx = self.nops
        self.nops += 1
        deps = []
        for b in reads:
            if b.w is not None:
                deps.append(b.w)
        for b in writes:
            if b.w is not None:
                deps.append(b.w)
            deps.extend(b.r)
        deps.extend(extra_deps)
        for b in reads:
            if not dma:
                b.r = [q for q in b.r if q.dma or q.eng != eng]
            b.r.append(op)
        for b in writes:
            b.w = op
            b.r = []
        seen = set()
        for p in deps:
            if p is op or id(p) in seen:
                continue
            seen.add(id(p))
            if (not p.dma) and (not dma) and p.eng == eng and eng == 'pe':
                continue
            op.deps.append(p)
            p.signal = True
        if dma:
            lst = self.dma_ops[eng]
            k = len(lst)
            if eng not in self.dma_sems:
                self.dma_sems[eng] = [self.sem(f"dq_{eng}_{i}") for i in range(self.n_dma_sems)]
            n = self.n_dma_sems
            op.sem = self.dma_sems[eng][k % n]
            op.val = 16 * (k // n + 1)
            if k >= n:
                op.deps.append(lst[k - n])
            lst.append(op)
        self.ops[eng].append(op)
        return op

    def dma(self, out, in_, reads=(), writes=(), eng='sp', **kw):
        return self.add(eng, lambda e: e.dma_start(out=out, in_=in_, **kw), reads, writes, dma=True)

    def _lasts(self):
        lasts = []
        for e, ops in self.ops.items():
            for op in reversed(ops):
                if not op.dma and op.fn is not None:
                    lasts.append(op)
                    break
        for e, lst in self.dma_ops.items():
            last = {}
            for op in lst:
                last[op.sem.num] = op
            lasts.extend(last.values())
        return lasts

    def barrier(self):
        lasts = self._lasts()
        for p in lasts:
            p.signal = True
        for e in ENG_ATTR:
            op = Op(e, None)
            op.deps = list(lasts)
            self.ops[e].append(op)

    def finalize(self):
        nc = self.nc
        self.barrier()
        for e, ops in self.ops.items():
            cnt = 0
            cur = None
            k = 0
            for op in ops:
                if op.dma or not op.signal:
                    continue
                if cur is None or cnt >= SEM_ROT:
                    cur = self.sem(f"cs_{e}_{k}")
                    k += 1
                    cnt = 0
                cnt += 1
                op.sem = cur
                op.val = cnt
        self.stats = {e: (sum(1 for o in ops if o.signal and not o.dma), len(ops), len(self.dma_ops[e])) for e, ops in self.ops.items()}
        with nc.Block() as block:
            for e, attr in ENG_ATTR.items():
                ops = self.ops[e]
                if not ops:
                    continue

                def body(eng, ops=ops):
                    waited = {}
                    for op in ops:
                        for p in op.deps:
                            s, v = p.sem, p.val
                            if waited.get(s.num, 0) >= v:
                                continue
                            eng.wait_ge(s, v)
                            waited[s.num] = v
                        if op.fn is None:
                            continue
                        ins = op.fn(eng)
                        if op.signal:
                            ins.then_inc(op.sem, op.inc)

                getattr(block, attr)(body)
        if self.phase_es is not None:
            self.phase_es.close()
        self.es.close()


def sl(start, n, step):
    return slice(start, start + (n - 1) * step + 1, step)


class Pool:
    def __init__(self, items):
        self.items = items
        self.i = 0

    def next(self):
        it = self.items[self.i % len(self.items)]
        self.i += 1
        return it


class Builder:
    def __init__(self, T, depth=DEPTH, debug=None, mixers=('swa', 'hg', 'na', 'dn'), couple=True, n_cores=8):
        self.T = T
        self.depth = depth
        self.debug = debug or ()
        self.mixers = mixers
        self.couple = couple
        nc = bass.Bass("TRN2", target_bir_lowering=False)
        nc.allow_low_precision("bf16 matmul operands, fp32 accumulation (reference tolerance is bf16-matmul)")
        self.nc = nc
        self.P = Prog(nc)
        L = depth
        dt = nc.dram_tensor
        self.xT_in = dt("xT", [D, T], F32, kind="ExternalInput").ap()
        self.cvec = dt("cvec", [128, NCH], F32, kind="ExternalInput").ap()
        self.ada_w = dt("ada_w", [L, D, 6 * D], F32, kind="ExternalInput").ap()
        self.ada_b = dt("ada_b", [L, 128, 96], F32, kind="ExternalInput").ap()
        self.ng = dt("ng", [L, 2, 128, NCH], F32, kind="ExternalInput").ap()
        self.w_in = dt("w_in", [L, D, INW], F32, kind="ExternalInput").ap()
        self.w_out = dt("w_out", [L, D, D], F32, kind="ExternalInput").ap()
        self.w1 = dt("w1", [L, D, DFF], F32, kind="ExternalInput").ap()
        self.w2 = dt("w2", [L, DFF, D], F32, kind="ExternalInput").ap()
        self.consts_in = dt("consts", [128, NCONST, 128], F32, kind="ExternalInput").ap()
        self.yT_out = dt("yT", [D, T], F32, kind="ExternalOutput").ap()
        self.xs = dt("xs", [D, T], F32).ap()
        self.w_in_b = dt("w_in_b", [L, D, INW], BF16).ap()
        self.w_out_b = dt("w_out_b", [L, D, D], BF16).ap()
        self.w1_b = dt("w1_b", [L, D, DFF], BF16).ap()
        self.w2_b = dt("w2_b", [L, NCH, 128, 64, 128], BF16).ap()
        self.Utok = dt("Utok", [T, INW], F32).ap()
        self.UT = dt("UT", [INW, T], F32).ap()
        self.yT = dt("yTs", [D, T], BF16).ap()
        self.dbufs = {}
        NT = T // 128
        self.NT = NT
        self.gains_in = dt("gains", [L, 4, 128, 512], F32, kind="ExternalInput").ap()
        self.rope_in = dt("rope", [128, NT, 2, 16], F32, kind="ExternalInput").ap()
        self.swa_mask_in = dt("swa_mask", [128, 4, 128], F32, kind="ExternalInput").ap()
        self.na_bias_in = dt("na_bias", [L, 4, 128, 24, 128], F32, kind="ExternalInput").ap()
        self.swa_qT = dt("swa_qT", [4, 128, T], BF16).ap()
        self.swa_kT = dt("swa_kT", [4, 128, T], BF16).ap()
        self.swa_V = dt("swa_V", [T, 512], BF16).ap()
        self.na_qT = dt("na_qT", [4, 128, T], BF16).ap()
        self.na_kT = dt("na_kT", [4, 128, T], BF16).ap()
        self.na_V = dt("na_V", [T, 512], BF16).ap()
        self.hg_lb_in = dt("hg_lb", [128, L, 2, 512], F32, kind="ExternalInput").ap()
        self.hg_lbT_in = dt("hg_lbT", [128, L, 2, 4], F32, kind="ExternalInput").ap()
        self.hg_gain_in = dt("hg_gain", [L, 128, 512], F32, kind="ExternalInput").ap()
        self.flag_in = dt("flag", [128, 2], F32, kind="ExternalInput").ap()
        self.lb_d = dt("lb_d", [L, 2, 2, 128, 512], F32).ap()
        self.lbT_d = dt("lbT_d", [L, 128, 2, 2, 4], F32).ap()
        self.o1_d = dt("o1_d", [T, 512], F32).ap()
        self.hg_state_d = dt("hg_state_d", [128, 4, 128], F32).ap()
        self.dn_conv_in = dt("dn_conv", [L, 128, 12, 5], F32, kind="ExternalInput").ap()
        self.dn_par_in = dt("dn_par", [L, 128, 2, 2, 4], F32, kind="ExternalInput").ap()
        self.dn_gain_in = dt("dn_gain", [L, 128, 512], F32, kind="ExternalInput").ap()
        self.dnc_d = dt("dnc_d", [T, 1536], BF16).ap()
        self.o1dn_d = dt("o1dn_d", [T, 512], F32).ap()
        self.dn_state_d = dt("dn_state_d", [128, 4, 128], F32).ap()
        self.h_dn_x = dt("h_dn_x", [1536, 2], F32).ap()
        self.HS = min(1024, T)
        self.XW = 8 * self.HS + 2048
        self.XWs = [4 * self.HS, 4 * self.HS, 2048]
        self.pubA = [dt(f"pubA{i}", [128, w], BF16).ap() for i, w in enumerate(self.XWs)]
        self.gathA = [dt(f"gathA{i}", [256, w], BF16).ap() for i, w in enumerate(self.XWs)]
        self.pubB = dt("pubB", [128, 24], F32).ap()
        self.gathB = dt("gathB", [256, 24], F32).ap()
        self.pubS = dt("pubS", [128, 512], F32).ap()
        self.gathS = dt("gathS", [256, 512], F32).ap()
        self.RG = [[2 * i, 2 * i + 1] for i in range(n_cores // 2)]
        self.h_swa_kT = dt("h_swa_kT", [4, 128, self.HS], BF16).ap()
        self.h_swa_V = dt("h_swa_V", [self.HS, 512], BF16).ap()
        self.h_na_kT = dt("h_na_kT", [4, 128, 256], BF16).ap()
        self.h_na_V = dt("h_na_V", [256, 512], BF16).ap()
        if 'UT' in self.debug:
            self.dbg_UT = dt("dbg_UT", [INW, T], F32, kind="ExternalOutput").ap()
            self.dbg_Utok = dt("dbg_Utok", [T, INW], F32, kind="ExternalOutput").ap()

    def dbuf(self, key):
        if key not in self.dbufs:
            self.dbufs[key] = Buf(str(key))
        return self.dbufs[key]

    def setup(self):
        P = self.P
        nc = self.nc
        self.psb = [P.psum(f"ps{i}", [128, 512], F32) for i in range(8)]
        self.psbuf = [Buf(f"ps{i}") for i in range(8)]
        self.ps_main = Pool([(self.psb[i], self.psbuf[i]) for i in range(0, 6)])
        self.ps_aux = Pool([(self.psb[i], self.psbuf[i]) for i in range(6, 8)])
        self.cf = P.sbuf("cf", [128, NCONST, 128], F32, perm=True)
        self.cb = P.sbuf("cb", [128, NCONST, 128], BF16, perm=True)
        self.b_cf = Buf("cf")
        self.b_cb = Buf("cb")
        P.dma(self.cf[:], self.consts_in, writes=[self.b_cf])
        P.add('dve', lambda e: e.tensor_copy(self.cb[:], self.cf[:]), reads=[self.b_cf], writes=[self.b_cb])
        self.ident_f = self.cf[:, 0, :]
        self.ident_b = self.cb[:, 0, :]
        self.ones_b = self.cb[:, 1, :]
        self.ones_f = self.cf[:, 1, :]
        L = self.depth
        self.modT = P.sbuf("modT", [128, L, 96], F32, perm=True)
        self.amod = P.sbuf("amod", [128, L, 2, NCH], F32, perm=True)
        self.b_mod = Buf("mod")

    def compute_mod(self):
        P = self.P
        L = self.depth
        P.begin_phase()
        cv = P.sbuf("cv", [128, NCH], F32)
        sc = P.sbuf("sc", [128, NCH], F32)
        adb = P.sbuf("adb", [128, L, 96], F32)
        ngt = P.sbuf("ngt", [128, L, 2, NCH], F32)
        b_cv, b_sc, b_adb, b_ng = Buf(), Buf(), Buf(), Buf()
        P.dma(cv[:], self.cvec, writes=[b_cv])
        P.dma(adb[:], self.ada_b.rearrange("l p j -> p l j"), writes=[b_adb])
        P.dma(ngt[:], self.ng.rearrange("l a p c -> p l a c"), writes=[b_ng])
        P.add('act', lambda e: e.activation(out=sc[:], in_=cv[:], func=AF.Silu), reads=[b_cv], writes=[b_sc])
        wts = [(P.sbuf(f"adw{i}", [128, NCH, 512], F32), Buf()) for i in range(2)]
        wp = Pool(wts)
        for l in range(L):
            ps, pb = self.ps_main.next()
            for blk in range(24):
                wt, wb = wp.next()
                src = self.ada_w[l, :, blk * 512:(blk + 1) * 512].rearrange("(c p) n -> p c n", p=128)
                P.dma(wt[:], src, writes=[wb])
                for j in range(4):
                    col = blk * 4 + j
                    for c in range(NCH):
                        P.add('pe', lambda e, wt=wt, c=c, j=j, col=col, ps=ps: e.matmul(
                            ps[:, col:col + 1], wt[:, c, j * 128:(j + 1) * 128], sc[:, c:c + 1],
                            start=(c == 0), stop=(c == NCH - 1)),
                            reads=[wb, b_sc], writes=[pb])
            P.add('dve', lambda e, l=l, ps=ps: e.tensor_tensor(out=self.modT[:, l, :], in0=ps[:, 0:96], in1=adb[:, l, :], op=ALU.add),
                  reads=[pb, b_adb], writes=[self.b_mod])
            for k, so in ((0, 16), (1, 64)):
                P.add('dve', lambda e, l=l, k=k, so=so: e.scalar_tensor_tensor(
                    out=self.amod[:, l, k, :], in0=self.modT[:, l, so:so + 16], scalar=1.0, in1=ngt[:, l, k, :],
                    op0=ALU.add, op1=ALU.mult), reads=[self.b_mod, b_ng], writes=[self.b_mod])

    def mod(self, l, which, c):
        if which == 'a1':
            return self.amod[:, l, 0, c:c + 1]
        if which == 'a2':
            return self.amod[:, l, 1, c:c + 1]
        off = {'b1': 0, 'g1': 32, 'b2': 48, 'g2': 80}[which]
        return self.modT[:, l, off + c:off + c + 1]

    def convert_weights(self):
        P = self.P
        P.begin_phase()
        L = self.depth
        st = [(P.sbuf(f"wcf{i}", [128, 8192], F32), Buf()) for i in range(2)]
        sb = [(P.sbuf(f"wcb{i}", [128, 8192], BF16), Buf()) for i in range(2)]
        fp, bp = Pool(st), Pool(sb)
        k = 0
        engs = ['dve', 'act', 'pool']
        for l in range(L):
            jobs = []
            for c in range(NCH):
                jobs.append((self.w_in[l, c * 128:(c + 1) * 128, :], self.w_in_b[l, c * 128:(c + 1) * 128, :], INW, None))
            for c in range(NCH):
                jobs.append((self.w_out[l, c * 128:(c + 1) * 128, :], self.w_out_b[l, c * 128:(c + 1) * 128, :], D, None))
            for c in range(NCH):
                jobs.append((self.w1[l, c * 128:(c + 1) * 128, :], self.w1_b[l, c * 128:(c + 1) * 128, :], DFF, None))
            for f in range(0, 64, 4):
                jobs.append((self.w2[l, f * 128:(f + 4) * 128, :], None, 4 * D, f))
            for (src, dst, n, f) in jobs:
                ft, fb = fp.next()
                bt, bb = bp.next()
                if f is None:
                    P.dma(ft[:, 0:n], src, writes=[fb])
                else:
                    P.dma(ft[:, 0:n].rearrange("p (a d) -> p a d", a=4), src.rearrange("(a p) d -> p a d", p=128), writes=[fb])
                eng = engs[k % 2]
                k += 1
                if eng == 'act':
                    P.add('act', lambda e, ft=ft, bt=bt, n=n: e.copy(bt[:, 0:n], ft[:, 0:n]), reads=[fb], writes=[bb])
                else:
                    P.add(eng, lambda e, ft=ft, bt=bt, n=n: e.tensor_copy(bt[:, 0:n], ft[:, 0:n]), reads=[fb], writes=[bb])
                if f is None:
                    P.dma(dst, bt[:, 0:n], reads=[bb], writes=[self.dbuf(('w', l))])
                else:
                    for a in range(4):
                        dstv = self.w2_b[l, :, :, f + a, :].rearrange("j p d -> p j d")
                        P.dma(dstv, bt[:, a * D:(a + 1) * D].rearrange("p (j d) -> p j d", j=NCH), reads=[bb], writes=[self.dbuf(('w', l))])

    def norm_mod(self, l, which, xg, b_xg, hT, b_hT, G, sq, b_sq, rstd, b_rstd, tmp, b_tmp):
        P = self.P
        a_key, b_key = ('a1', 'b1') if which == 1 else ('a2', 'b2')
        for c in range(NCH):
            eng = 'act' if c % 2 == 0 else 'pool'
            if eng == 'act':
                P.add('act', lambda e, c=c: e.activation(out=sq[:, c, :], in_=xg[:, c, :], func=AF.Square),
                      reads=[b_xg], writes=[b_sq[c]])
            else:
                P.add('pool', lambda e, c=c: e.tensor_tensor(out=sq[:, c, :], in0=xg[:, c, :], in1=xg[:, c, :], op=ALU.mult),
                      reads=[b_xg], writes=[b_sq[c]])
        for h in range(G // 512):
            ps, pb = self.ps_aux.next()
            for c in range(NCH):
                P.add('pe', lambda e, c=c, h=h, ps=ps: e.matmul(ps[:, :], self.ones_b, sq[:, c, h * 512:(h + 1) * 512],
                                                               start=(c == 0), stop=(c == NCH - 1)),
                      reads=[b_sq[c], self.b_cb], writes=[pb])
            P.add('dve', lambda e, h=h, ps=ps: e.tensor_scalar(out=rstd[:, h * 512:(h + 1) * 512], in0=ps[:, :], scalar1=1.0 / D, scalar2=EPS,
                                                              op0=ALU.mult, op1=ALU.add), reads=[pb], writes=[b_rstd])
        P.add('act', lambda e: e.activation(out=rstd[:, :], in_=rstd[:, :], func=AF.Sqrt), reads=[b_rstd], writes=[b_rstd])
        P.add('dve', lambda e: e.reciprocal(rstd[:, :], rstd[:, :]), reads=[b_rstd], writes=[b_rstd])
        for c in range(NCH):
            t, tb = tmp[c % len(tmp)], b_tmp[c % len(tmp)]
            P.add('dve', lambda e, c=c, t=t: e.tensor_tensor(out=t[:, :], in0=xg[:, c, :], in1=rstd[:, :], op=ALU.mult),
                  reads=[b_xg, b_rstd], writes=[tb])
            P.add('act', lambda e, c=c, t=t: e.activation(out=hT[:, c, :], in_=t[:, :], func=AF.Identity,
                                                          scale=self.mod(l, a_key, c), bias=self.mod(l, b_key, c)),
                  reads=[tb, self.b_mod], writes=[b_hT[c]])

    def tok_blocks(self):
        blocks = [(C_SWA_Q, 512), (C_SWA_K, 512), (C_SWA_V, 512), (C_HG_F1, 512), (C_HG_F2, 512), (C_HG_I, 512),
                  (C_HG_G, 512), (C_NA_Q, 512), (C_NA_K, 512), (C_NA_V, 512), (C_DN_Z, 512), (C_DN_AB, 16)]
        return blocks

    def fm_blocks(self):
        return [(C_HG_Q, 512), (C_HG_F1, 512), (C_HG_F2, 512), (C_DN_QKV, 512), (C_DN_QKV + 512, 512), (C_DN_QKV + 1024, 512)]

    def phase_A(self, l, x_src):
        P = self.P
        T = self.T
        G = 1024 if T % 1024 == 0 else 512
        P.begin_phase()
        xg = P.sbuf("A_xg", [128, NCH, G], F32)
        b_xg = Buf()
        hT = P.sbuf("A_hT", [128, NCH, G], BF16)
        b_hT = [Buf() for _ in range(NCH)]
        sq = P.sbuf("A_sq", [128, NCH, G], BF16)
        b_sq = [Buf() for _ in range(NCH)]
        rstd = P.sbuf("A_rstd", [128, G], F32)
        b_rstd = Buf()
        tmp = [P.sbuf(f"A_tmp{i}", [128, G], F32) for i in range(2)]
        b_tmp = [Buf() for _ in range(2)]
        wts = Pool([(P.sbuf(f"A_w{i}", [128, NCH, 512], BF16), Buf()) for i in range(2)])
        stg = Pool([(P.sbuf(f"A_st{i}", [128, 512], F32), Buf()) for i in range(4)])
        xv = x_src.rearrange("(c p) t -> p c t", p=128)
        evk = 0
        for g in range(T // G):
            t0 = g * G
            P.dma(xg[:], xv[:, :, t0:t0 + G], reads=[self.dbuf(('x', g * G // 512)), self.dbuf(('x', (g * G + G - 1) // 512))], writes=[b_xg])
            self.norm_mod(l, 1, xg, b_xg, hT, b_hT, G, sq, b_sq, rstd, b_rstd, tmp, b_tmp)
            for (c0, ncol) in self.fm_blocks():
                wt, wb = wts.next()
                P.dma(wt[:, :, 0:ncol], self.w_in_b[l, :, c0:c0 + ncol].rearrange("(c p) n -> p c n", p=128),
                      reads=[self.dbuf(('w', l))], writes=[wb])
                for j in range(ncol // 128):
                    for h in range(G // 512):
                        ps, pb = self.ps_main.next()
                        for c in range(NCH):
                            P.add('pe', lambda e, wt=wt, c=c, j=j, h=h, ps=ps: e.matmul(
                                ps[:, :], wt[:, c, j * 128:(j + 1) * 128], hT[:, c, h * 512:(h + 1) * 512],
                                start=(c == 0), stop=(c == NCH - 1)), reads=[wb, b_hT[c]], writes=[pb])
                        st, sb_ = stg.next()
                        evk += 1
                        if evk % 2:
                            P.add('act', lambda e, st=st, ps=ps: e.copy(st[:, :], ps[:, :]), reads=[pb], writes=[sb_])
                        else:
                            P.add('dve', lambda e, st=st, ps=ps: e.tensor_copy(st[:, :], ps[:, :]), reads=[pb], writes=[sb_])
                        r0 = c0 + j * 128
                        P.dma(self.UT[r0:r0 + 128, t0 + h * 512:t0 + (h + 1) * 512], st[:, :], reads=[sb_],
                              writes=[self.dbuf(('UT', l))])
            for (c0, ncol) in self.tok_blocks():
                wt, wb = wts.next()
                P.dma(wt[:, :, 0:ncol], self.w_in_b[l, :, c0:c0 + ncol].rearrange("(c p) n -> p c n", p=128),
                      reads=[self.dbuf(('w', l))], writes=[wb])
                for tt in range(G // 128):
                    ps, pb = self.ps_main.next()
                    for c in range(NCH):
                        P.add('pe', lambda e, wt=wt, c=c, tt=tt, ps=ps, ncol=ncol: e.matmul(
                            ps[:, 0:ncol], hT[:, c, tt * 128:(tt + 1) * 128], wt[:, c, 0:ncol],
                            start=(c == 0), stop=(c == NCH - 1)), reads=[wb, b_hT[c]], writes=[pb])
                    st, sb_ = stg.next()
                    evk += 1
                    if evk % 2:
                        P.add('act', lambda e, st=st, ps=ps, ncol=ncol: e.copy(st[:, 0:ncol], ps[:, 0:ncol]), reads=[pb], writes=[sb_])
                    else:
                        P.add('dve', lambda e, st=st, ps=ps, ncol=ncol: e.tensor_copy(st[:, 0:ncol], ps[:, 0:ncol]), reads=[pb], writes=[sb_])
                    P.dma(self.Utok[t0 + tt * 128:t0 + (tt + 1) * 128, c0:c0 + ncol], st[:, 0:ncol], reads=[sb_],
                          writes=[self.dbuf(('Utok', l))])

    def phase_C(self, l, x_src, x_dst):
        P = self.P
        T = self.T
        G = 512
        P.begin_phase()
        xg = P.sbuf("C_xg", [128, NCH, G], F32)
        b_xg = Buf()
        hT = P.sbuf("C_hT", [128, NCH, G], BF16)
        b_hT = [Buf() for _ in range(NCH)]
        yg = hT
        rstd = P.sbuf("C_rstd", [128, G], F32)
        b_rstd = Buf()
        tmp = [P.sbuf(f"C_tmp{i}", [128, G], F32) for i in range(2)]
        b_tmp = [Buf() for _ in range(2)]
        hid = P.sbuf("C_hid", [128, 64, G], BF16)
        b_hid = [Buf() for _ in range(64)]
        sq = hid
        b_sq = b_hid[0:NCH]
        rl = Pool([(P.sbuf(f"C_rl{i}", [128, G], BF16), Buf()) for i in range(3)])
        wts = Pool([(P.sbuf(f"C_w{i}", [128, NCH, 512], BF16), Buf()) for i in range(2)])
        w2s = Pool([(P.sbuf(f"C_w2{i}", [128, 64, 128], BF16), Buf()) for i in range(2)])
        xv = x_src.rearrange("(c p) t -> p c t", p=128)
        xo = x_dst.rearrange("(c p) t -> p c t", p=128)
        yv = self.yT.rearrange("(c p) t -> p c t", p=128)
        for g in range(T // G):
            t0 = g * G
            P.dma(xg[:], xv[:, :, t0:t0 + G], reads=[self.dbuf(('x', g))], writes=[b_xg])
            P.dma(yg[:], yv[:, :, t0:t0 + G], reads=[self.dbuf(('yT', l))], writes=b_hT)
            for blk in range(4):
                wt, wb = wts.next()
                P.dma(wt[:], self.w_out_b[l, :, blk * 512:(blk + 1) * 512].rearrange("(c p) n -> p c n", p=128),
                      reads=[self.dbuf(('w', l))], writes=[wb])
                for j4 in range(4):
                    j = blk * 4 + j4
                    ps, pb = self.ps_main.next()
                    for c in range(NCH):
                        P.add('pe', lambda e, wt=wt, c=c, j4=j4, ps=ps: e.matmul(
                            ps[:, :], wt[:, c, j4 * 128:(j4 + 1) * 128], yg[:, c, :], start=(c == 0), stop=(c == NCH - 1)),
                            reads=[wb, b_hT[c]], writes=[pb])
                    P.add('dve', lambda e, j=j, ps=ps: e.scalar_tensor_tensor(
                        out=xg[:, j, :], in0=ps[:, :], scalar=self.mod(l, 'g1', j), in1=xg[:, j, :], op0=ALU.mult, op1=ALU.add),
                        reads=[pb, self.b_mod, b_xg], writes=[b_xg])
            self.norm_mod(l, 2, xg, b_xg, hT, b_hT, G, sq, b_sq, rstd, b_rstd, tmp, b_tmp)
            for blk in range(16):
                wt, wb = wts.next()
                P.dma(wt[:], self.w1_b[l, :, blk * 512:(blk + 1) * 512].rearrange("(c p) n -> p c n", p=128),
                      reads=[self.dbuf(('w', l))], writes=[wb])
                for j4 in range(4):
                    f = blk * 4 + j4
                    ps, pb = self.ps_main.next()
                    for c in range(NCH):
                        P.add('pe', lambda e, wt=wt, c=c, j4=j4, ps=ps: e.matmul(
                            ps[:, :], wt[:, c, j4 * 128:(j4 + 1) * 128], hT[:, c, :], start=(c == 0), stop=(c == NCH - 1)),
                            reads=[wb, b_hT[c]], writes=[pb])
                    r, rb = rl.next()
                    P.add('act', lambda e, r=r, ps=ps: e.activation(out=r[:, :], in_=ps[:, :], func=AF.Relu), reads=[pb], writes=[rb])
                    eng = 'pool' if f % 3 == 0 else 'dve'
                    P.add(eng, lambda e, r=r, f=f: e.tensor_tensor(out=hid[:, f, :], in0=r[:, :], in1=r[:, :], op=ALU.mult),
                          reads=[rb], writes=[b_hid[f]])
            for j in range(NCH):
                wt, wb = w2s.next()
                P.dma(wt[:], self.w2_b[l, j], reads=[self.dbuf(('w', l))], writes=[wb])
                ps, pb = self.ps_main.next()
                for f in range(64):
                    P.add('pe', lambda e, wt=wt, f=f, ps=ps: e.matmul(ps[:, :], wt[:, f, :], hid[:, f, :], start=(f == 0), stop=(f == 63)),
                          reads=[wb, b_hid[f]], writes=[pb])
                P.add('dve', lambda e, j=j, ps=ps: e.scalar_tensor_tensor(
                    out=xg[:, j, :], in0=ps[:, :], scalar=self.mod(l, 'g2', j), in1=xg[:, j, :], op0=ALU.mult, op1=ALU.add),
                    reads=[pb, self.b_mod, b_xg], writes=[b_xg])
            P.dma(xo[:, :, t0:t0 + G], xg[:], reads=[b_xg], writes=[self.dbuf(('x', g))])


    def zero_halos(self):
        P = self.P
        P.begin_phase()
        z = P.sbuf("zt", [128, 4096], BF16)
        bz = Buf()
        P.add('pool', lambda e: e.memset(z[:], 0.0), writes=[bz])
        HS = self.HS
        P.dma(self.h_swa_kT.rearrange("h p t -> p h t"), z[:, 0:4 * HS].rearrange("p (h t) -> p h t", h=4), reads=[bz], writes=[self.dbuf('halo')])
        P.dma(self.h_swa_V.rearrange("(a p) n -> p a n", p=128), z[:, 0:(HS // 128) * 512].rearrange("p (a n) -> p a n", n=512), reads=[bz], writes=[self.dbuf('halo')])
        P.dma(self.h_na_kT.rearrange("h p t -> p h t"), z[:, 0:1024].rearrange("p (h t) -> p h t", h=4), reads=[bz], writes=[self.dbuf('halo')])
        P.dma(self.h_na_V.rearrange("(a p) n -> p a n", p=128), z[:, 0:1024].rearrange("p (a n) -> p a n", n=512), reads=[bz], writes=[self.dbuf('halo')])
        zf = P.sbuf("ztf", [128, 12, 2], F32)
        bzf = Buf()
        P.add('pool', lambda e: e.memset(zf[:], 0.0), writes=[bzf])
        P.dma(self.h_dn_x.rearrange("(c p) t -> p c t", p=128), zf[:], reads=[bzf], writes=[self.dbuf('halo')])

    def phase_B1(self, l):
        P = self.P
        T, NT = self.T, self.NT
        P.begin_phase()
        gains = P.sbuf("B_gains", [128, 4, 512], F32)
        b_g = Buf()
        rope = P.sbuf("B_rope", [128, NT, 2, 16], F32)
        b_rope = Buf()
        P.dma(gains[:], self.gains_in[l].rearrange("a p n -> p a n"), writes=[b_g])
        P.dma(rope[:], self.rope_in, writes=[b_rope])
        for a in (0, 2):
            P.add('dve', lambda e, a=a: e.tensor_scalar(out=gains[:, a, :], in0=gains[:, a, :], scalar1=128.0 ** -0.5, scalar2=None, op0=ALU.mult),
                  reads=[b_g], writes=[b_g])
        upool = Pool([(P.sbuf(f"B_u{i}", [128, 3072], F32), Buf()) for i in range(2)])
        sq = P.sbuf("B_sq", [128, 4, 512], F32)
        b_sq = [Buf() for _ in range(4)]
        ss = P.sbuf("B_ss", [128, 16], F32)
        b_ss = Buf()
        tq = [P.sbuf(f"B_t{i}", [128, 512], F32) for i in range(2)]
        b_tq = [Buf() for _ in range(2)]
        t2 = [P.sbuf(f"B_t2{i}", [128, 512], F32) for i in range(2)]
        b_t2 = [Buf() for _ in range(2)]
        rt = P.sbuf("B_rt", [128, 4, 4, 16], F32)
        b_rt = [Buf() for _ in range(4)]
        obp = Pool([(P.sbuf(f"B_ob{i}", [128, 512], BF16), Buf()) for i in range(3)])
        stp = Pool([(P.sbuf(f"B_st{i}", [128, 4, 128], BF16), Buf()) for i in range(3)])
        vbp = Pool([(P.sbuf(f"B_vb{i}", [128, 512], BF16), Buf()) for i in range(3)])
        pieces = [(0, 0, True, self.swa_qT), (1, 512, True, self.swa_kT), (2, 1536, False, self.na_qT), (3, 2048, False, self.na_kT)]
        for tt in range(NT):
            r0 = tt * 128
            u, ub = upool.next()
            P.dma(u[:, 0:1536], self.Utok[r0:r0 + 128, 0:1536], reads=[self.dbuf(('Utok', l))], writes=[ub])
            P.dma(u[:, 1536:3072], self.Utok[r0:r0 + 128, 4096:5632], reads=[self.dbuf(('Utok', l))], writes=[ub])
            for (a, off, _, _) in pieces:
                eng = 'pool' if a % 2 else 'dve'
                P.add(eng, lambda e, a=a, off=off, u=u: e.tensor_tensor(out=sq[:, a, :], in0=u[:, off:off + 512], in1=u[:, off:off + 512], op=ALU.mult),
                      reads=[ub], writes=[b_sq[a]])
                P.add('dve', lambda e, a=a: e.tensor_reduce(out=ss[:, 4 * a:4 * a + 4], in_=sq[:, a, :].rearrange("p (h d) -> p h d", h=4),
                                                            axis=AX.X, op=ALU.add), reads=[b_sq[a]], writes=[b_ss])
            P.add('dve', lambda e: e.tensor_scalar(out=ss[:, :], in0=ss[:, :], scalar1=1.0 / 128, scalar2=EPS, op0=ALU.mult, op1=ALU.add),
                  reads=[b_ss], writes=[b_ss])
            P.add('act', lambda e: e.activation(out=ss[:, :], in_=ss[:, :], func=AF.Sqrt), reads=[b_ss], writes=[b_ss])
            P.add('dve', lambda e: e.reciprocal(ss[:, :], ss[:, :]), reads=[b_ss], writes=[b_ss])
            for (a, off, is_swa, dst) in pieces:
                k = a % 2
                ob, obb = obp.next()
                P.add('dve', lambda e, a=a, off=off, u=u, k=k: e.tensor_tensor(
                    out=tq[k][:, :].rearrange("p (h d) -> p h d", h=4), in0=u[:, off:off + 512].rearrange("p (h d) -> p h d", h=4),
                    in1=ss[:, 4 * a:4 * a + 4].unsqueeze(2).to_broadcast([128, 4, 128]), op=ALU.mult),
                    reads=[ub, b_ss], writes=[b_tq[k]])
                if not is_swa:
                    P.add('pool', lambda e, a=a, k=k, ob=ob: e.tensor_tensor(out=ob[:, :], in0=tq[k][:, :], in1=gains[:, a, :], op=ALU.mult),
                          reads=[b_tq[k], b_g], writes=[obb])
                else:
                    P.add('pool', lambda e, a=a, k=k: e.tensor_tensor(out=t2[k][:, :], in0=tq[k][:, :], in1=gains[:, a, :], op=ALU.mult),
                          reads=[b_tq[k], b_g], writes=[b_t2[k]])
                    tv = t2[k][:, :].rearrange("p (h d) -> p h d", h=4)
                    obv = ob[:, :].rearrange("p (h d) -> p h d", h=4)
                    cosb = rope[:, tt, 0, :].unsqueeze(1).to_broadcast([128, 4, 16])
                    sinb = rope[:, tt, 1, :].unsqueeze(1).to_broadcast([128, 4, 16])
                    x1, x2 = tv[:, :, 0:16], tv[:, :, 16:32]
                    P.add('dve', lambda e, x1=x1, cosb=cosb: e.tensor_tensor(out=rt[:, 0], in0=x1, in1=cosb, op=ALU.mult), reads=[b_t2[k], b_rope], writes=[b_rt[0]])
                    P.add('pool', lambda e, x2=x2, sinb=sinb: e.tensor_tensor(out=rt[:, 1], in0=x2, in1=sinb, op=ALU.mult), reads=[b_t2[k], b_rope], writes=[b_rt[1]])
                    P.add('dve', lambda e, x2=x2, cosb=cosb: e.tensor_tensor(out=rt[:, 2], in0=x2, in1=cosb, op=ALU.mult), reads=[b_t2[k], b_rope], writes=[b_rt[2]])
                    P.add('pool', lambda e, x1=x1, sinb=sinb: e.tensor_tensor(out=rt[:, 3], in0=x1, in1=sinb, op=ALU.mult), reads=[b_t2[k], b_rope], writes=[b_rt[3]])
                    P.add('act', lambda e, tv=tv, obv=obv: e.copy(obv[:, :, 32:128], tv[:, :, 32:128]), reads=[b_t2[k]], writes=[obb])
                    P.add('dve', lambda e, obv=obv: e.tensor_tensor(out=obv[:, :, 0:16], in0=rt[:, 0], in1=rt[:, 1], op=ALU.subtract),
                          reads=[b_rt[0], b_rt[1]], writes=[obb])
                    P.add('dve', lambda e, obv=obv: e.tensor_tensor(out=obv[:, :, 16:32], in0=rt[:, 2], in1=rt[:, 3], op=ALU.add),
                          reads=[b_rt[2], b_rt[3]], writes=[obb])
                ps, pb = self.ps_main.next()
                psv = ps[:, 0:256].bitcast(BF16)
                for h in range(4):
                    P.add('pe', lambda e, h=h, ob=ob, psv=psv: e.transpose(psv[:, h * 128:(h + 1) * 128], ob[:, h * 128:(h + 1) * 128], self.ident_b),
                          reads=[obb, self.b_cb], writes=[pb])
                st, stb = stp.next()
                P.add('act', lambda e, st=st, psv=psv: e.copy(st[:, :, :].rearrange("p h t -> p (h t)"), psv[:, :]), reads=[pb], writes=[stb])
                P.dma(dst[:, :, r0:r0 + 128].rearrange("h p t -> p h t"), st[:, :, :], reads=[stb], writes=[self.dbuf(('qk', l))])
            for (off, dstv) in ((1024, self.swa_V), (2560, self.na_V)):
                vb, vbb = vbp.next()
                P.add('act', lambda e, vb=vb, u=u, off=off: e.copy(vb[:, :], u[:, off:off + 512]), reads=[ub], writes=[vbb])
                P.dma(dstv[r0:r0 + 128, :], vb[:, :], reads=[vbb], writes=[self.dbuf(('qk', l))])

    def phase_SWA(self, l):
        P = self.P
        T = self.T
        HS = self.HS
        P.begin_phase()
        PADL = HS
        mk = P.sbuf("S_mk", [128, 4, 128], BF16)
        mkf = P.sbuf("S_mkf", [128, 4, 128], F32)
        b_mk = Buf()
        P.dma(mkf[:], self.swa_mask_in, writes=[b_mk])
        P.add('dve', lambda e: e.tensor_copy(mk[:], mkf[:]), reads=[b_mk], writes=[b_mk])
        qT = P.sbuf("S_qT", [128, T], BF16)
        kT = P.sbuf("S_kT", [128, PADL + T], BF16)
        kh = P.sbuf("S_kh", [128, HS], BF16)
        b_q, b_k, b_kh = Buf(), Buf(), Buf()
        accn = P.sbuf("S_accn", [128, T], F32)
        accd = P.sbuf("S_accd", [128, T], F32)
        b_acc = Buf()
        kbl = Pool([(P.sbuf(f"S_kbl{i}", [128, 128], BF16), Buf()) for i in range(2)])
        vt = Pool([(P.sbuf(f"S_vt{i}", [128, 128], BF16), Buf()) for i in range(4)])
        pts = Pool([(P.sbuf(f"S_pt{i}", [128, 2, 128], BF16), Buf()) for i in range(3)])
        yst = Pool([(P.sbuf(f"S_y{i}", [128, 2048 if T >= 2048 else T], BF16), Buf()) for i in range(2)])
        P.add('pool', lambda e: e.memset(kT[:, 0:PADL], 0.0), writes=[b_k])
        psS = Pool([(self.psb[i], self.psbuf[i]) for i in (0, 1, 2)])
        psO = Pool([(self.psb[i], self.psbuf[i]) for i in (3, 4, 5)])
        for h in range(4):
            P.dma(qT[:], self.swa_qT[h], reads=[self.dbuf(('qk', l))], writes=[b_q])
            P.dma(kT[:, PADL:PADL + T], self.swa_kT[h], reads=[self.dbuf(('qk', l))], writes=[b_k])
            P.dma(kh[:], self.h_swa_kT[h], reads=[self.dbuf('halo')], writes=[b_kh])
            first = True
            import os
            for dil in tuple(int(v) for v in os.environ.get('SWA_DILS', '1,4,16').split(',')):
                Tn = T // dil
                nq = Tn // 128
                for r in range(dil):
                    prevV = None
                    for i in range(nq):
                        qv = qT[:, sl(r + dil * 128 * i, 128, dil)]
                        lastq = (i == nq - 1)
                        a0 = PADL + r + dil * (128 * i - 64)
                        kA = kT[:, sl(a0, 128, dil)]
                        mA = mk[:, 0 if i == 0 else 1, :]
                        if not lastq:
                            b0 = PADL + r + dil * (128 * i + 64)
                            kB = kT[:, sl(b0, 128, dil)]
                            kB_reads = [b_k]
                            mB = mk[:, 2, :]
                        else:
                            kbt, kbb = kbl.next()
                            b0 = PADL + r + dil * (128 * i + 64)
                            P.add('pool', lambda e, kbt=kbt, b0=b0, dil=dil: e.tensor_copy(kbt[:, 0:64], kT[:, sl(b0, 64, dil)]), reads=[b_k], writes=[kbb])
                            rr = dil - 1 - r
                            h0 = rr + dil * (HS // dil - 64)
                            P.add('pool', lambda e, kbt=kbt, h0=h0, dil=dil: e.tensor_copy(kbt[:, 64:128], kh[:, sl(h0, 64, dil)]), reads=[b_kh], writes=[kbb])
                            kB = kbt[:, :]
                            kB_reads = [kbb]
                            mB = mk[:, 3, :]
                        if prevV is None:
                            vA, vAb = vt.next()
                            if i == 0:
                                P.add('pool', lambda e, vA=vA: e.memset(vA[0:64, :], 0.0), writes=[vAb])
                                src = self.swa_V[sl(r, 64, dil), h * 128:(h + 1) * 128]
                                P.dma(vA[64:128, :], src, reads=[self.dbuf(('qk', l))], writes=[vAb])
                            else:
                                raise AssertionError
                        else:
                            vA, vAb = prevV
                        vB, vBb = vt.next()
                        if not lastq:
                            t0 = r + dil * (128 * i + 64)
                            P.dma(vB[:, :], self.swa_V[sl(t0, 128, dil), h * 128:(h + 1) * 128], reads=[self.dbuf(('qk', l))], writes=[vBb])
                        else:
                            t0 = r + dil * (128 * i + 64)
                            P.dma(vB[0:64, :], self.swa_V[sl(t0, 64, dil), h * 128:(h + 1) * 128], reads=[self.dbuf(('qk', l))], writes=[vBb])
                            rr = dil - 1 - r
                            h0 = rr + dil * (HS // dil - 64)
                            P.dma(vB[64:128, :], self.h_swa_V[sl(h0, 64, dil), h * 128:(h + 1) * 128], reads=[self.dbuf('halo')], writes=[vBb])
                        prevV = (vB, vBb)
                        ps, pb = psS.next()
                        for (kk, kx, mx, rds) in ((0, kA, mA, [b_k]), (1, kB, mB, kB_reads)):
                            P.add('pe', lambda e, ps=ps, kk=kk, kx=kx, qv=qv: e.matmul(ps[:, kk * 128:(kk + 1) * 128], kx, qv, start=True, stop=False),
                                  reads=rds + [b_q], writes=[pb])
                            P.add('pe', lambda e, ps=ps, kk=kk, mx=mx: e.matmul(ps[:, kk * 128:(kk + 1) * 128], self.ident_b, mx, start=False, stop=True),
                                  reads=[b_mk, self.b_cb], writes=[pb])
                        pt, ptb = pts.next()
                        P.add('act', lambda e, pt=pt, ps=ps: e.activation(out=pt[:, :, :].rearrange("p a n -> p (a n)"), in_=ps[:, 0:256], func=AF.Exp),
                              reads=[pb], writes=[ptb])
                        po, pob = psO.next()
                        for (kk, vx, vxb) in ((0, vA, vAb), (1, vB, vBb)):
                            P.add('pe', lambda e, po=po, kk=kk, vx=vx, pt=pt: e.matmul(po[:, 0:128], vx[:, :], pt[:, kk, :], start=(kk == 0), stop=(kk == 1)),
                                  reads=[vxb, ptb], writes=[pob])
                        for kk in (0, 1):
                            P.add('pe', lambda e, po=po, kk=kk, pt=pt: e.matmul(po[:, 128:256], self.ones_b, pt[:, kk, :], start=(kk == 0), stop=(kk == 1)),
                                  reads=[self.b_cb, ptb], writes=[pob])
                        q0 = r + dil * 128 * i
                        an = accn[:, sl(q0, 128, dil)]
                        ad = accd[:, sl(q0, 128, dil)]
                        if first:
                            P.add('dve', lambda e, an=an, po=po: e.tensor_copy(an, po[:, 0:128]), reads=[pob], writes=[b_acc])
                            P.add('act', lambda e, ad=ad, po=po: e.copy(ad, po[:, 128:256]), reads=[pob], writes=[b_acc])
                        else:
                            P.add('dve', lambda e, an=an, po=po: e.tensor_tensor(out=an, in0=an, in1=po[:, 0:128], op=ALU.add), reads=[pob, b_acc], writes=[b_acc])
                            P.add('dve', lambda e, ad=ad, po=po: e.tensor_tensor(out=ad, in0=ad, in1=po[:, 128:256], op=ALU.add), reads=[pob, b_acc], writes=[b_acc])
                first = False
            CW = 2048 if T >= 2048 else T
            for c0 in range(0, T, CW):
                P.add('dve', lambda e, c0=c0, CW=CW: e.reciprocal(accd[:, c0:c0 + CW], accd[:, c0:c0 + CW]), reads=[b_acc], writes=[b_acc])
                y, yb = yst.next()
                P.add('pool', lambda e, c0=c0, CW=CW, y=y: e.tensor_tensor(out=y[:, 0:CW], in0=accn[:, c0:c0 + CW], in1=accd[:, c0:c0 + CW], op=ALU.mult),
                      reads=[b_acc], writes=[yb])
                P.dma(self.yT[h * 128:(h + 1) * 128, c0:c0 + CW], y[:, 0:CW], reads=[yb], writes=[self.dbuf(('yT', l))])

    NA_TYPES = {'F0': (0, [0, 1, 2, 3]), 'F1': (4, [-1, 0, 1, 2]), 'INT': (8, [-2, -1, 0, 1, 2]), 'L1': (13, [-2, -1, 0, 1, 2]),
                'L0': (18, [-3, -2, -1, 0, 1, 2])}

    def phase_NA(self, l):
        P = self.P
        T, NT = self.T, self.NT
        J = NT
        P.begin_phase()
        qT = P.sbuf("N_qT", [128, T], BF16)
        kT = P.sbuf("N_kT", [128, T + 256], BF16)
        V = P.sbuf("N_V", [128, J + 2, 128], BF16)
        bias_f = P.sbuf("N_bf", [128, 24, 128], F32)
        bias = P.sbuf("N_b", [128, 24, 128], BF16)
        yb_ = P.sbuf("N_y", [128, T], BF16)
        b_q, b_k, b_v, b_bf, b_b, b_y = Buf(), Buf(), Buf(), Buf(), Buf(), Buf()
        pts = Pool([(P.sbuf(f"N_pt{i}", [128, 6, 128], BF16), Buf()) for i in range(3)])
        rd = Pool([(P.sbuf(f"N_rd{i}", [128, 128], F32), Buf()) for i in range(3)])
        psS = Pool([((self.psb[i], self.psb[i + 1]), (self.psbuf[i], self.psbuf[i + 1])) for i in (0, 2)])
        psO = Pool([(self.psb[i], self.psbuf[i]) for i in (4, 5, 6)])
        for h in range(4):
            P.dma(qT[:], self.na_qT[h], reads=[self.dbuf(('qk', l))], writes=[b_q])
            P.dma(kT[:, 0:T], self.na_kT[h], reads=[self.dbuf(('qk', l))], writes=[b_k])
            P.dma(kT[:, T:T + 128], self.h_na_kT[h, :, 128:256], reads=[self.dbuf('halo')], writes=[b_k])
            P.dma(kT[:, T + 128:T + 256], self.h_na_kT[h, :, 0:128], reads=[self.dbuf('halo')], writes=[b_k])
            P.dma(V[:, 0:J, :], self.na_V[:, h * 128:(h + 1) * 128].rearrange("(j p) d -> p j d", p=128), reads=[self.dbuf(('qk', l))], writes=[b_v])
            P.dma(V[:, J, :], self.h_na_V[128:256, h * 128:(h + 1) * 128], reads=[self.dbuf('halo')], writes=[b_v])
            P.dma(V[:, J + 1, :], self.h_na_V[0:128, h * 128:(h + 1) * 128], reads=[self.dbuf('halo')], writes=[b_v])
            P.dma(bias_f[:], self.na_bias_in[l, h], writes=[b_bf])
            P.add('dve', lambda e: e.tensor_copy(bias[:], bias_f[:]), reads=[b_bf], writes=[b_b])
            for j in range(J):
                ty = 'F0' if j == 0 else 'F1' if j == 1 else 'L0' if j == J - 1 else 'L1' if j == J - 2 else 'INT'
                base, offs = self.NA_TYPES[ty]
                (psA, psB), (pbA, pbB) = psS.next()
                qv = qT[:, j * 128:(j + 1) * 128]
                for oi, o in enumerate(offs):
                    kt = j + o
                    ps, pb = (psA, pbA) if oi < 4 else (psB, pbB)
                    col = (oi % 4) * 128
                    P.add('pe', lambda e, ps=ps, col=col, kt=kt, qv=qv: e.matmul(ps[:, col:col + 128], kT[:, kt * 128:(kt + 1) * 128], qv, start=True, stop=False),
                          reads=[b_k, b_q], writes=[pb])
                    P.add('pe', lambda e, ps=ps, col=col, bi=base + oi: e.matmul(ps[:, col:col + 128], self.ident_b, bias[:, bi, :], start=False, stop=True),
                          reads=[b_b, self.b_cb], writes=[pb])
                pt, ptb = pts.next()
                n0 = min(4, len(offs))
                P.add('act', lambda e, pt=pt, psA=psA, n0=n0: e.activation(out=pt[:, 0:n0, :].rearrange("p a n -> p (a n)"), in_=psA[:, 0:n0 * 128], func=AF.Exp),
                      reads=[pbA], writes=[ptb])
                if len(offs) > 4:
                    n1 = len(offs) - 4
                    P.add('act', lambda e, pt=pt, psB=psB, n1=n1: e.activation(out=pt[:, 4:4 + n1, :].rearrange("p a n -> p (a n)"), in_=psB[:, 0:n1 * 128], func=AF.Exp),
                          reads=[pbB], writes=[ptb])
                po, pob = psO.next()
                no = len(offs)
                for oi, o in enumerate(offs):
                    kt = j + o
                    P.add('pe', lambda e, po=po, oi=oi, kt=kt, pt=pt, no=no: e.matmul(po[:, 0:128], V[:, kt, :], pt[:, oi, :], start=(oi == 0), stop=(oi == no - 1)),
                          reads=[b_v, ptb], writes=[pob])
                for oi, o in enumerate(offs):
                    P.add('pe', lambda e, po=po, oi=oi, pt=pt, no=no: e.matmul(po[:, 128:256], self.ones_b, pt[:, oi, :], start=(oi == 0), stop=(oi == no - 1)),
                          reads=[self.b_cb, ptb], writes=[pob])
                r, rb = rd.next()
                P.add('dve', lambda e, r=r, po=po: e.reciprocal(r[:, :], po[:, 128:256]), reads=[pob], writes=[rb])
                P.add('dve', lambda e, r=r, po=po, j=j: e.tensor_tensor(out=yb_[:, j * 128:(j + 1) * 128], in0=po[:, 0:128], in1=r[:, :], op=ALU.mult),
                      reads=[pob, rb], writes=[b_y])
            P.dma(self.yT[1024 + h * 128:1024 + (h + 1) * 128, :], yb_[:], reads=[b_y], writes=[self.dbuf(('yT', l))])

    def zero_y(self, l, rows):
        P = self.P
        P.begin_phase()
        z = P.sbuf("zy", [128, self.T], BF16)
        bz = Buf()
        P.add('pool', lambda e: e.memset(z[:], 0.0), writes=[bz])
        for r0 in rows:
            P.dma(self.yT[r0:r0 + 128, :], z[:], reads=[bz], writes=[self.dbuf(('yT', l))])


    def compute_lb(self):
        P = self.P
        L = self.depth
        P.begin_phase()
        for (src, dstkind, W) in ((self.hg_lb_in, 'tok', 512), (self.hg_lbT_in, 'fm', 4)):
            lg = P.sbuf(f"lb_lg{W}", [128, L, 2, W], F32)
            sm = P.sbuf(f"lb_sm{W}", [128, 2, W], F32)
            out = P.sbuf(f"lb_out{W}", [128, L, 2, 2, W], F32)
            b = Buf()
            P.dma(lg[:], src, writes=[b])
            P.add('act', lambda e, lg=lg: e.activation(out=lg[:], in_=lg[:], func=AF.Exp), reads=[b], writes=[b])
            P.add('dve', lambda e, lg=lg, sm=sm: e.tensor_copy(sm[:], lg[:, 0]), reads=[b], writes=[b])
            for l in range(1, L):
                P.add('dve', lambda e, lg=lg, sm=sm, l=l: e.tensor_tensor(out=sm[:], in0=sm[:], in1=lg[:, l], op=ALU.add), reads=[b], writes=[b])
            P.add('dve', lambda e, sm=sm: e.reciprocal(sm[:], sm[:]), reads=[b], writes=[b])
            P.add('pool', lambda e, out=out: e.memset(out[:, 0, :, 0, :], 0.0), writes=[b])
            for l in range(1, L):
                P.add('dve', lambda e, lg=lg, sm=sm, l=l: e.tensor_tensor(out=lg[:, l], in0=lg[:, l], in1=sm[:], op=ALU.mult), reads=[b], writes=[b])
                P.add('dve', lambda e, lg=lg, out=out, l=l: e.tensor_tensor(out=out[:, l, :, 0, :], in0=out[:, l - 1, :, 0, :], in1=lg[:, l], op=ALU.add),
                      reads=[b], writes=[b])
            for l in range(L):
                P.add('dve', lambda e, out=out, l=l: e.tensor_scalar(out=out[:, l, :, 1, :], in0=out[:, l, :, 0, :], scalar1=-1.0, scalar2=1.0,
                                                                     op0=ALU.mult, op1=ALU.add), reads=[b], writes=[b])
            if dstkind == 'tok':
                for l in range(L):
                    P.dma(self.lb_d[l].rearrange("d a p w -> p d a w"), out[:, l], reads=[b], writes=[self.dbuf('lb')])
            else:
                for l in range(L):
                    P.dma(self.lbT_d[l], out[:, l], reads=[b], writes=[self.dbuf('lb')])

    def phase_HG(self, l):
        P = self.P
        T, NT = self.T, self.NT
        P.begin_phase()
        lbt = P.sbuf("H_lbt", [128, 2, 2, 512], F32)
        lbf = P.sbuf("H_lbf", [128, 2, 2, 4], F32)
        gain = P.sbuf("H_gain", [128, 512], F32)
        b_lb = Buf()
        P.dma(lbt[:], self.lb_d[l].rearrange("d a p w -> p d a w"), reads=[self.dbuf('lb')], writes=[b_lb])
        P.dma(lbf[:], self.lbT_d[l], reads=[self.dbuf('lb')], writes=[b_lb])
        P.dma(gain[:], self.hg_gain_in[l], writes=[b_lb])
        S = P.sbuf("H_S", [128, 4, 128], F32)
        Sb = P.sbuf("H_Sb", [128, 4, 128], BF16)
        b_S = [Buf() for _ in range(4)]
        b_Sb = [Buf() for _ in range(4)]
        up = Pool([(P.sbuf(f"H_u{i}", [128, 3, 512], F32), Buf()) for i in range(2)])
        fp_ = Pool([(P.sbuf(f"H_f{i}", [128, 8, 128], F32), Buf()) for i in range(2)])
        sc1 = P.sbuf("H_sc1", [128, 512], F32)
        ff = P.sbuf("H_ff", [128, 512], F32)
        b_sc1, b_ff = Buf(), Buf()
        ktok = Pool([(P.sbuf(f"H_kt{i}", [128, 512], F32), Buf()) for i in range(2)])
        logf = Pool([(P.sbuf(f"H_lf{i}", [128, 512], F32), Buf()) for i in range(2)])
        vbp = Pool([(P.sbuf(f"H_vb{i}", [128, 512], BF16), Buf()) for i in range(2)])
        qkp = Pool([(P.sbuf(f"H_qk{i}", [128, 8, 128], F32), Buf()) for i in range(2)])
        exq = Pool([(P.sbuf(f"H_ex{i}", [128, 4, 128], F32), Buf()) for i in range(3)])
        Qb2 = [P.sbuf(f"H_Qb{i}", [128, 4, 128], BF16) for i in range(3)]
        b_Qb2 = [Buf() for _ in range(3)]
        QK = Pool([(P.sbuf(f"H_QK{i}", [128, 2, 128], BF16), Buf()) for i in range(3)])
        Kd2 = [P.sbuf(f"H_Kd{i}", [128, 4, 128], BF16) for i in range(3)]
        kdf = P.sbuf("H_kdf", [128, 128], F32)
        b_kdf = Buf()
        b_Kd2 = [Buf() for _ in range(3)]
        aTm = Pool([(P.sbuf(f"H_aT{i}", [128, 128], BF16), Buf()) for i in range(3)])
        ost = Pool([(P.sbuf(f"H_o{i}", [128, 512], F32), Buf()) for i in range(2)])
        o1t = Pool([(P.sbuf(f"H_o1{i}", [128, 512], F32), Buf()) for i in range(2)])
        sq = P.sbuf("H_sq", [128, 512], F32)
        ss = P.sbuf("H_ss", [128, 4], F32)
        b_sq, b_ss = Buf(), Buf()
        gg = P.sbuf("H_gg", [128, 512], F32)
        b_gg = Buf()
        yb = Pool([(P.sbuf(f"H_y{i}", [128, 512], BF16), Buf()) for i in range(2)])
        yst = Pool([(P.sbuf(f"H_ys{i}", [128, 4, 128], BF16), Buf()) for i in range(2)])
        for i in range(3):
            P.add('pool', lambda e, i=i: e.memset(Qb2[i][:], 0.0), writes=[b_Qb2[i]])
        psO = Pool([(self.psb[i], self.psbuf[i]) for i in (4,)])
        slotb = {5: [self.psbuf[5]] * 4, 6: [self.psbuf[6]] * 4}
        hres = []
        for par in range(2):
            row = []
            for h in range(4):
                r = dict(ex=(P.sbuf(f"H_rex{par}{h}", [128, 3, 128], F32), Buf()),
                         qb=(P.sbuf(f"H_rqb{par}{h}", [128, 4, 128], BF16), Buf()),
                         qk=(P.sbuf(f"H_rqk{par}{h}", [128, 2, 128], BF16), Buf()),
                         kd=(P.sbuf(f"H_rkd{par}{h}", [128, 4, 128], BF16), Buf()),
                         kdf=(P.sbuf(f"H_rkf{par}{h}", [128, 128], F32), Buf()),
                         at=(P.sbuf(f"H_rat{par}{h}", [128, 128], BF16), Buf()))
                P.add('pool', lambda e, t=r['qb'][0]: e.memset(t[:], 0.0), writes=[r['qb'][1]])
                row.append(r)
            hres.append(row)
        UTv = self.UT
        kq = 0
        tqi = 0
        for d in (0, 1):
            cbase = 2 if d == 0 else 6
            Mrem, M1, M2, msk = (self.cf[:, cbase + k, :] for k in range(4))
            fcol = C_HG_F1 if d == 0 else C_HG_F2
            if d == 0:
                P.add('pool', lambda e: e.memset(S[:], 0.0), writes=b_S)
                P.add('pool', lambda e: e.memset(Sb[:], 0.0), writes=b_Sb)
            else:
                self.exchange_state(l, 'hg', S, b_S, Sb, b_Sb)
            tiles = range(NT) if d == 0 else range(NT - 1, -1, -1)
            for tt in tiles:
                r0 = tt * 128
                u, ub = up.next()
                P.dma(u[:, 0, :], self.Utok[r0:r0 + 128, fcol:fcol + 512], reads=[self.dbuf(('Utok', l))], writes=[ub])
                P.dma(u[:, 1, :], self.Utok[r0:r0 + 128, C_HG_I:C_HG_I + 512], reads=[self.dbuf(('Utok', l))], writes=[ub])
                if d == 1:
                    P.dma(u[:, 2, :], self.Utok[r0:r0 + 128, C_HG_G:C_HG_G + 512], reads=[self.dbuf(('Utok', l))], writes=[ub])
                f, fb = fp_.next()
                P.dma(f[:, 0:4, :], UTv[C_HG_Q:C_HG_Q + 512, r0:r0 + 128].rearrange("(h p) t -> p h t", p=128), reads=[self.dbuf(('UT', l))], writes=[fb])
                P.dma(f[:, 4:8, :], UTv[fcol:fcol + 512, r0:r0 + 128].rearrange("(h p) t -> p h t", p=128), reads=[self.dbuf(('UT', l))], writes=[fb])
                kt, ktb = ktok.next()
                lf, lfb = logf.next()
                vb, vbb = vbp.next()
                P.add('act', lambda e, u=u: e.activation(out=sc1[:], in_=u[:, 0, :], func=AF.Sigmoid), reads=[ub], writes=[b_sc1])
                P.add('pool', lambda e, d=d: e.tensor_tensor(out=sc1[:], in0=sc1[:], in1=lbt[:, d, 1, :], op=ALU.mult), reads=[b_sc1, b_lb], writes=[b_sc1])
                P.add('dve', lambda e, d=d: e.tensor_tensor(out=ff[:], in0=sc1[:], in1=lbt[:, d, 0, :], op=ALU.add), reads=[b_sc1, b_lb], writes=[b_ff])
                P.add('pool', lambda e, d=d, kt=kt: e.tensor_tensor(out=kt[:], in0=lbt[:, d, 1, :], in1=sc1[:], op=ALU.subtract), reads=[b_sc1, b_lb], writes=[ktb])
                P.add('act', lambda e, lf=lf: e.activation(out=lf[:], in_=ff[:], func=AF.Ln), reads=[b_ff], writes=[lfb])
                P.add('act', lambda e, vb=vb, u=u: e.copy(vb[:], u[:, 1, :]), reads=[ub], writes=[vbb])
                qk, qkb = qkp.next()
                P.add('act', lambda e, qk=qk, f=f: e.activation(out=qk[:, 0:4, :], in_=f[:, 0:4, :], func=AF.Silu), reads=[fb], writes=[qkb])
                P.add('act', lambda e, qk=qk, f=f: e.activation(out=qk[:, 4:8, :], in_=f[:, 4:8, :], func=AF.Sigmoid, scale=-1.0), reads=[fb], writes=[qkb])
                for h in range(4):
                    P.add('pool', lambda e, qk=qk, h=h, d=d: e.tensor_scalar(out=qk[:, 4 + h, :], in0=qk[:, 4 + h, :], scalar1=lbf[:, d, 1, h:h + 1], scalar2=None,
                                                                              op0=ALU.mult), reads=[qkb, b_lb], writes=[qkb])
                par = tqi % 2
                tqi += 1
                HS4 = [slice(h * 128, (h + 1) * 128) for h in range(4)]
                R = [hres[par][h] for h in range(4)]
                for h in range(4):
                    pe_, peb = self.psb[h], self.psbuf[h]
                    P.add('pe', lambda e, pe_=pe_, lf=lf, hs=HS4[h], Mrem=Mrem: e.matmul(pe_[:, 0:128], Mrem, lf[:, hs], start=True, stop=True),
                          reads=[lfb, self.b_cf], writes=[peb])
                    P.add('pe', lambda e, pe_=pe_, lf=lf, hs=HS4[h], M1=M1: e.matmul(pe_[:, 128:256], lf[:, hs], M1, start=True, stop=True),
                          reads=[lfb, self.b_cf], writes=[peb])
                for h in range(4):
                    pe_, peb = self.psb[h], self.psbuf[h]
                    ex, exb = R[h]['ex']
                    P.add('act', lambda e, ex=ex, pe_=pe_: e.activation(out=ex[:, 0:2, :].rearrange("p a n -> p (a n)"), in_=pe_[:, 0:256], func=AF.Exp),
                          reads=[peb], writes=[exb])
                    P.add('act', lambda e, ex=ex, pe_=pe_: e.activation(out=ex[:, 2, :], in_=pe_[:, 128:256], func=AF.Exp, scale=-1.0),
                          reads=[peb], writes=[exb])
                for h in range(4):
                    ex, exb = R[h]['ex']
                    qb, qbb = R[h]['qb']
                    qbf = qb[:]
                    qb_out = bass.AP(qbf.tensor, qbf.offset, [list(qbf.ap[0]), [160, 4], [1, 32]])
                    qT_h = qk[:, h, :]
                    P.add('dve', lambda e, qb_out=qb_out, qT_h=qT_h, ex=ex: e.tensor_tensor(
                        out=qb_out, in0=qT_h.rearrange("p (c n) -> p c n", c=4), in1=ex[:, 1, :].rearrange("p (c n) -> p c n", c=4), op=ALU.mult),
                        reads=[qkb, exb], writes=[qbb])
                    QKt, QKb = R[h]['qk']
                    P.add('pool', lambda e, QKt=QKt, qT_h=qT_h, ex=ex: e.tensor_tensor(out=QKt[:, 0, :], in0=qT_h, in1=ex[:, 1, :], op=ALU.mult),
                          reads=[qkb, exb], writes=[QKb])
                    P.add('dve', lambda e, QKt=QKt, qk=qk, h=h, ex=ex: e.tensor_tensor(out=QKt[:, 1, :], in0=qk[:, 4 + h, :], in1=ex[:, 2, :], op=ALU.mult),
                          reads=[qkb, exb], writes=[QKb])
                    kd, kdb_ = R[h]['kd']
                    kf, kfb = R[h]['kdf']
                    P.add('pool', lambda e, kt=kt, hs=HS4[h], ex=ex, kf=kf: e.tensor_tensor(out=kf[:], in0=kt[:, hs], in1=ex[:, 0, :], op=ALU.mult),
                          reads=[ktb, exb], writes=[kfb])
                    P.add('dve', lambda e, kd=kd, kf=kf: e.tensor_tensor(out=kd[:], in0=kf[:].unsqueeze(1).to_broadcast([128, 4, 128]), in1=self.cf[:, 16:20, :], op=ALU.mult),
                          reads=[kfb, self.b_cf], writes=[kdb_])
                for h in range(4):
                    pe_, peb = self.psb[h], self.psbuf[h]
                    QKt, QKb = R[h]['qk']
                    P.add('pe', lambda e, pe_=pe_, QKt=QKt: e.matmul(pe_[:, 384:512], QKt[:, 1, :], QKt[:, 0, :], start=True, stop=True),
                          reads=[QKb], writes=[peb])
                for h in range(4):
                    pe_, peb = self.psb[h], self.psbuf[h]
                    at, atb = R[h]['at']
                    P.add('dve', lambda e, at=at, pe_=pe_, msk=msk: e.tensor_tensor(out=at[:], in0=pe_[:, 384:512], in1=msk, op=ALU.mult),
                          reads=[peb, self.b_cf], writes=[atb])
                corder = (0, 1, 2, 3) if d == 0 else (3, 2, 1, 0)
                for ci, c in enumerate(corder):
                    bank = 5 + (ci % 2)
                    pS = self.psb[bank]
                    for h in range(4):
                        hs = HS4[h]
                        qb, qbb = R[h]['qb']
                        kd, kdb_ = R[h]['kd']
                        at, atb = R[h]['at']
                        P.add('pe', lambda e, qb=qb, c=c, h=h, ci=ci: e.matmul(self.psb[h][:, 0:128], qb[:, c, :], Sb[:, h, :], start=(ci == 0), stop=False),
                              reads=[qbb, b_Sb[h]], writes=[self.psbuf[h]])
                        if ci == 3:
                            P.add('pe', lambda e, h=h, hs=hs, at=at, vb=vb: e.matmul(self.psb[h][:, 0:128], at[:], vb[:, hs], start=False, stop=True),
                                  reads=[atb, vbb], writes=[self.psbuf[h]])
                        P.add('pe', lambda e, pS=pS, kd=kd, c=c, vb=vb, hs=hs: e.matmul(pS[:, hs], kd[:, c, :], vb[:, hs], start=True, stop=True),
                              reads=[kdb_, vbb], writes=[slotb[bank][h]])
                    for h in range(4):
                        hs = HS4[h]
                        ex, exb = R[h]['ex']
                        deccol = (c * 32 + 31) if d == 0 else (c * 32)
                        P.add('dve', lambda e, pS=pS, h=h, hs=hs, ex=ex, deccol=deccol: e.scalar_tensor_tensor(
                            out=S[:, h, :], in0=S[:, h, :], scalar=ex[:, 1, deccol:deccol + 1], in1=pS[:, hs], op0=ALU.mult, op1=ALU.add),
                            reads=[slotb[bank][h], exb, b_S[h]], writes=[b_S[h]])
                    for h in range(4):
                        P.add('act', lambda e, h=h: e.copy(Sb[:, h, :], S[:, h, :]), reads=[b_S[h]], writes=[b_Sb[h]])
                if d == 0:
                    o, ob = ost.next()
                    for h in range(4):
                        P.add('act', lambda e, o=o, h=h: e.copy(o[:, h * 128:(h + 1) * 128], self.psb[h][:, 0:128]), reads=[self.psbuf[h]], writes=[ob])
                    P.dma(self.o1_d[r0:r0 + 128, :], o[:], reads=[ob], writes=[self.dbuf(('o1', l))])
                else:
                    o1, o1b = o1t.next()
                    P.dma(o1[:], self.o1_d[r0:r0 + 128, :], reads=[self.dbuf(('o1', l))], writes=[o1b])
                    o, ob = ost.next()
                    for h in range(4):
                        P.add('dve', lambda e, o=o, h=h, o1=o1: e.tensor_tensor(out=o[:, h * 128:(h + 1) * 128], in0=self.psb[h][:, 0:128], in1=o1[:, h * 128:(h + 1) * 128], op=ALU.add),
                              reads=[self.psbuf[h], o1b], writes=[ob])
                    self.norm_gate_emit(l, o, ob, u[:, 2, :], ub, gain, b_lb, sq, b_sq, ss, b_ss, gg, b_gg, yb, yst, 512, r0)

    def norm_gate_emit(self, l, o, ob, gt, gtb, gain, b_gain, sq, b_sq, ss, b_ss, gg, b_gg, yb, yst, yrow0, r0):
        P = self.P
        P.add('pool', lambda e: e.tensor_tensor(out=sq[:], in0=o[:], in1=o[:], op=ALU.mult), reads=[ob], writes=[b_sq])
        P.add('dve', lambda e: e.tensor_reduce(out=ss[:, 0:4], in_=sq[:].rearrange("p (h d) -> p h d", h=4), axis=AX.X, op=ALU.add),
              reads=[b_sq], writes=[b_ss])
        P.add('dve', lambda e: e.tensor_scalar(out=ss[:, 0:4], in0=ss[:, 0:4], scalar1=1.0 / 128, scalar2=EPS, op0=ALU.mult, op1=ALU.add),
              reads=[b_ss], writes=[b_ss])
        P.add('act', lambda e: e.activation(out=ss[:, 0:4], in_=ss[:, 0:4], func=AF.Sqrt), reads=[b_ss], writes=[b_ss])
        P.add('dve', lambda e: e.reciprocal(ss[:, 0:4], ss[:, 0:4]), reads=[b_ss], writes=[b_ss])
        P.add('act', lambda e: e.activation(out=gg[:], in_=gt, func=AF.Silu), reads=[gtb], writes=[b_gg])
        P.add('pool', lambda e: e.tensor_tensor(out=gg[:], in0=gg[:], in1=gain[:], op=ALU.mult), reads=[b_gg, b_gain], writes=[b_gg])
        P.add('dve', lambda e: e.tensor_tensor(out=o[:].rearrange("p (h d) -> p h d", h=4), in0=o[:].rearrange("p (h d) -> p h d", h=4),
                                               in1=ss[:, 0:4].unsqueeze(2).to_broadcast([128, 4, 128]), op=ALU.mult), reads=[ob, b_ss], writes=[ob])
        y, ybb = yb.next()
        P.add('dve', lambda e, y=y: e.tensor_tensor(out=y[:], in0=o[:], in1=gg[:], op=ALU.mult), reads=[ob, b_gg], writes=[ybb])
        ps, pb = self.psb[7], self.psbuf[7]
        psv = ps[:, 0:256].bitcast(BF16)
        for h in range(4):
            P.add('pe', lambda e, h=h, y=y, psv=psv: e.transpose(psv[:, h * 128:(h + 1) * 128], y[:, h * 128:(h + 1) * 128], self.ident_b),
                  reads=[ybb, self.b_cb], writes=[pb])
        st, stb = yst.next()
        P.add('act', lambda e, st=st, psv=psv: e.copy(st[:, :, :].rearrange("p h t -> p (h t)"), psv[:, :]), reads=[pb], writes=[stb])
        P.dma(self.yT[yrow0:yrow0 + 512, r0:r0 + 128].rearrange("(h p) t -> p h t", p=128), st[:, :, :], reads=[stb], writes=[self.dbuf(('yT', l))])


    def exchange_halos(self, l):
        P = self.P
        T, HS = self.T, self.HS
        P.begin_phase()
        bpub, bg = Buf(), Buf()
        src = [self.dbuf(('qk', l)), self.dbuf(('UT', l))]
        P.dma(self.pubA[0].rearrange("p (h t) -> p h t", h=4), self.swa_kT[:, :, T - HS:T].rearrange("h p t -> p h t"), reads=src, writes=[bpub])
        P.dma(self.pubA[1].rearrange("p (a n) -> p a n", n=512), self.swa_V[T - HS:T, :].rearrange("(a p) n -> p a n", p=128), reads=src, writes=[bpub])
        P.dma(self.pubA[2][:, 0:1024].rearrange("p (h t) -> p h t", h=4), self.na_kT[:, :, T - 256:T].rearrange("h p t -> p h t"), reads=src, writes=[bpub])
        P.dma(self.pubA[2][:, 1024:2048].rearrange("p (a n) -> p a n", n=512), self.na_V[T - 256:T, :].rearrange("(a p) n -> p a n", p=128), reads=src, writes=[bpub])
        P.dma(self.pubB.rearrange("p (c t) -> p c t", t=2), self.UT[C_DN_QKV:C_DN_QKV + 1536, T - 2:T].rearrange("(c p) t -> p c t", p=128), reads=src, writes=[bpub])
        for i in range(3):
            P.add('pool', lambda e, i=i: e.collective_compute("AllGather", ALU.bypass, replica_groups=self.RG, ins=[self.pubA[i]], outs=[self.gathA[i]]),
                  reads=[bpub], writes=[bg])
        P.add('pool', lambda e: e.collective_compute("AllGather", ALU.bypass, replica_groups=self.RG, ins=[self.pubB], outs=[self.gathB]),
              reads=[bpub], writes=[bg])
        fl = P.sbuf("X_fl", [128, 2], F32)
        b_fl = Buf()
        P.dma(fl[:], self.flag_in, writes=[b_fl])
        hb = self.dbuf('halo')
        WM = max(self.XWs)
        p0 = P.sbuf("X_p0", [128, WM], BF16)
        p1 = P.sbuf("X_p1", [128, WM], BF16)
        b_p = Buf()
        for i, w in enumerate(self.XWs):
            P.dma(p0[:, 0:w], self.gathA[i][0:128, :], reads=[bg], writes=[b_p])
            P.dma(p1[:, 0:w], self.gathA[i][128:256, :], reads=[bg], writes=[b_p])
            P.add('dve', lambda e, w=w: e.tensor_scalar(out=p0[:, 0:w], in0=p0[:, 0:w], scalar1=fl[:, 0:1], scalar2=None, op0=ALU.mult), reads=[b_p, b_fl], writes=[b_p])
            P.add('dve', lambda e, w=w: e.scalar_tensor_tensor(out=p0[:, 0:w], in0=p1[:, 0:w], scalar=fl[:, 1:2], in1=p0[:, 0:w], op0=ALU.mult, op1=ALU.add),
                  reads=[b_p, b_fl], writes=[b_p])
            if i == 0:
                P.dma(self.h_swa_kT.rearrange("h p t -> p h t"), p0[:, 0:w].rearrange("p (h t) -> p h t", h=4), reads=[b_p], writes=[hb])
            elif i == 1:
                P.dma(self.h_swa_V.rearrange("(a p) n -> p a n", p=128), p0[:, 0:w].rearrange("p (a n) -> p a n", n=512), reads=[b_p], writes=[hb])
            else:
                P.dma(self.h_na_kT.rearrange("h p t -> p h t"), p0[:, 0:1024].rearrange("p (h t) -> p h t", h=4), reads=[b_p], writes=[hb])
                P.dma(self.h_na_V.rearrange("(a p) n -> p a n", p=128), p0[:, 1024:2048].rearrange("p (a n) -> p a n", n=512), reads=[b_p], writes=[hb])
        q0 = P.sbuf("X_q0", [128, 12, 2], F32)
        q1 = P.sbuf("X_q1", [128, 12, 2], F32)
        q2 = P.sbuf("X_q2", [128, 12, 2], F32)
        b_q = Buf()
        P.dma(q0[:], self.gathB[0:128, :].rearrange("p (c t) -> p c t", t=2), reads=[bg], writes=[b_q])
        P.dma(q1[:], self.gathB[128:256, :].rearrange("p (c t) -> p c t", t=2), reads=[bg], writes=[b_q])
        P.add('dve', lambda e: e.tensor_scalar(out=q0[:], in0=q0[:], scalar1=fl[:, 0:1], scalar2=None, op0=ALU.mult), reads=[b_q, b_fl], writes=[b_q])
        P.add('dve', lambda e: e.scalar_tensor_tensor(out=q0[:], in0=q1[:], scalar=fl[:, 1:2], in1=q0[:], op0=ALU.mult, op1=ALU.add), reads=[b_q, b_fl], writes=[b_q])
        P.add('dve', lambda e: e.tensor_copy(q2[:, :, 0:1], q0[:, :, 1:2]), reads=[b_q], writes=[b_q])
        P.add('dve', lambda e: e.tensor_copy(q2[:, :, 1:2], q0[:, :, 0:1]), reads=[b_q], writes=[b_q])
        P.dma(self.h_dn_x.rearrange("(c p) t -> p c t", p=128), q2[:], reads=[b_q], writes=[hb])

    def exchange_state(self, l, kind, S, b_S, Sb, b_Sb):
        P = self.P
        if not self.couple:
            P.add('pool', lambda e: e.memset(S[:], 0.0), writes=b_S)
            P.add('pool', lambda e: e.memset(Sb[:], 0.0), writes=b_Sb)
            return
        bpub, bg = self.dbuf(('pubS', kind, l)), self.dbuf(('gathS', kind, l))
        P.dma(self.pubS.rearrange("p (h n) -> p h n", h=4), S[:], reads=b_S, writes=[self.dbuf('pubS_any')])
        P.add('pool', lambda e: e.collective_compute("AllGather", ALU.bypass, replica_groups=self.RG, ins=[self.pubS], outs=[self.gathS]),
              reads=[self.dbuf('pubS_any')], writes=[self.dbuf('gathS_any')])
        t1 = P.sbuf(f"XS_{kind}", [128, 4, 128], F32)
        fl = P.sbuf(f"XSf_{kind}", [128, 2], F32)
        bt, bf = Buf(), Buf()
        P.dma(fl[:], self.flag_in, writes=[bf])
        P.dma(S[:], self.gathS[0:128, :].rearrange("p (h n) -> p h n", h=4), reads=[self.dbuf('gathS_any')], writes=b_S)
        P.dma(t1[:], self.gathS[128:256, :].rearrange("p (h n) -> p h n", h=4), reads=[self.dbuf('gathS_any')], writes=[bt])
        P.add('dve', lambda e: e.tensor_scalar(out=S[:], in0=S[:], scalar1=fl[:, 0:1], scalar2=None, op0=ALU.mult), reads=b_S + [bf], writes=b_S)
        P.add('dve', lambda e: e.scalar_tensor_tensor(out=S[:], in0=t1[:], scalar=fl[:, 1:2], in1=S[:], op0=ALU.mult, op1=ALU.add), reads=b_S + [bf, bt], writes=b_S)
        P.add('act', lambda e: e.copy(Sb[:], S[:]), reads=b_S, writes=b_Sb)

    def phase_DNconv(self, l):
        P = self.P
        T, NT = self.T, self.NT
        P.begin_phase()
        cw = P.sbuf("DC_w", [128, 12, 5], F32)
        b_cw = Buf()
        P.dma(cw[:], self.dn_conv_in[l], writes=[b_cw])
        xp = Pool([(P.sbuf(f"DC_x{i}", [128, T + 4], F32), Buf()) for i in range(2)])
        acc = Pool([(P.sbuf(f"DC_a{i}", [128, T], F32), Buf()) for i in range(2)])
        sb = Pool([(P.sbuf(f"DC_s{i}", [128, T], BF16), Buf()) for i in range(2)])
        st = Pool([(P.sbuf(f"DC_t{i}", [128, 512], BF16), Buf()) for i in range(3)])
        pss = Pool([(self.psb[i], self.psbuf[i]) for i in range(4)])
        for ct in range(12):
            x, xb = xp.next()
            r0 = C_DN_QKV + ct * 128
            P.add('pool', lambda e, x=x: e.memset(x[:, 0:2], 0.0), writes=[xb])
            P.dma(x[:, 2:2 + T], self.UT[r0:r0 + 128, :], reads=[self.dbuf(('UT', l))], writes=[xb])
            P.dma(x[:, 2 + T:4 + T], self.h_dn_x[ct * 128:(ct + 1) * 128, :], reads=[self.dbuf('halo')], writes=[xb])
            a, ab = acc.next()
            eng = 'dve'
            P.add(eng, lambda e, a=a, x=x, ct=ct: e.tensor_scalar(out=a[:], in0=x[:, 0:T], scalar1=cw[:, ct, 0:1], scalar2=None, op0=ALU.mult),
                  reads=[xb, b_cw], writes=[ab])
            for j in range(1, 5):
                P.add(eng, lambda e, a=a, x=x, ct=ct, j=j: e.scalar_tensor_tensor(out=a[:], in0=x[:, j:j + T], scalar=cw[:, ct, j:j + 1], in1=a[:],
                                                                                  op0=ALU.mult, op1=ALU.add), reads=[xb, b_cw, ab], writes=[ab])
            sbt, sbb = sb.next()
            P.add('act', lambda e, sbt=sbt, a=a: e.activation(out=sbt[:], in_=a[:], func=AF.Silu), reads=[ab], writes=[sbb])
            for tg in range(NT // 4):
                ps, pb = pss.next()
                psv = ps[:, 0:256].bitcast(BF16)
                for k in range(4):
                    tt = tg * 4 + k
                    P.add('pe', lambda e, psv=psv, k=k, tt=tt, sbt=sbt: e.transpose(psv[:, k * 128:(k + 1) * 128], sbt[:, tt * 128:(tt + 1) * 128], self.ident_b),
                          reads=[sbb, self.b_cb], writes=[pb])
                s_, s_b = st.next()
                P.add('act' if tg % 2 else 'dve', (lambda e, s_=s_, psv=psv: e.copy(s_[:], psv[:, :])) if tg % 2 else (lambda e, s_=s_, psv=psv: e.tensor_copy(s_[:], psv[:, :])),
                      reads=[pb], writes=[s_b])
                P.dma(self.dnc_d[tg * 512:(tg + 1) * 512, ct * 128:(ct + 1) * 128].rearrange("(k p) c -> p k c", p=128),
                      s_[:].rearrange("p (k c) -> p k c", k=4), reads=[s_b], writes=[self.dbuf(('dnc', l))])

    def phase_DN(self, l):
        P = self.P
        T, NT = self.T, self.NT
        self.phase_DNconv(l)
        P.begin_phase()
        par = P.sbuf("D_par", [128, 2, 2, 4], F32)
        gain = P.sbuf("D_gain", [128, 512], F32)
        b_par = Buf()
        P.dma(par[:], self.dn_par_in[l], writes=[b_par])
        P.dma(gain[:], self.dn_gain_in[l], writes=[b_par])
        P.add('act', lambda e: e.activation(out=par[:, 0], in_=par[:, 0], func=AF.Exp), reads=[b_par], writes=[b_par])
        P.add('dve', lambda e: e.tensor_scalar(out=par[:, 0], in0=par[:, 0], scalar1=-1.0, scalar2=None, op0=ALU.mult), reads=[b_par], writes=[b_par])
        S = P.sbuf("D_S", [128, 4, 128], F32)
        Sb = P.sbuf("D_Sb", [128, 4, 128], BF16)
        b_S = [Buf() for _ in range(4)]
        b_Sb = [Buf() for _ in range(4)]
        cin = Pool([(P.sbuf(f"D_c{i}", [128, 1536], BF16), Buf()) for i in range(2)])
        zin = Pool([(P.sbuf(f"D_z{i}", [128, 512], F32), Buf()) for i in range(2)])
        abin = Pool([(P.sbuf(f"D_ab{i}", [128, 16], F32), Buf()) for i in range(2)])
        sm = Pool([(P.sbuf(f"D_sm{i}", [128, 12, 4], F32), Buf()) for i in range(2)])
        sqk = P.sbuf("D_sqk", [128, 1024], F32)
        b_sqk = Buf()
        tokb = Pool([(P.sbuf(f"D_tb{i}", [128, 3, 512], BF16), Buf()) for i in range(2)])
        kdec = Pool([(P.sbuf(f"D_kd{i}", [128, 512], BF16), Buf()) for i in range(2)])
        vb4 = Pool([(P.sbuf(f"D_vb4{i}", [128, 4, 128], F32), Buf()) for i in range(3)])
        vbe = Pool([(P.sbuf(f"D_vbe{i}", [128, 512], F32), Buf()) for i in range(2)])
        fm = [P.sbuf(f"D_fm{i}", [128, 3, 4, 128], BF16) for i in range(3)]
        b_fm = [Buf() for _ in range(3)]
        qTp = Pool([(P.sbuf(f"D_qT{i}", [128, 128], BF16), Buf()) for i in range(3)])
        gbc = Pool([(P.sbuf(f"D_gbc{i}", [128, 2, 128], F32), Buf()) for i in range(3)])
        Wm = Pool([(P.sbuf(f"D_W{i}", [128, 3, 128], F32), Buf()) for i in range(3)])
        Nm = Pool([(P.sbuf(f"D_N{i}", [128, 2, 128], F32), Buf()) for i in range(3)])
        FT = Pool([(P.sbuf(f"D_FT{i}", [128, 128], F32), Buf()) for i in range(3)])
        Yb = Pool([(P.sbuf(f"D_Yb{i}", [128, 128], BF16), Buf()) for i in range(3)])
        aTb = Pool([(P.sbuf(f"D_aT{i}", [128, 128], BF16), Buf()) for i in range(3)])
        last4 = Pool([(P.sbuf(f"D_l4{i}", [128, 4], F32), Buf()) for i in range(3)])
        rhsc = Pool([(P.sbuf(f"D_rc{i}", [128, 128], BF16), Buf()) for i in range(4)])
        vnew = Pool([(P.sbuf(f"D_vn{i}", [128, 128], BF16), Buf()) for i in range(4)])
        ost = Pool([(P.sbuf(f"D_o{i}", [128, 512], F32), Buf()) for i in range(2)])
        o1t = Pool([(P.sbuf(f"D_o1{i}", [128, 512], F32), Buf()) for i in range(2)])
        sq = P.sbuf("D_sq", [128, 512], F32)
        ss = P.sbuf("D_ss", [128, 4], F32)
        b_sq, b_ss = Buf(), Buf()
        gg = P.sbuf("D_gg", [128, 512], F32)
        b_gg = Buf()
        yb = Pool([(P.sbuf(f"D_y{i}", [128, 512], BF16), Buf()) for i in range(2)])
        yst = Pool([(P.sbuf(f"D_ys{i}", [128, 4, 128], BF16), Buf()) for i in range(2)])
        for i in range(3):
            P.add('pool', lambda e, i=i: e.memset(fm[i][:], 0.0), writes=[b_fm[i]])
        psA = Pool([(self.psb[i], self.psbuf[i]) for i in (0, 1)])
        psW = Pool([(self.psb[i], self.psbuf[i]) for i in (2, 3)])
        psN = Pool([(self.psb[i], self.psbuf[i]) for i in (4,)])
        psO = Pool([(self.psb[i], self.psbuf[i]) for i in (5,)])
        psS = Pool([(self.psb[i], self.psbuf[i]) for i in (6, 7)])
        ind4 = self.cf[:, 16:20, :]
        kq = 0
        for d in (0, 1):
            cb = 2 if d == 0 else 6
            Mrem, M1 = self.cf[:, cb, :], self.cf[:, cb + 1, :]
            mk_ij_s = self.cf[:, 12 if d == 0 else 10, :]
            mk_ji_s = self.cf[:, 10 if d == 0 else 12, :]
            mk_ji_i = self.cf[:, 11 if d == 0 else 13, :]
            if d == 0:
                P.add('pool', lambda e: e.memset(S[:], 0.0), writes=b_S)
                P.add('pool', lambda e: e.memset(Sb[:], 0.0), writes=b_Sb)
            else:
                self.exchange_state(l, 'dn', S, b_S, Sb, b_Sb)
            tiles = range(NT) if d == 0 else range(NT - 1, -1, -1)
            for tt in tiles:
                r0 = tt * 128
                c_, cb_ = cin.next()
                P.dma(c_[:], self.dnc_d[r0:r0 + 128, :], reads=[self.dbuf(('dnc', l))], writes=[cb_])
                ab, abb = abin.next()
                P.dma(ab[:], self.Utok[r0:r0 + 128, C_DN_AB:C_DN_AB + 16], reads=[self.dbuf(('Utok', l))], writes=[abb])
                if d == 1:
                    z, zb = zin.next()
                    P.dma(z[:], self.Utok[r0:r0 + 128, C_DN_Z:C_DN_Z + 512], reads=[self.dbuf(('Utok', l))], writes=[zb])
                m, mb = sm.next()
                P.add('pool', lambda e, c_=c_: e.tensor_tensor(out=sqk[:], in0=c_[:, 0:1024], in1=c_[:, 0:1024], op=ALU.mult), reads=[cb_], writes=[b_sqk])
                P.add('dve', lambda e, m=m: e.tensor_reduce(out=m[:, 0:2, :].rearrange("p a h -> p (a h)"), in_=sqk[:].rearrange("p (a d) -> p a d", a=8),
                                                            axis=AX.X, op=ALU.add), reads=[b_sqk], writes=[mb])
                P.add('dve', lambda e, m=m: e.tensor_scalar(out=m[:, 0:2, :], in0=m[:, 0:2, :], scalar1=EPS, scalar2=None, op0=ALU.add), reads=[mb], writes=[mb])
                P.add('act', lambda e, m=m: e.activation(out=m[:, 0:2, :], in_=m[:, 0:2, :], func=AF.Sqrt), reads=[mb], writes=[mb])
                P.add('dve', lambda e, m=m: e.reciprocal(m[:, 0:2, :], m[:, 0:2, :]), reads=[mb], writes=[mb])
                P.add('dve', lambda e, m=m: e.tensor_scalar(out=m[:, 0, :], in0=m[:, 0, :], scalar1=128.0 ** -0.5, scalar2=None, op0=ALU.mult), reads=[mb], writes=[mb])
                P.add('dve', lambda e, m=m, ab=ab, d=d: e.tensor_tensor(out=m[:, 2, :], in0=ab[:, 4 * d:4 * d + 4], in1=par[:, 1, d, :], op=ALU.add),
                      reads=[abb, b_par], writes=[mb])
                P.add('act', lambda e, m=m: e.activation(out=m[:, 2, :], in_=m[:, 2, :], func=AF.Exp), reads=[mb], writes=[mb])
                P.add('act', lambda e, m=m: e.activation(out=m[:, 2, :], in_=m[:, 2, :], func=AF.Ln, bias=1.0), reads=[mb], writes=[mb])
                P.add('dve', lambda e, m=m, d=d: e.tensor_tensor(out=m[:, 2, :], in0=m[:, 2, :], in1=par[:, 0, d, :], op=ALU.mult), reads=[mb, b_par], writes=[mb])
                P.add('act', lambda e, m=m, ab=ab, d=d: e.activation(out=m[:, 3, :], in_=ab[:, 8 + 4 * d:12 + 4 * d], func=AF.Sigmoid), reads=[abb], writes=[mb])
                P.add('act', lambda e, m=m: e.activation(out=m[:, 4, :], in_=m[:, 3, :], func=AF.Ln), reads=[mb], writes=[mb])
                pa, pab = psA.next()
                P.add('pe', lambda e, pa=pa, m=m, M1=M1: e.matmul(pa[:, 0:4], M1, m[:, 2, :], start=True, stop=True), reads=[mb, self.b_cf], writes=[pab])
                P.add('pe', lambda e, pa=pa, m=m, Mrem=Mrem: e.matmul(pa[:, 4:8], Mrem, m[:, 2, :], start=True, stop=True), reads=[mb, self.b_cf], writes=[pab])
                P.add('dve', lambda e, pa=pa, m=m: e.tensor_copy(m[:, 5:7, :].rearrange("p a h -> p (a h)"), pa[:, 0:8]), reads=[pab], writes=[mb])
                P.add('act', lambda e, m=m: e.activation(out=m[:, 7:9, :], in_=m[:, 5:7, :], func=AF.Exp), reads=[mb], writes=[mb])
                P.add('dve', lambda e, m=m: e.tensor_tensor(out=m[:, 9, :], in0=m[:, 5, :], in1=m[:, 4, :], op=ALU.add), reads=[mb], writes=[mb])
                P.add('dve', lambda e, m=m: e.tensor_scalar(out=m[:, 10, :], in0=m[:, 5, :], scalar1=-1.0, scalar2=None, op0=ALU.mult), reads=[mb], writes=[mb])
                P.add('dve', lambda e, m=m: e.scalar_tensor_tensor(out=m[:, 11, :], in0=m[:, 3, :], scalar=-1.0, in1=m[:, 7, :], op0=ALU.mult, op1=ALU.mult),
                      reads=[mb], writes=[mb])
                tb, tbb = tokb.next()
                kdt, kdb = kdec.next()
                ve, veb = vbe.next()
                c3 = c_[:].rearrange("p (a h d) -> p a h d", a=3, h=4)
                bc = lambda k, m=m: m[:, k, :].unsqueeze(2).to_broadcast([128, 4, 128])
                b0, b1, b3, b7, b8 = bc(0), bc(1), bc(3), bc(7), bc(8)
                v4h = lambda ap: ap.rearrange("p (h d) -> p h d", h=4)
                P.add('dve', lambda e, o_=v4h(tb[:, 0, :]), i0=c3[:, 1], i1=b1: e.tensor_tensor(out=o_, in0=i0, in1=i1, op=ALU.mult), reads=[cb_, mb], writes=[tbb])
                P.add('pool', lambda e, o_=v4h(tb[:, 1, :]), i0=c3[:, 0], i1=b0: e.tensor_tensor(out=o_, in0=i0, in1=i1, op=ALU.mult), reads=[cb_, mb], writes=[tbb])
                P.add('pool', lambda e, o_=v4h(tb[:, 2, :]), i0=v4h(tb[:, 1, :]), i1=b7: e.tensor_tensor(out=o_, in0=i0, in1=i1, op=ALU.mult), reads=[tbb, mb], writes=[tbb])
                P.add('dve', lambda e, o_=v4h(kdt[:]), i0=v4h(tb[:, 0, :]), i1=b8: e.tensor_tensor(out=o_, in0=i0, in1=i1, op=ALU.mult), reads=[tbb, mb], writes=[kdb])
                P.add('pool', lambda e, o_=v4h(ve[:]), i0=c3[:, 2], i1=b3: e.tensor_tensor(out=o_, in0=i0, in1=i1, op=ALU.mult), reads=[cb_, mb], writes=[veb])
                po, pob = psO.next()
                for h in range(4):
                    hs = slice(h * 128, (h + 1) * 128)
                    k3 = kq % 3
                    kq += 1
                    f = fm[k3]
                    fbuf = b_fm[k3]
                    pt, ptb = psA.next()
                    ptv = pt[:, 0:256].bitcast(BF16)
                    for k in range(3):
                        P.add('pe', lambda e, ptv=ptv, k=k, tb=tb, hs=hs: e.transpose(ptv[:, k * 128:(k + 1) * 128], tb[:, k, hs], self.ident_b),
                              reads=[tbb, self.b_cb], writes=[ptb])
                    ff_ = f[:]
                    base = ff_.offset
                    pstr = list(ff_.ap[0])
                    k4_out = bass.AP(ff_.tensor, base + 512, [pstr, [160, 4], [1, 32]])
                    q4_out = bass.AP(ff_.tensor, base + 1024, [pstr, [160, 4], [1, 32]])
                    qt, qtb = qTp.next()
                    P.add('act', lambda e, f=f, ptv=ptv: e.copy(f[:, 0, 0, :], ptv[:, 0:128]), reads=[ptb], writes=[fbuf])
                    P.add('dve', lambda e, k4_out=k4_out, ptv=ptv: e.tensor_copy(k4_out, ptv[:, 0:128].rearrange("p (c n) -> p c n", c=4)), reads=[ptb], writes=[fbuf])
                    P.add('act', lambda e, qt=qt, ptv=ptv: e.copy(qt[:], ptv[:, 128:256]), reads=[ptb], writes=[qtb])
                    P.add('dve', lambda e, q4_out=q4_out, ptv=ptv: e.tensor_copy(q4_out, ptv[:, 256:384].rearrange("p (c n) -> p c n", c=4)), reads=[ptb], writes=[fbuf])
                    kT = f[:, 0, 0, :]
                    gb, gbb = gbc.next()
                    P.add('pool', lambda e, gb=gb, m=m, h=h: e.tensor_scalar(out=gb[:, 0, :], in0=self.ones_f, scalar1=m[:, 2, h:h + 1], scalar2=None, op0=ALU.mult),
                          reads=[mb, self.b_cf], writes=[gbb])
                    P.add('pool', lambda e, gb=gb, m=m, h=h: e.tensor_scalar(out=gb[:, 1, :], in0=self.ones_f, scalar1=m[:, 4, h:h + 1], scalar2=None, op0=ALU.mult),
                          reads=[mb, self.b_cf], writes=[gbb])
                    pw, pwb = psW.next()
                    P.add('pe', lambda e, pw=pw, gb=gb, M1=M1: e.matmul(pw[:, 0:128], gb[:, 0, :], M1, start=True, stop=False), reads=[gbb, self.b_cf], writes=[pwb])
                    P.add('pe', lambda e, pw=pw, mk=mk_ij_s: e.matmul(pw[:, 0:128], self.ident_f, mk, start=False, stop=True), reads=[self.b_cf], writes=[pwb])
                    P.add('pe', lambda e, pw=pw, gb=gb, M1=M1: e.matmul(pw[:, 128:256], gb[:, 0, :], M1, start=True, stop=False), reads=[gbb, self.b_cf], writes=[pwb])
                    P.add('pe', lambda e, pw=pw, gb=gb: e.matmul(pw[:, 128:256], gb[:, 1, :], self.ident_f, start=False, stop=False), reads=[gbb, self.b_cf], writes=[pwb])
                    P.add('pe', lambda e, pw=pw, mk=mk_ji_s: e.matmul(pw[:, 128:256], self.cf[:, 20, :], mk, start=False, stop=True), reads=[self.b_cf], writes=[pwb])
                    P.add('pe', lambda e, pw=pw, gb=gb, M1=M1: e.matmul(pw[:, 256:384], gb[:, 0, :], M1, start=True, stop=False), reads=[gbb, self.b_cf], writes=[pwb])
                    P.add('pe', lambda e, pw=pw, mk=mk_ji_i: e.matmul(pw[:, 256:384], self.cf[:, 20, :], mk, start=False, stop=True), reads=[self.b_cf], writes=[pwb])
                    P.add('pe', lambda e, pw=pw, kT=kT: e.matmul(pw[:, 384:512], kT, kT, start=True, stop=True), reads=[fbuf], writes=[pwb])
                    W, Wb = Wm.next()
                    P.add('act', lambda e, W=W, pw=pw, m=m, h=h: e.activation(out=W[:, 0, :], in_=pw[:, 0:128], func=AF.Exp, scale=-1.0, bias=m[:, 9, h:h + 1]),
                          reads=[pwb, mb], writes=[Wb])
                    P.add('act', lambda e, W=W, pw=pw, m=m, h=h: e.activation(out=W[:, 1:3, :].rearrange("p a n -> p (a n)"), in_=pw[:, 128:384], func=AF.Exp,
                                                                              bias=m[:, 10, h:h + 1]), reads=[pwb, mb], writes=[Wb])
                    Nt, Nb = Nm.next()
                    P.add('dve', lambda e, Nt=Nt, pw=pw, W=W: e.tensor_tensor(out=Nt[:, 0, :], in0=pw[:, 384:512], in1=W[:, 0, :], op=ALU.mult), reads=[pwb, Wb], writes=[Nb])
                    P.add('pool', lambda e, Nt=Nt: e.tensor_tensor(out=Nt[:, 0, :], in0=Nt[:, 0, :], in1=self.ident_f, op=ALU.add), reads=[Nb, self.b_cf], writes=[Nb])
                    P.add('dve', lambda e, Nt=Nt, pw=pw, W=W: e.tensor_tensor(out=Nt[:, 1, :], in0=pw[:, 384:512], in1=W[:, 1, :], op=ALU.mult), reads=[pwb, Wb], writes=[Nb])
                    P.add('pool', lambda e, Nt=Nt: e.tensor_tensor(out=Nt[:, 1, :], in0=self.ident_f, in1=Nt[:, 1, :], op=ALU.subtract), reads=[Nb, self.b_cf], writes=[Nb])
                    pq, pqb = psA.next()
                    P.add('pe', lambda e, pq=pq, kT=kT, qt=qt: e.matmul(pq[:, 0:128], kT, qt[:], start=True, stop=True), reads=[fbuf, qtb], writes=[pqb])
                    at, atb = aTb.next()
                    P.add('dve', lambda e, at=at, pq=pq, W=W: e.tensor_tensor(out=at[:], in0=pq[:, 0:128], in1=W[:, 2, :], op=ALU.mult), reads=[pqb, Wb], writes=[atb])
                    P.add('pe', lambda e, pq=pq, gb=gb: e.matmul(pq[:, 128:132], gb[:, 0, :], self.cf[:, 16:20, 0], start=True, stop=True), reads=[gbb, self.b_cf], writes=[pqb])
                    l4, l4b = last4.next()
                    P.add('act', lambda e, l4=l4, pq=pq: e.activation(out=l4[:], in_=pq[:, 128:132], func=AF.Exp), reads=[pqb], writes=[l4b])
                    for it in range(4):
                        pn, pnb = psN.next()
                        P.add('pe', lambda e, pn=pn, Nt=Nt: e.matmul(pn[:, 0:128], Nt[:, 1, :], Nt[:, 0, :], start=True, stop=True), reads=[Nb], writes=[pnb])
                        ft, ftb = FT.next()
                        P.add('dve', lambda e, ft=ft, pn=pn: e.tensor_tensor(out=ft[:], in0=self.ident_f, in1=pn[:, 0:128], op=ALU.subtract), reads=[pnb, self.b_cf], writes=[ftb])
                        P.add('pe', lambda e, pn=pn, ft=ft, Nt=Nt: e.matmul(pn[:, 128:256], ft[:], Nt[:, 1, :], start=True, stop=True), reads=[ftb, Nb], writes=[pnb])
                        P.add('dve', lambda e, Nt=Nt, pn=pn: e.tensor_tensor(out=Nt[:, 1, :], in0=Nt[:, 1, :], in1=pn[:, 128:256], op=ALU.add), reads=[pnb, Nb], writes=[Nb])
                    ybf, ybb_ = Yb.next()
                    P.add('act', lambda e, ybf=ybf, Nt=Nt: e.copy(ybf[:], Nt[:, 1, :]), reads=[Nb], writes=[ybb_])
                    v4, v4b = vb4.next()
                    P.add('pool', lambda e, v4=v4, ve=ve, hs=hs: e.tensor_tensor(out=v4[:], in0=ve[:, hs].unsqueeze(1).to_broadcast([128, 4, 128]), in1=ind4, op=ALU.mult),
                          reads=[veb, self.b_cf], writes=[v4b])
                    corder = (0, 1, 2, 3) if d == 0 else (3, 2, 1, 0)
                    for ci, c in enumerate(corder):
                        pS, pSb = psS.next()
                        P.add('pe', lambda e, pS=pS, f=f, c=c, h=h: e.matmul(pS[:, 0:128], f[:, 1, c, :], Sb[:, h, :], start=True, stop=True),
                              reads=[fbuf, b_Sb[h]], writes=[pSb])
                        P.add('pe', lambda e, po=po, hs=hs, f=f, c=c, h=h, ci=ci: e.matmul(po[:, hs], f[:, 2, c, :], Sb[:, h, :], start=(ci == 0), stop=False),
                              reads=[fbuf, b_Sb[h]], writes=[pob])
                        rc, rcb = rhsc.next()
                        P.add('dve', lambda e, rc=rc, pS=pS, m=m, h=h, v4=v4, c=c: e.scalar_tensor_tensor(
                            out=rc[:], in0=pS[:, 0:128], scalar=m[:, 11, h:h + 1], in1=v4[:, c, :], op0=ALU.mult, op1=ALU.add), reads=[pSb, mb, v4b], writes=[rcb])
                        P.add('pe', lambda e, pS=pS, ybf=ybf, rc=rc: e.matmul(pS[:, 128:256], ybf[:], rc[:], start=True, stop=True), reads=[ybb_, rcb], writes=[pSb])
                        vn, vnb = vnew.next()
                        P.add('act', lambda e, vn=vn, pS=pS: e.copy(vn[:], pS[:, 128:256]), reads=[pSb], writes=[vnb])
                        P.add('pe', lambda e, po=po, hs=hs, at=at, vn=vn, ci=ci: e.matmul(po[:, hs], at[:], vn[:], start=False, stop=(ci == 3)),
                              reads=[atb, vnb], writes=[pob])
                        P.add('pe', lambda e, pS=pS, kdt=kdt, hs=hs, vn=vn: e.matmul(pS[:, 256:384], kdt[:, hs], vn[:], start=True, stop=True), reads=[kdb, vnb], writes=[pSb])
                        P.add('dve', lambda e, pS=pS, h=h, l4=l4, c=c: e.scalar_tensor_tensor(
                            out=S[:, h, :], in0=S[:, h, :], scalar=l4[:, c:c + 1], in1=pS[:, 256:384], op0=ALU.mult, op1=ALU.add),
                            reads=[pSb, l4b, b_S[h]], writes=[b_S[h]])
                        P.add('act', lambda e, h=h: e.copy(Sb[:, h, :], S[:, h, :]), reads=[b_S[h]], writes=[b_Sb[h]])
                if d == 0:
                    o, ob = ost.next()
                    P.add('act', lambda e, o=o, po=po: e.copy(o[:], po[:]), reads=[pob], writes=[ob])
                    P.dma(self.o1dn_d[r0:r0 + 128, :], o[:], reads=[ob], writes=[self.dbuf(('o1dn', l))])
                else:
                    o1, o1b = o1t.next()
                    P.dma(o1[:], self.o1dn_d[r0:r0 + 128, :], reads=[self.dbuf(('o1dn', l))], writes=[o1b])
                    o, ob = ost.next()
                    P.add('dve', lambda e, o=o, po=po, o1=o1: e.tensor_tensor(out=o[:], in0=po[:], in1=o1[:], op=ALU.add), reads=[pob, o1b], writes=[ob])
                    self.norm_gate_emit(l, o, ob, z[:], zb, gain, b_par, sq, b_sq, ss, b_ss, gg, b_gg, yb, yst, 1536, r0)

    def fake_mixer(self, l):
        P = self.P
        P.begin_phase()
        T = self.T
        st = Pool([(P.sbuf(f"fk{i}", [128, T], F32), Buf()) for i in range(2)])
        sb = Pool([(P.sbuf(f"fkb{i}", [128, T], BF16), Buf()) for i in range(2)])
        rows = [C_HG_Q + i * 128 for i in range(12)] + [C_DN_QKV + i * 128 for i in range(4)]
        for i, r0 in enumerate(rows):
            s, sbf = st.next()
            b, bbf = sb.next()
            P.dma(s[:], self.UT[r0:r0 + 128, :], reads=[self.dbuf(('UT', l))], writes=[sbf])
            P.add('dve', lambda e, s=s, b=b: e.tensor_copy(b[:], s[:]), reads=[sbf], writes=[bbf])
            P.dma(self.yT[i * 128:(i + 1) * 128, :], b[:], reads=[bbf], writes=[self.dbuf(('yT', l))])

    def dump_debug(self):
        P = self.P
        if 'UT' in self.debug:
            P.barrier()
            P.dma(self.dbg_UT, self.UT, reads=[self.dbuf(('UT', 0))])
            P.dma(self.dbg_Utok, self.Utok, reads=[self.dbuf(('Utok', 0))])

    def build(self):
        self.setup()
        self.compute_mod()
        self.convert_weights()
        if not self.couple:
            self.zero_halos()
        if 'hg' in self.mixers:
            self.compute_lb()
        for l in range(self.depth):
            x_src = self.xT_in if l == 0 else self.xs
            x_dst = self.yT_out if l == self.depth - 1 else self.xs
            self.phase_A(l, x_src)
            if 'UT' in self.debug and l == 0:
                self.dump_debug()
            if 'fake' in self.mixers:
                self.fake_mixer(l)
            else:
                self.phase_B1(l)
                if self.couple:
                    self.exchange_halos(l)
                zr = []
                if 'swa' in self.mixers:
                    self.phase_SWA(l)
                else:
                    zr += [i * 128 for i in range(0, 4)]
                if 'hg' in self.mixers:
                    self.phase_HG(l)
                else:
                    zr += [i * 128 for i in range(4, 8)]
                if 'na' in self.mixers:
                    self.phase_NA(l)
                else:
                    zr += [i * 128 for i in range(8, 12)]
                if 'dn' in self.mixers:
                    self.phase_DN(l)
                else:
                    zr += [i * 128 for i in range(12, 16)]
                if zr:
                    self.zero_y(l, zr)
            self.phase_C(l, x_src, x_dst)
        self.P.finalize()
        return self.nc


def make_consts():
    c = np.zeros((128, NCONST, 128), np.float32)
    c[:, 0, :] = np.eye(128)
    c[:, 1, :] = 1.0
    t = np.arange(128)[:, None]
    i = np.arange(128)[None, :]
    same = (t // 32) == (i // 32)
    c[:, 2, :] = same & (t > i)
    c[:, 3, :] = same & (t <= i)
    c[:, 5, :] = same & (t <= i)
    c[:, 6, :] = same & (t < i)
    c[:, 7, :] = same & (t >= i)
    c[:, 9, :] = same & (t >= i)
    for cc in range(4):
        c[:, 16 + cc, :] = ((np.arange(128) // 32) == cc)[:, None]
    c[:, 20, :] = -np.eye(128)
    same = (t // 32) == (i // 32)
    BIG = 30000.0
    c[:, 10, :] = np.where(same & (t < i), 0.0, BIG)
    c[:, 11, :] = np.where(same & (t <= i), 0.0, BIG)
    c[:, 12, :] = np.where(same & (t > i), 0.0, BIG)
    c[:, 13, :] = np.where(same & (t >= i), 0.0, BIG)
    c[:, 14, :] = same
    c[:, 15, 0:64] = (t < 64)
    c[:, 15, :] = 0.0
    c[:, 15, 0] = (np.arange(128) < 64)
    c[:, 15, 1] = (np.arange(128) >= 64)
    return c


def arrange_vec(v):
    n = v.shape[-1] // 128
    return np.ascontiguousarray(np.swapaxes(v.reshape(*v.shape[:-1], n, 128), -1, -2))


NEG = -30000.0


def rope_tables(pos):
    T = len(pos)
    half = 16
    inv_freq = np.power(np.float32(500000.0), -np.arange(half, dtype=np.float32) / np.float32(half)).astype(np.float32)
    ang = pos.astype(np.float32)[:, None] * inv_freq[None, :]
    cs = np.stack([np.cos(ang), np.sin(ang)], axis=1).astype(np.float32)
    return np.ascontiguousarray(cs.reshape(T // 128, 128, 2, 16).transpose(1, 0, 2, 3))


def swa_masks(coupled):
    p = np.arange(128)[:, None]
    c = np.arange(128)[None, :]
    m = np.full((128, 4, 128), NEG, np.float32)
    ai = np.abs(p - 64 - c) <= 64
    m[:, 0, :] = np.where(ai & (p >= 64), 0.0, NEG)
    m[:, 1, :] = np.where(ai, 0.0, NEG)
    bi = np.abs(p + 64 - c) <= 64
    m[:, 2, :] = np.where(bi, 0.0, NEG)
    bl = np.where(p < 64, bi, ((p - 64) + c >= 127) & bool(coupled))
    m[:, 3, :] = np.where(bl, 0.0, NEG)
    return m


def na_bias_mats(rpb, tok_own, tok_partner, Ls, T):
    rows = Ls // 64
    J = T // 128
    jm = J // 2
    specs = [(0, [('o', 0), ('o', 1), ('o', 2), ('o', 3)]),
             (1, [('o', 0), ('o', 1), ('o', 2), ('o', 3)]),
             (jm, [('o', jm - 2), ('o', jm - 1), ('o', jm), ('o', jm + 1), ('o', jm + 2)]),
             (J - 2, [('o', J - 4), ('o', J - 3), ('o', J - 2), ('o', J - 1), ('h', 0)]),
             (J - 1, [('o', J - 4), ('o', J - 3), ('o', J - 2), ('o', J - 1), ('h', 0), ('h', 1)])]
    out = np.full((4, 128, 24, 128), NEG, np.float32)
    idx = 0
    for (j, keys) in specs:
        tq = tok_own[j * 128:(j + 1) * 128]
        rq, cq = tq // 64, tq % 64
        r0 = np.clip(rq - 4, 0, rows - 8)
        c0 = np.clip(cq - 8, 0, 64 - 16)
        for (kind, kt) in keys:
            if kind == 'o':
                tk = tok_own[kt * 128:(kt + 1) * 128]
            elif tok_partner is not None:
                pt = J - 1 - kt
                tk = tok_partner[pt * 128:(pt + 1) * 128]
            else:
                tk = None
            if tk is not None:
                rk, ck = tk // 64, tk % 64
                valid = ((rk[:, None] >= r0[None, :]) & (rk[:, None] < r0[None, :] + 8) &
                         (ck[:, None] >= c0[None, :]) & (ck[:, None] < c0[None, :] + 16))
                ro = np.clip(rk[:, None] - rq[None, :] + 7, 0, 14)
                co = np.clip(ck[:, None] - cq[None, :], -15, 15) + 15
                for h in range(4):
                    g = rpb[h][ro, co]
                    out[h, :, idx, :] = np.where(valid, g, NEG)
            idx += 1
    assert idx == 24
    return out


def rep128(v):
    return np.ascontiguousarray(np.broadcast_to(v[None, :], (128, v.shape[0])))


def extra_inputs(inp, L, T, tok, rev, sel):
    lg = np.asarray(inp['hgrn_lb_logits'])[:L]
    if rev:
        lg = lg[:, ::-1]
    hg_lb = np.ascontiguousarray(np.broadcast_to(lg[None], (128, L, 2, 512))).astype(np.float32)
    hg_lbT = np.ascontiguousarray(lg.reshape(L, 2, 4, 128).transpose(3, 0, 1, 2)).astype(np.float32)
    hg_gain = np.stack([rep128(np.tile(np.asarray(inp['hgrn_norm_g'])[l], 4)) for l in range(L)]).astype(np.float32)
    flag = np.ascontiguousarray(np.broadcast_to(np.asarray(sel, np.float32)[None, :], (128, 2)))
    return dict(hg_lb=hg_lb, hg_lbT=hg_lbT, hg_gain=hg_gain, flag=flag)


def dn_inputs(inp, L, rev):
    cw = np.asarray(inp['dn_conv_w'])[:L]
    if rev:
        cw = cw[:, ::-1]
    dn_conv = np.ascontiguousarray(cw.reshape(L, 5, 12, 128).transpose(0, 3, 2, 1)).astype(np.float32)
    al = np.asarray(inp['dn_a_log'])[:L]
    db = np.asarray(inp['dn_dt_bias'])[:L]
    if rev:
        al, db = al[:, ::-1], db[:, ::-1]
    par = np.stack([al, db], axis=1)
    dn_par = np.ascontiguousarray(np.broadcast_to(par[:, None], (L, 128, 2, 2, 4))).astype(np.float32)
    dn_gain = np.stack([rep128(np.tile(np.asarray(inp['dn_norm_g'])[l], 4)) for l in range(L)]).astype(np.float32)
    return dict(dn_conv=dn_conv, dn_par=dn_par, dn_gain=dn_gain)


W_PERM_CACHE = {}


def permute_w_in(w_in):
    idx = np.arange(INW)
    idx[C_HG_F1:C_HG_F1 + 512], idx[C_HG_F2:C_HG_F2 + 512] = np.arange(C_HG_F2, C_HG_F2 + 512), np.arange(C_HG_F1, C_HG_F1 + 512)
    for base in (C_DN_AB, C_DN_AB + 8):
        idx[base:base + 4], idx[base + 4:base + 8] = np.arange(base + 4, base + 8), np.arange(base, base + 4)
    return np.ascontiguousarray(w_in[:, :, idx])


def run_model(seqs, T, depth, inp, n_cores, mixers=('swa', 'hg', 'na', 'dn'), trace=False):
    L = depth
    roles = []
    for si, (x, c) in enumerate(seqs):
        Ls = x.shape[0]
        if Ls == 2 * T:
            if len(roles) % 2:
                roles.append(None)
            t0 = np.arange(T)
            t1 = 2 * T - 1 - np.arange(T)
            roles.append(dict(si=si, tok=t0, ptok=t1, rev=False, sel=(0.0, 1.0), Ls=Ls))
            roles.append(dict(si=si, tok=t1, ptok=t0, rev=True, sel=(1.0, 0.0), Ls=Ls))
        else:
            assert Ls == T
            roles.append(dict(si=si, tok=np.arange(T), ptok=None, rev=False, sel=(0.0, 0.0), Ls=Ls))
    assert len(roles) <= n_cores
    real = [r for r in roles if r is not None]
    k = 0
    while len(roles) < n_cores:
        roles.append(dict(real[k % len(real)], dup=True) if True else None)
        k += 1
    roles = [r if r is not None else dict(real[0], dup=True) for r in roles]
    import os
    b = Builder(T, depth=L, mixers=mixers, couple=(os.environ.get('COUPLE', '1') == '1'), n_cores=n_cores)
    nc = b.build()
    w_in = np.asarray(inp['w_in'])[:L]
    w_in_rev = permute_w_in(w_in) if any(r['rev'] for r in roles) else None
    shared = dict(ada_w=np.asarray(inp['ada_w'])[:L], ada_b=arrange_vec(np.asarray(inp['ada_b'])[:L]),
                  ng=np.stack([arrange_vec(np.asarray(inp['norm1_g'])[:L]), arrange_vec(np.asarray(inp['norm2_g'])[:L])], axis=1),
                  w_out=np.asarray(inp['w_out'])[:L], w1=np.asarray(inp['w_mlp_in'])[:L], w2=np.asarray(inp['w_mlp_out'])[:L],
                  consts=make_consts(),
                  gains=np.stack([np.stack([rep128(np.tile(np.asarray(inp[kk])[l], 4)) for kk in ('swa_q_norm', 'swa_k_norm', 'na_q_norm', 'na_k_norm')])
                                  for l in range(L)]).astype(np.float32))
    in_maps = []
    cache = {}
    for r in roles:
        key = (r['si'], r['rev'])
        if key in cache:
            in_maps.append(cache[key])
            continue
        x, c = seqs[r['si']]
        m = dict(shared)
        m['xT'] = np.ascontiguousarray(np.asarray(x)[r['tok']].T)
        m['cvec'] = arrange_vec(np.asarray(c))
        m['w_in'] = w_in_rev if r['rev'] else w_in
        m['rope'] = rope_tables(r['tok'])
        m['swa_mask'] = swa_masks(r['ptok'] is not None)
        m['na_bias'] = np.stack([na_bias_mats(np.asarray(inp['na_rpb'])[l], r['tok'], r['ptok'], r['Ls'], T) for l in range(L)])
        m.update(extra_inputs(inp, L, T, r['tok'], r['rev'], r['sel']))
        m.update(dn_inputs(inp, L, r['rev']))
        cache[key] = m
        in_maps.append(m)
    res = run_bass_kernel_spmd(nc, in_maps, core_ids=list(range(n_cores)), **({'trace': True} if trace else {}))
    outs = [np.zeros((x.shape[0], D), np.float32) for (x, c) in seqs]
    for ci, r in enumerate(roles):
        if r.get('dup'):
            continue
        y = np.asarray(res.results[ci]['yT']).T
        outs[r['si']][r['tok']] = y
    return outs, res


def kernel(x_prompt, x_sample, c_prompt, c_sample, **w):
    x_prompt, x_sample = np.asarray(x_prompt), np.asarray(x_sample)
    c_prompt, c_sample = np.asarray(c_prompt), np.asarray(c_sample)
    T = 8192
    seqs = [(x_sample[0], c_sample[0]), (x_sample[1], c_sample[1]), (x_prompt[0], c_prompt[0]), (x_prompt[1], c_prompt[1])]
    outs, _ = run_model(seqs, T, DEPTH, w, 8)
    y_sample = np.stack([outs[0], outs[1]]).astype(np.float32)
    y_prompt = np.stack([outs[2], outs[3]]).astype(np.float32)
    return (y_prompt, y_sample)
```

```python
from contextlib import ExitStack
import numpy as np
import ml_dtypes
import concourse.bass as bass
import concourse.mybir as mybir
from concourse.bass_utils import run_bass_kernel_spmd

F32 = mybir.dt.float32
BF16 = mybir.dt.bfloat16
AF = mybir.ActivationFunctionType
ALU = mybir.AluOpType
AX = mybir.AxisListType

D = 2048
NCH = 16
DFF = 8192
INW = 7696
DEPTH = 4
EPS = 1e-6
NCONST = 24

C_SWA_Q, C_SWA_K, C_SWA_V = 0, 512, 1024
C_HG_Q, C_HG_F1, C_HG_F2, C_HG_I, C_HG_G = 1536, 2048, 2560, 3072, 3584
C_NA_Q, C_NA_K, C_NA_V = 4096, 4608, 5120
C_DN_QKV, C_DN_Z, C_DN_AB = 5632, 7168, 7680

ENG_ATTR = {'pe': 'tensor', 'act': 'scalar', 'dve': 'vector', 'pool': 'gpsimd', 'sp': 'sync'}
SEM_ROT = 30000


class Buf:
    __slots__ = ('name', 'w', 'r')

    def __init__(self, name=''):
        self.name = name
        self.w = None
        self.r = []


class Op:
    __slots__ = ('eng', 'fn', 'deps', 'signal', 'sem', 'val', 'inc', 'dma', 'idx')

    def __init__(self, eng, fn, dma=False):
        self.eng = eng
        self.fn = fn
        self.deps = []
        self.signal = dma
        self.sem = None
        self.val = 0
        self.inc = 16 if dma else 1
        self.dma = dma


class Prog:
    def __init__(self, nc, n_dma_sems=20):
        self.nc = nc
        self.es = ExitStack()
        self.ops = {e: [] for e in ENG_ATTR}
        self.dma_ops = {e: [] for e in ENG_ATTR}
        self.n_dma_sems = n_dma_sems
        self.dma_sems = {}
        self.nops = 0
        self.phase_es = None

    def sem(self, name):
        return self.es.enter_context(self.nc.semaphore(name))

    def sbuf(self, name, shape, dt, perm=False):
        es = self.es if (perm or self.phase_es is None) else self.phase_es
        self.nsb = getattr(self, 'nsb', 0) + 1
        return es.enter_context(self.nc.sbuf_tensor(f"{name}_{self.nsb}", list(shape), dt))

    def psum(self, name, shape, dt):
        return self.es.enter_context(self.nc.psum_tensor(name, list(shape), dt))

    def begin_phase(self):
        self.barrier()
        if self.phase_es is not None:
            self.phase_es.close()
        self.phase_es = ExitStack()

    def add(self, eng, fn, reads=(), writes=(), dma=False, extra_deps=()):
        op = Op(eng, fn, dma)
        op.idx = self.nops
        self.nops += 1
        deps = []
        for b in reads:
            if b.w is not None:
                deps.append(b.w)
        for b in writes:
            if b.w is not None:
                deps.append(b.w)
            deps.extend(b.r)
        deps.extend(extra_deps)
        for b in reads:
            if not dma:
                b.r = [q for q in b.r if q.dma or q.eng != eng]
            b.r.append(op)
        for b in writes:
            b.w = op
            b.r = []
        seen = set()
        for p in deps:
            if p is op or id(p) in seen:
                continue
            seen.add(id(p))
            if (not p.dma) and (not dma) and p.eng == eng and eng == 'pe':
                continue
            op.deps.append(p)
            p.signal = True
        if dma:
            lst = self.dma_ops[eng]
            k = len(lst)
            if eng not in self.dma_sems:
                self.dma_sems[eng] = [self.sem(f"dq_{eng}_{i}") for i in range(self.n_dma_sems)]
            n = self.n_dma_sems
            op.sem = self.dma_sems[eng][k % n]
            op.val = 16 * (k // n + 1)
            if k >= n:
                op.deps.append(lst[k - n])
            lst.append(op)
        self.ops[eng].append(op)
        return op

    def dma(self, out, in_, reads=(), writes=(), eng='sp', **kw):
        return self.add(eng, lambda e: e.dma_start(out=out, in_=in_, **kw), reads, writes, dma=True)

    def _lasts(self):
        lasts = []
        for e, ops in self.ops.items():
            for op in reversed(ops):
                if not op.dma and op.fn is not None:
                    lasts.append(op)
                    break
        for e, lst in self.dma_ops.items():
            last = {}
            for op in lst:
                last[op.sem.num] = op
            lasts.extend(last.values())
        return lasts

    def barrier(self):
        lasts = self._lasts()
        for p in lasts:
            p.signal = True
        for e in ENG_ATTR:
            op = Op(e, None)
            op.deps = list(lasts)
            self.ops[e].append(op)

    def finalize(self):
        nc = self.nc
        self.barrier()
        for e, ops in self.ops.items():
            cnt = 0
            cur = None
            k = 0
            for op in ops:
                if op.dma or not op.signal:
                    continue
                if cur is None or cnt >= SEM_ROT:
                    cur = self.sem(f"cs_{e}_{k}")
                    k += 1
                    cnt = 0
                cnt += 1
                op.sem = cur
                op.val = cnt
        self.stats = {e: (sum(1 for o in ops if o.signal and not o.dma), len(ops), len(self.dma_ops[e])) for e, ops in self.ops.items()}
        with nc.Block() as block:
            for e, attr in ENG_ATTR.items():
                ops = self.ops[e]
                if not ops:
                    continue

                def body(eng, ops=ops):
                    waited = {}
                    for op in ops:
                        for p in op.deps:
                            s, v = p.sem, p.val
                            if waited.get(s.num, 0) >= v:
                                continue
                            eng.wait_ge(s, v)
                            waited[s.num] = v
                        if op.fn is None:
                            continue
                        ins = op.fn(eng)
                        if op.signal:
                            ins.then_inc(op.sem, op.inc)

                getattr(block, attr)(body)
        if self.phase_es is not None:
            self.phase_es.close()
        self.es.close()


def sl(start, n, step):
    return slice(start, start + (n - 1) * step + 1, step)


class Pool:
    def __init__(self, items):
        self.items = items
        self.i = 0

    def next(self):
        it = self.items[self.i % len(self.items)]
        self.i += 1
        return it


class Builder:
    def __init__(self, T, depth=DEPTH, debug=None, mixers=('swa', 'hg', 'na', 'dn'), couple=True, n_cores=8):
        self.T = T
        self.depth = depth
        self.debug = debug or ()
        self.mixers = mixers
        self.couple = couple
        nc = bass.Bass("TRN2", target_bir_lowering=False)
        nc.allow_low_precision("bf16 matmul operands, fp32 accumulation (reference tolerance is bf16-matmul)")
        self.nc = nc
        self.P = Prog(nc)
        L = depth
        dt = nc.dram_tensor
        self.xT_in = dt("xT", [D, T], F32, kind="ExternalInput").ap()
        self.cvec = dt("cvec", [128, NCH], F32, kind="ExternalInput").ap()
        self.ada_w = dt("ada_w", [L, D, 6 * D], F32, kind="ExternalInput").ap()
        self.ada_b = dt("ada_b", [L, 128, 96], F32, kind="ExternalInput").ap()
        self.ng = dt("ng", [L, 2, 128, NCH], F32, kind="ExternalInput").ap()
        self.w_in = dt("w_in", [L, D, INW], F32, kind="ExternalInput").ap()
        self.w_out = dt("w_out", [L, D, D], F32, kind="ExternalInput").ap()
        self.w1 = dt("w1", [L, D, DFF], F32, kind="ExternalInput").ap()
        self.w2 = dt("w2", [L, DFF, D], F32, kind="ExternalInput").ap()
        self.consts_in = dt("consts", [128, NCONST, 128], F32, kind="ExternalInput").ap()
        self.yT_out = dt("yT", [D, T], F32, kind="ExternalOutput").ap()
        self.xs = dt("xs", [D, T], F32).ap()
        self.w_in_b = dt("w_in_b", [L, D, INW], BF16).ap()
        self.w_out_b = dt("w_out_b", [L, D, D], BF16).ap()
        self.w1_b = dt("w1_b", [L, D, DFF], BF16).ap()
        self.w2_b = dt("w2_b", [L, NCH, 128, 64, 128], BF16).ap()
        self.Utok = dt("Utok", [T, INW], F32).ap()
        self.UT = dt("UT", [INW, T], F32).ap()
        self.yT = dt("yTs", [D, T], BF16).ap()
        self.dbufs = {}
        NT = T // 128
        self.NT = NT
        self.gains_in = dt("gains", [L, 4, 128, 512], F32, kind="ExternalInput").ap()
        self.rope_in = dt("rope", [128, NT, 2, 16], F32, kind="ExternalInput").ap()
        self.swa_mask_in = dt("swa_mask", [128, 4, 128], F32, kind="ExternalInput").ap()
        self.na_bias_in = dt("na_bias", [L, 4, 128, 24, 128], F32, kind="ExternalInput").ap()
        self.swa_qT = dt("swa_qT", [4, 128, T], BF16).ap()
        self.swa_kT = dt("swa_kT", [4, 128, T], BF16).ap()
        self.swa_V = dt("swa_V", [T, 512], BF16).ap()
        self.na_qT = dt("na_qT", [4, 128, T], BF16).ap()
        self.na_kT = dt("na_kT", [4, 128, T], BF16).ap()
        self.na_V = dt("na_V", [T, 512], BF16).ap()
        self.hg_lb_in = dt("hg_lb", [128, L, 2, 512], F32, kind="ExternalInput").ap()
        self.hg_lbT_in = dt("hg_lbT", [128, L, 2, 4], F32, kind="ExternalInput").ap()
        self.hg_gain_in = dt("hg_gain", [L, 128, 512], F32, kind="ExternalInput").ap()
        self.flag_in = dt("flag", [128, 2], F32, kind="ExternalInput").ap()
        self.lb_d = dt("lb_d", [L, 2, 2, 128, 512], F32).ap()
        self.lbT_d = dt("lbT_d", [L, 128, 2, 2, 4], F32).ap()
        self.o1_d = dt("o1_d", [T, 512], F32).ap()
        self.hg_state_d = dt("hg_state_d", [128, 4, 128], F32).ap()
        self.dn_conv_in = dt("dn_conv", [L, 128, 12, 5], F32, kind="ExternalInput").ap()
        self.dn_par_in = dt("dn_par", [L, 128, 2, 2, 4], F32, kind="ExternalInput").ap()
        self.dn_gain_in = dt("dn_gain", [L, 128, 512], F32, kind="ExternalInput").ap()
        self.dnc_d = dt("dnc_d", [T, 1536], BF16).ap()
        self.o1dn_d = dt("o1dn_d", [T, 512], F32).ap()
        self.dn_state_d = dt("dn_state_d", [128, 4, 128], F32).ap()
        self.h_dn_x = dt("h_dn_x", [1536, 2], F32).ap()
        self.HS = min(1024, T)
        self.XW = 8 * self.HS + 2048
        self.XWs = [4 * self.HS, 4 * self.HS, 2048]
        self.pubA = [dt(f"pubA{i}", [128, w], BF16).ap() for i, w in enumerate(self.XWs)]
        self.gathA = [dt(f"gathA{i}", [256, w], BF16).ap() for i, w in enumerate(self.XWs)]
        self.pubB = dt("pubB", [128, 24], F32).ap()
        self.gathB = dt("gathB", [256, 24], F32).ap()
        self.pubS = dt("pubS", [128, 512], F32).ap()
        self.gathS = dt("gathS", [256, 512], F32).ap()
        self.RG = [[2 * i, 2 * i + 1] for i in range(n_cores // 2)]
        self.h_swa_kT = dt("h_swa_kT", [4, 128, self.HS], BF16).ap()
        self.h_swa_V = dt("h_swa_V", [self.HS, 512], BF16).ap()
        self.h_na_kT = dt("h_na_kT", [4, 128, 256], BF16).ap()
        self.h_na_V = dt("h_na_V", [256, 512], BF16).ap()
        if 'UT' in self.debug:
            self.dbg_UT = dt("dbg_UT", [INW, T], F32, kind="ExternalOutput").ap()
            self.dbg_Utok = dt("dbg_Utok", [T, INW], F32, kind="ExternalOutput").ap()

    def dbuf(self, key):
        if key not in self.dbufs:
            self.dbufs[key] = Buf(str(key))
        return self.dbufs[key]

    def setup(self):
        P = self.P
        nc = self.nc
        self.psb = [P.psum(f"ps{i}", [128, 512], F32) for i in range(8)]
        self.psbuf = [Buf(f"ps{i}") for i in range(8)]
        self.ps_main = Pool([(self.psb[i], self.psbuf[i]) for i in range(0, 6)])
        self.ps_aux = Pool([(self.psb[i], self.psbuf[i]) for i in range(6, 8)])
        self.cf = P.sbuf("cf", [128, NCONST, 128], F32, perm=True)
        self.cb = P.sbuf("cb", [128, NCONST, 128], BF16, perm=True)
        self.b_cf = Buf("cf")
        self.b_cb = Buf("cb")
        P.dma(self.cf[:], self.consts_in, writes=[self.b_cf])
        P.add('dve', lambda e: e.tensor_copy(self.cb[:], self.cf[:]), reads=[self.b_cf], writes=[self.b_cb])
        self.ident_f = self.cf[:, 0, :]
        self.ident_b = self.cb[:, 0, :]
        self.ones_b = self.cb[:, 1, :]
        self.ones_f = self.cf[:, 1, :]
        L = self.depth
        self.modT = P.sbuf("modT", [128, L, 96], F32, perm=True)
        self.amod = P.sbuf("amod", [128, L, 2, NCH], F32, perm=True)
        self.b_mod = Buf("mod")

    def compute_mod(self):
        P = self.P
        L = self.depth
        P.begin_phase()
        cv = P.sbuf("cv", [128, NCH], F32)
        sc = P.sbuf("sc", [128, NCH], F32)
        adb = P.sbuf("adb", [128, L, 96], F32)
        ngt = P.sbuf("ngt", [128, L, 2, NCH], F32)
        b_cv, b_sc, b_adb, b_ng = Buf(), Buf(), Buf(), Buf()
        P.dma(cv[:], self.cvec, writes=[b_cv])
        P.dma(adb[:], self.ada_b.rearrange("l p j -> p l j"), writes=[b_adb])
        P.dma(ngt[:], self.ng.rearrange("l a p c -> p l a c"), writes=[b_ng])
        P.add('act', lambda e: e.activation(out=sc[:], in_=cv[:], func=AF.Silu), reads=[b_cv], writes=[b_sc])
        wts = [(P.sbuf(f"adw{i}", [128, NCH, 512], F32), Buf()) for i in range(2)]
        wp = Pool(wts)
        for l in range(L):
            ps, pb = self.ps_main.next()
            for blk in range(24):
                wt, wb = wp.next()
                src = self.ada_w[l, :, blk * 512:(blk + 1) * 512].rearrange("(c p) n -> p c n", p=128)
                P.dma(wt[:], src, writes=[wb])
                for j in range(4):
                    col = blk * 4 + j
                    for c in range(NCH):
                        P.add('pe', lambda e, wt=wt, c=c, j=j, col=col, ps=ps: e.matmul(
                            ps[:, col:col + 1], wt[:, c, j * 128:(j + 1) * 128], sc[:, c:c + 1],
                            start=(c == 0), stop=(c == NCH - 1)),
                            reads=[wb, b_sc], writes=[pb])
            P.add('dve', lambda e, l=l, ps=ps: e.tensor_tensor(out=self.modT[:, l, :], in0=ps[:, 0:96], in1=adb[:, l, :], op=ALU.add),
                  reads=[pb, b_adb], writes=[self.b_mod])
            for k, so in ((0, 16), (1, 64)):
                P.add('dve', lambda e, l=l, k=k, so=so: e.scalar_tensor_tensor(
                    out=self.amod[:, l, k, :], in0=self.modT[:, l, so:so + 16], scalar=1.0, in1=ngt[:, l, k, :],
                    op0=ALU.add, op1=ALU.mult), reads=[self.b_mod, b_ng], writes=[self.b_mod])

    def mod(self, l, which, c):
        if which == 'a1':
            return self.amod[:, l, 0, c:c + 1]
        if which == 'a2':
            return self.amod[:, l, 1, c:c + 1]
        off = {'b1': 0, 'g1': 32, 'b2': 48, 'g2': 80}[which]
        return self.modT[:, l, off + c:off + c + 1]

    def convert_weights(self):
        P = self.P
        P.begin_phase()
        L = self.depth
        st = [(P.sbuf(f"wcf{i}", [128, 8192], F32), Buf()) for i in range(2)]
        sb = [(P.sbuf(f"wcb{i}", [128, 8192], BF16), Buf()) for i in range(2)]
        fp, bp = Pool(st), Pool(sb)
        k = 0
        engs = ['dve', 'act', 'pool']
        for l in range(L):
            jobs = []
            for c in range(NCH):
                jobs.append((self.w_in[l, c * 128:(c + 1) * 128, :], self.w_in_b[l, c * 128:(c + 1) * 128, :], INW, None))
            for c in range(NCH):
                jobs.append((self.w_out[l, c * 128:(c + 1) * 128, :], self.w_out_b[l, c * 128:(c + 1) * 128, :], D, None))
            for c in range(NCH):
                jobs.append((self.w1[l, c * 128:(c + 1) * 128, :], self.w1_b[l, c * 128:(c + 1) * 128, :], DFF, None))
            for f in range(0, 64, 4):
                jobs.append((self.w2[l, f * 128:(f + 4) * 128, :], None, 4 * D, f))
            for (src, dst, n, f) in jobs:
                ft, fb = fp.next()
                bt, bb = bp.next()
                if f is None:
                    P.dma(ft[:, 0:n], src, writes=[fb])
                else:
                    P.dma(ft[:, 0:n].rearrange("p (a d) -> p a d", a=4), src.rearrange("(a p) d -> p a d", p=128), writes=[fb])
                eng = engs[k % 2]
                k += 1
                if eng == 'act':
                    P.add('act', lambda e, ft=ft, bt=bt, n=n: e.copy(bt[:, 0:n], ft[:, 0:n]), reads=[fb], writes=[bb])
                else:
                    P.add(eng, lambda e, ft=ft, bt=bt, n=n: e.tensor_copy(bt[:, 0:n], ft[:, 0:n]), reads=[fb], writes=[bb])
                if f is None:
                    P.dma(dst, bt[:, 0:n], reads=[bb], writes=[self.dbuf(('w', l))])
                else:
                    for a in range(4):
                        dstv = self.w2_b[l, :, :, f + a, :].rearrange("j p d -> p j d")
                        P.dma(dstv, bt[:, a * D:(a + 1) * D].rearrange("p (j d) -> p j d", j=NCH), reads=[bb], writes=[self.dbuf(('w', l))])

    def norm_mod(self, l, which, xg, b_xg, hT, b_hT, G, sq, b_sq, rstd, b_rstd, tmp, b_tmp):
        P = self.P
        a_key, b_key = ('a1', 'b1') if which == 1 else ('a2', 'b2')
        for c in range(NCH):
            eng = 'act' if c % 2 == 0 else 'pool'
            if eng == 'act':
                P.add('act', lambda e, c=c: e.activation(out=sq[:, c, :], in_=xg[:, c, :], func=AF.Square),
                      reads=[b_xg], writes=[b_sq[c]])
            else:
                P.add('pool', lambda e, c=c: e.tensor_tensor(out=sq[:, c, :], in0=xg[:, c, :], in1=xg[:, c, :], op=ALU.mult),
                      reads=[b_xg], writes=[b_sq[c]])
        for h in range(G // 512):
            ps, pb = self.ps_aux.next()
            for c in range(NCH):
                P.add('pe', lambda e, c=c, h=h, ps=ps: e.matmul(ps[:, :], self.ones_b, sq[:, c, h * 512:(h + 1) * 512],
                                                               start=(c == 0), stop=(c == NCH - 1)),
                      reads=[b_sq[c], self.b_cb], writes=[pb])
            P.add('dve', lambda e, h=h, ps=ps: e.tensor_scalar(out=rstd[:, h * 512:(h + 1) * 512], in0=ps[:, :], scalar1=1.0 / D, scalar2=EPS,
                                                              op0=ALU.mult, op1=ALU.add), reads=[pb], writes=[b_rstd])
        P.add('act', lambda e: e.activation(out=rstd[:, :], in_=rstd[:, :], func=AF.Sqrt), reads=[b_rstd], writes=[b_rstd])
        P.add('dve', lambda e: e.reciprocal(rstd[:, :], rstd[:, :]), reads=[b_rstd], writes=[b_rstd])
        for c in range(NCH):
            t, tb = tmp[c % len(tmp)], b_tmp[c % len(tmp)]
            P.add('dve', lambda e, c=c, t=t: e.tensor_tensor(out=t[:, :], in0=xg[:, c, :], in1=rstd[:, :], op=ALU.mult),
                  reads=[b_xg, b_rstd], writes=[tb])
            P.add('act', lambda e, c=c, t=t: e.activation(out=hT[:, c, :], in_=t[:, :], func=AF.Identity,
                                                          scale=self.mod(l, a_key, c), bias=self.mod(l, b_key, c)),
                  reads=[tb, self.b_mod], writes=[b_hT[c]])

    def tok_blocks(self):
        blocks = [(C_SWA_Q, 512), (C_SWA_K, 512), (C_SWA_V, 512), (C_HG_F1, 512), (C_HG_F2, 512), (C_HG_I, 512),
                  (C_HG_G, 512), (C_NA_Q, 512), (C_NA_K, 512), (C_NA_V, 512), (C_DN_Z, 512), (C_DN_AB, 16)]
        return blocks

    def fm_blocks(self):
        return [(C_HG_Q, 512), (C_HG_F1, 512), (C_HG_F2, 512), (C_DN_QKV, 512), (C_DN_QKV + 512, 512), (C_DN_QKV + 1024, 512)]

    def phase_A(self, l, x_src):
        P = self.P
        T = self.T
        G = 1024 if T % 1024 == 0 else 512
        P.begin_phase()
        xg = P.sbuf("A_xg", [128, NCH, G], F32)
        b_xg = Buf()
        hT = P.sbuf("A_hT", [128, NCH, G], BF16)
        b_hT = [Buf() for _ in range(NCH)]
        sq = P.sbuf("A_sq", [128, NCH, G], BF16)
        b_sq = [Buf() for _ in range(NCH)]
        rstd = P.sbuf("A_rstd", [128, G], F32)
        b_rstd = Buf()
        tmp = [P.sbuf(f"A_tmp{i}", [128, G], F32) for i in range(2)]
        b_tmp = [Buf() for _ in range(2)]
        wts = Pool([(P.sbuf(f"A_w{i}", [128, NCH, 512], BF16), Buf()) for i in range(2)])
        stg = Pool([(P.sbuf(f"A_st{i}", [128, 512], F32), Buf()) for i in range(4)])
        xv = x_src.rearrange("(c p) t -> p c t", p=128)
        evk = 0
        for g in range(T // G):
            t0 = g * G
            P.dma(xg[:], xv[:, :, t0:t0 + G], reads=[self.dbuf(('x', g * G // 512)), self.dbuf(('x', (g * G + G - 1) // 512))], writes=[b_xg])
            self.norm_mod(l, 1, xg, b_xg, hT, b_hT, G, sq, b_sq, rstd, b_rstd, tmp, b_tmp)
            for (c0, ncol) in self.fm_blocks():
                wt, wb = wts.next()
                P.dma(wt[:, :, 0:ncol], self.w_in_b[l, :, c0:c0 + ncol].rearrange("(c p) n -> p c n", p=128),
                      reads=[self.dbuf(('w', l))], writes=[wb])
                for j in range(ncol // 128):
                    for h in range(G // 512):
                        ps, pb = self.ps_main.next()
                        for c in range(NCH):
                            P.add('pe', lambda e, wt=wt, c=c, j=j, h=h, ps=ps: e.matmul(
                                ps[:, :], wt[:, c, j * 128:(j + 1) * 128], hT[:, c, h * 512:(h + 1) * 512],
                                start=(c == 0), stop=(c == NCH - 1)), reads=[wb, b_hT[c]], writes=[pb])
                        st, sb_ = stg.next()
                        evk += 1
                        if evk % 2:
                            P.add('act', lambda e, st=st, ps=ps: e.copy(st[:, :], ps[:, :]), reads=[pb], writes=[sb_])
                        else:
                            P.add('dve', lambda e, st=st, ps=ps: e.tensor_copy(st[:, :], ps[:, :]), reads=[pb], writes=[sb_])
                        r0 = c0 + j * 128
                        P.dma(self.UT[r0:r0 + 128, t0 + h * 512:t0 + (h + 1) * 512], st[:, :], reads=[sb_],
                              writes=[self.dbuf(('UT', l))], eng='pool')
            for (c0, ncol) in self.tok_blocks():
                wt, wb = wts.next()
                P.dma(wt[:, :, 0:ncol], self.w_in_b[l, :, c0:c0 + ncol].rearrange("(c p) n -> p c n", p=128),
                      reads=[self.dbuf(('w', l))], writes=[wb])
                for tt in range(G // 128):
                    ps, pb = self.ps_main.next()
                    for c in range(NCH):
                        P.add('pe', lambda e, wt=wt, c=c, tt=tt, ps=ps, ncol=ncol: e.matmul(
                            ps[:, 0:ncol], hT[:, c, tt * 128:(tt + 1) * 128], wt[:, c, 0:ncol],
                            start=(c == 0), stop=(c == NCH - 1)), reads=[wb, b_hT[c]], writes=[pb])
                    st, sb_ = stg.next()
                    evk += 1
                    if evk % 2:
                        P.add('act', lambda e, st=st, ps=ps, ncol=ncol: e.copy(st[:, 0:ncol], ps[:, 0:ncol]), reads=[pb], writes=[sb_])
                    else:
                        P.add('dve', lambda e, st=st, ps=ps, ncol=ncol: e.tensor_copy(st[:, 0:ncol], ps[:, 0:ncol]), reads=[pb], writes=[sb_])
                    P.dma(self.Utok[t0 + tt * 128:t0 + (tt + 1) * 128, c0:c0 + ncol], st[:, 0:ncol], reads=[sb_],
                          writes=[self.dbuf(('Utok', l))], eng='pool')

    def phase_C(self, l, x_src, x_dst):
        P = self.P
        T = self.T
        G = 512
        P.begin_phase()
        xg = P.sbuf("C_xg", [128, NCH, G], F32)
        b_xg = Buf()
        hT = P.sbuf("C_hT", [128, NCH, G], BF16)
        b_hT = [Buf() for _ in range(NCH)]
        yg = hT
        rstd = P.sbuf("C_rstd", [128, G], F32)
        b_rstd = Buf()
        tmp = [P.sbuf(f"C_tmp{i}", [128, G], F32) for i in range(2)]
        b_tmp = [Buf() for _ in range(2)]
        hid = P.sbuf("C_hid", [128, 64, G], BF16)
        b_hid = [Buf() for _ in range(64)]
        sq = hid
        b_sq = b_hid[0:NCH]
        rl = Pool([(P.sbuf(f"C_rl{i}", [128, G], BF16), Buf()) for i in range(3)])
        wts = Pool([(P.sbuf(f"C_w{i}", [128, NCH, 512], BF16), Buf()) for i in range(2)])
        w2s = Pool([(P.sbuf(f"C_w2{i}", [128, 64, 128], BF16), Buf()) for i in range(2)])
        xv = x_src.rearrange("(c p) t -> p c t", p=128)
        xo = x_dst.rearrange("(c p) t -> p c t", p=128)
        yv = self.yT.rearrange("(c p) t -> p c t", p=128)
        for g in range(T // G):
            t0 = g * G
            P.dma(xg[:], xv[:, :, t0:t0 + G], reads=[self.dbuf(('x', g))], writes=[b_xg])
            P.dma(yg[:], yv[:, :, t0:t0 + G], reads=[self.dbuf(('yT', l))], writes=b_hT)
            for blk in range(4):
                wt, wb = wts.next()
                P.dma(wt[:], self.w_out_b[l, :, blk * 512:(blk + 1) * 512].rearrange("(c p) n -> p c n", p=128),
                      reads=[self.dbuf(('w', l))], writes=[wb])
                for j4 in range(4):
                    j = blk * 4 + j4
                    ps, pb = self.ps_main.next()
                    for c in range(NCH):
                        P.add('pe', lambda e, wt=wt, c=c, j4=j4, ps=ps: e.matmul(
                            ps[:, :], wt[:, c, j4 * 128:(j4 + 1) * 128], yg[:, c, :], start=(c == 0), stop=(c == NCH - 1)),
                            reads=[wb, b_hT[c]], writes=[pb])
                    P.add('dve', lambda e, j=j, ps=ps: e.scalar_tensor_tensor(
                        out=xg[:, j, :], in0=ps[:, :], scalar=self.mod(l, 'g1', j), in1=xg[:, j, :], op0=ALU.mult, op1=ALU.add),
                        reads=[pb, self.b_mod, b_xg], writes=[b_xg])
            self.norm_mod(l, 2, xg, b_xg, hT, b_hT, G, sq, b_sq, rstd, b_rstd, tmp, b_tmp)
            for blk in range(16):
                wt, wb = wts.next()
                P.dma(wt[:], self.w1_b[l, :, blk * 512:(blk + 1) * 512].rearrange("(c p) n -> p c n", p=128),
                      reads=[self.dbuf(('w', l))], writes=[wb])
                for j4 in range(4):
                    f = blk * 4 + j4
                    ps, pb = self.ps_main.next()
                    for c in range(NCH):
                        P.add('pe', lambda e, wt=wt, c=c, j4=j4, ps=ps: e.matmul(
                            ps[:, :], wt[:, c, j4 * 128:(j4 + 1) * 128], hT[:, c, :], start=(c == 0), stop=(c == NCH - 1)),
                            reads=[wb, b_hT[c]], writes=[pb])
                    r, rb = rl.next()
                    P.add('act', lambda e, r=r, ps=ps: e.activation(out=r[:, :], in_=ps[:, :], func=AF.Relu), reads=[pb], writes=[rb])
                    eng = 'pool' if f % 3 == 0 else 'dve'
                    P.add(eng, lambda e, r=r, f=f: e.tensor_tensor(out=hid[:, f, :], in0=r[:, :], in1=r[:, :], op=ALU.mult),
                          reads=[rb], writes=[b_hid[f]])
            for j in range(NCH):
                wt, wb = w2s.next()
                P.dma(wt[:], self.w2_b[l, j], reads=[self.dbuf(('w', l))], writes=[wb])
                ps, pb = self.ps_main.next()
                for f in range(64):
                    P.add('pe', lambda e, wt=wt, f=f, ps=ps: e.matmul(ps[:, :], wt[:, f, :], hid[:, f, :], start=(f == 0), stop=(f == 63)),
                          reads=[wb, b_hid[f]], writes=[pb])
                P.add('dve', lambda e, j=j, ps=ps: e.scalar_tensor_tensor(
                    out=xg[:, j, :], in0=ps[:, :], scalar=self.mod(l, 'g2', j), in1=xg[:, j, :], op0=ALU.mult, op1=ALU.add),
                    reads=[pb, self.b_mod, b_xg], writes=[b_xg])
            P.dma(xo[:, :, t0:t0 + G], xg[:], reads=[b_xg], writes=[self.dbuf(('x', g))])


    def zero_halos(self):
        P = self.P
        P.begin_phase()
        z = P.sbuf("zt", [128, 4096], BF16)
        bz = Buf()
        P.add('pool', lambda e: e.memset(z[:], 0.0), writes=[bz])
        HS = self.HS
        P.dma(self.h_swa_kT.rearrange("h p t -> p h t"), z[:, 0:4 * HS].rearrange("p (h t) -> p h t", h=4), reads=[bz], writes=[self.dbuf('halo')])
        P.dma(self.h_swa_V.rearrange("(a p) n -> p a n", p=128), z[:, 0:(HS // 128) * 512].rearrange("p (a n) -> p a n", n=512), reads=[bz], writes=[self.dbuf('halo')])
        P.dma(self.h_na_kT.rearrange("h p t -> p h t"), z[:, 0:1024].rearrange("p (h t) -> p h t", h=4), reads=[bz], writes=[self.dbuf('halo')])
        P.dma(self.h_na_V.rearrange("(a p) n -> p a n", p=128), z[:, 0:1024].rearrange("p (a n) -> p a n", n=512), reads=[bz], writes=[self.dbuf('halo')])
        zf = P.sbuf("ztf", [128, 12, 2], F32)
        bzf = Buf()
        P.add('pool', lambda e: e.memset(zf[:], 0.0), writes=[bzf])
        P.dma(self.h_dn_x.rearrange("(c p) t -> p c t", p=128), zf[:], reads=[bzf], writes=[self.dbuf('halo')])

    def phase_B1(self, l):
        P = self.P
        T, NT = self.T, self.NT
        P.begin_phase()
        gains = P.sbuf("B_gains", [128, 4, 512], F32)
        b_g = Buf()
        rope = P.sbuf("B_rope", [128, NT, 2, 16], F32)
        b_rope = Buf()
        P.dma(gains[:], self.gains_in[l].rearrange("a p n -> p a n"), writes=[b_g])
        P.dma(rope[:], self.rope_in, writes=[b_rope])
        for a in (0, 2):
            P.add('dve', lambda e, a=a: e.tensor_scalar(out=gains[:, a, :], in0=gains[:, a, :], scalar1=128.0 ** -0.5, scalar2=None, op0=ALU.mult),
                  reads=[b_g], writes=[b_g])
        upool = Pool([(P.sbuf(f"B_u{i}", [128, 3072], F32), Buf()) for i in range(2)])
        sq = P.sbuf("B_sq", [128, 4, 512], F32)
        b_sq = [Buf() for _ in range(4)]
        ss = P.sbuf("B_ss", [128, 16], F32)
        b_ss = Buf()
        tq = [P.sbuf(f"B_t{i}", [128, 512], F32) for i in range(2)]
        b_tq = [Buf() for _ in range(2)]
        t2 = [P.sbuf(f"B_t2{i}", [128, 512], F32) for i in range(2)]
        b_t2 = [Buf() for _ in range(2)]
        rt = P.sbuf("B_rt", [128, 4, 4, 16], F32)
        b_rt = [Buf() for _ in range(4)]
        obp = Pool([(P.sbuf(f"B_ob{i}", [128, 512], BF16), Buf()) for i in range(3)])
        stp = Pool([(P.sbuf(f"B_st{i}", [128, 4, 128], BF16), Buf()) for i in range(3)])
        vbp = Pool([(P.sbuf(f"B_vb{i}", [128, 512], BF16), Buf()) for i in range(3)])
        pieces = [(0, 0, True, self.swa_qT), (1, 512, True, self.swa_kT), (2, 1536, False, self.na_qT), (3, 2048, False, self.na_kT)]
        for tt in range(NT):
            r0 = tt * 128
            u, ub = upool.next()
            P.dma(u[:, 0:1536], self.Utok[r0:r0 + 128, 0:1536], reads=[self.dbuf(('Utok', l))], writes=[ub])
            P.dma(u[:, 1536:3072], self.Utok[r0:r0 + 128, 4096:5632], reads=[self.dbuf(('Utok', l))], writes=[ub])
            for (a, off, _, _) in pieces:
                eng = 'pool' if a % 2 else 'dve'
                P.add(eng, lambda e, a=a, off=off, u=u: e.tensor_tensor(out=sq[:, a, :], in0=u[:, off:off + 512], in1=u[:, off:off + 512], op=ALU.mult),
                      reads=[ub], writes=[b_sq[a]])
                P.add('dve', lambda e, a=a: e.tensor_reduce(out=ss[:, 4 * a:4 * a + 4], in_=sq[:, a, :].rearrange("p (h d) -> p h d", h=4),
                                                            axis=AX.X, op=ALU.add), reads=[b_sq[a]], writes=[b_ss])
            P.add('dve', lambda e: e.tensor_scalar(out=ss[:, :], in0=ss[:, :], scalar1=1.0 / 128, scalar2=EPS, op0=ALU.mult, op1=ALU.add),
                  reads=[b_ss], writes=[b_ss])
            P.add('act', lambda e: e.activation(out=ss[:, :], in_=ss[:, :], func=AF.Sqrt), reads=[b_ss], writes=[b_ss])
            P.add('dve', lambda e: e.reciprocal(ss[:, :], ss[:, :]), reads=[b_ss], writes=[b_ss])
            for (a, off, is_swa, dst) in pieces:
                k = a % 2
                ob, obb = obp.next()
                P.add('dve', lambda e, a=a, off=off, u=u, k=k: e.tensor_tensor(
                    out=tq[k][:, :].rearrange("p (h d) -> p h d", h=4), in0=u[:, off:off + 512].rearrange("p (h d) -> p h d", h=4),
                    in1=ss[:, 4 * a:4 * a + 4].unsqueeze(2).to_broadcast([128, 4, 128]), op=ALU.mult),
                    reads=[ub, b_ss], writes=[b_tq[k]])
                if not is_swa:
                    P.add('pool', lambda e, a=a, k=k, ob=ob: e.tensor_tensor(out=ob[:, :], in0=tq[k][:, :], in1=gains[:, a, :], op=ALU.mult),
                          reads=[b_tq[k], b_g], writes=[obb])
                else:
                    P.add('pool', lambda e, a=a, k=k: e.tensor_tensor(out=t2[k][:, :], in0=tq[k][:, :], in1=gains[:, a, :], op=ALU.mult),
                          reads=[b_tq[k], b_g], writes=[b_t2[k]])
                    tv = t2[k][:, :].rearrange("p (h d) -> p h d", h=4)
                    obv = ob[:, :].rearrange("p (h d) -> p h d", h=4)
                    cosb = rope[:, tt, 0, :].unsqueeze(1).to_broadcast([128, 4, 16])
                    sinb = rope[:, tt, 1, :].unsqueeze(1).to_broadcast([128, 4, 16])
                    x1, x2 = tv[:, :, 0:16], tv[:, :, 16:32]
                    P.add('dve', lambda e, x1=x1, cosb=cosb: e.tensor_tensor(out=rt[:, 0], in0=x1, in1=cosb, op=ALU.mult), reads=[b_t2[k], b_rope], writes=[b_rt[0]])
                    P.add('pool', lambda e, x2=x2, sinb=sinb: e.tensor_tensor(out=rt[:, 1], in0=x2, in1=sinb, op=ALU.mult), reads=[b_t2[k], b_rope], writes=[b_rt[1]])
                    P.add('dve', lambda e, x2=x2, cosb=cosb: e.tensor_tensor(out=rt[:, 2], in0=x2, in1=cosb, op=ALU.mult), reads=[b_t2[k], b_rope], writes=[b_rt[2]])
                    P.add('pool', lambda e, x1=x1, sinb=sinb: e.tensor_tensor(out=rt[:, 3], in0=x1, in1=sinb, op=ALU.mult), reads=[b_t2[k], b_rope], writes=[b_rt[3]])
                    P.add('act', lambda e, tv=tv, obv=obv: e.copy(obv[:, :, 32:128], tv[:, :, 32:128]), reads=[b_t2[k]], writes=[obb])
                    P.add('dve', lambda e, obv=obv: e.tensor_tensor(out=obv[:, :, 0:16], in0=rt[:, 0], in1=rt[:, 1], op=ALU.subtract),
                          reads=[b_rt[0], b_rt[1]], writes=[obb])
                    P.add('dve', lambda e, obv=obv: e.tensor_tensor(out=obv[:, :, 16:32], in0=rt[:, 2], in1=rt[:, 3], op=ALU.add),
                          reads=[b_rt[2], b_rt[3]], writes=[obb])
                ps, pb = self.ps_main.next()
                psv = ps[:, 0:256].bitcast(BF16)
                for h in range(4):
                    P.add('pe', lambda e, h=h, ob=ob, psv=psv: e.transpose(psv[:, h * 128:(h + 1) * 128], ob[:, h * 128:(h + 1) * 128], self.ident_b),
                          reads=[obb, self.b_cb], writes=[pb])
                st, stb = stp.next()
                P.add('act', lambda e, st=st, psv=psv: e.copy(st[:, :, :].rearrange("p h t -> p (h t)"), psv[:, :]), reads=[pb], writes=[stb])
                P.dma(dst[:, :, r0:r0 + 128].rearrange("h p t -> p h t"), st[:, :, :], reads=[stb], writes=[self.dbuf(('qk', l))])
            for (off, dstv) in ((1024, self.swa_V), (2560, self.na_V)):
                vb, vbb = vbp.next()
                P.add('act', lambda e, vb=vb, u=u, off=off: e.copy(vb[:, :], u[:, off:off + 512]), reads=[ub], writes=[vbb])
                P.dma(dstv[r0:r0 + 128, :], vb[:, :], reads=[vbb], writes=[self.dbuf(('qk', l))])

    def phase_SWA(self, l):
        P = self.P
        T = self.T
        HS = self.HS
        P.begin_phase()
        PADL = HS
        mk = P.sbuf("S_mk", [128, 4, 128], BF16)
        mkf = P.sbuf("S_mkf", [128, 4, 128], F32)
        b_mk = Buf()
        P.dma(mkf[:], self.swa_mask_in, writes=[b_mk])
        P.add('dve', lambda e: e.tensor_copy(mk[:], mkf[:]), reads=[b_mk], writes=[b_mk])
        qT = P.sbuf("S_qT", [128, T], BF16)
        kT = P.sbuf("S_kT", [128, PADL + T], BF16)
        kh = P.sbuf("S_kh", [128, HS], BF16)
        b_q, b_k, b_kh = Buf(), Buf(), Buf()
        accn = P.sbuf("S_accn", [128, T], F32)
        accd = P.sbuf("S_accd", [128, T], F32)
        b_acc = Buf()
        kbl = Pool([(P.sbuf(f"S_kbl{i}", [128, 128], BF16), Buf()) for i in range(2)])
        vt = Pool([(P.sbuf(f"S_vt{i}", [128, 128], BF16), Buf()) for i in range(4)])
        pts = Pool([(P.sbuf(f"S_pt{i}", [128, 2, 128], BF16), Buf()) for i in range(3)])
        yst = Pool([(P.sbuf(f"S_y{i}", [128, 2048 if T >= 2048 else T], BF16), Buf()) for i in range(2)])
        P.add('pool', lambda e: e.memset(kT[:, 0:PADL], 0.0), writes=[b_k])
        psS = Pool([(self.psb[i], self.psbuf[i]) for i in (0, 1, 2)])
        psO = Pool([(self.psb[i], self.psbuf[i]) for i in (3, 4, 5)])
        for h in range(4):
            P.dma(qT[:], self.swa_qT[h], reads=[self.dbuf(('qk', l))], writes=[b_q])
            P.dma(kT[:, PADL:PADL + T], self.swa_kT[h], reads=[self.dbuf(('qk', l))], writes=[b_k])
            P.dma(kh[:], self.h_swa_kT[h], reads=[self.dbuf('halo')], writes=[b_kh])
            first = True
            import os
            for dil in tuple(int(v) for v in os.environ.get('SWA_DILS', '1,4,16').split(',')):
                Tn = T // dil
                nq = Tn // 128
                for r in range(dil):
                    prevV = None
                    for i in range(nq):
                        qv = qT[:, sl(r + dil * 128 * i, 128, dil)]
                        lastq = (i == nq - 1)
                        a0 = PADL + r + dil * (128 * i - 64)
                        kA = kT[:, sl(a0, 128, dil)]
                        mA = mk[:, 0 if i == 0 else 1, :]
                        if not lastq:
                            b0 = PADL + r + dil * (128 * i + 64)
                            kB = kT[:, sl(b0, 128, dil)]
                            kB_reads = [b_k]
                            mB = mk[:, 2, :]
                        else:
                            kbt, kbb = kbl.next()
                            b0 = PADL + r + dil * (128 * i + 64)
                            P.add('pool', lambda e, kbt=kbt, b0=b0, dil=dil: e.tensor_copy(kbt[:, 0:64], kT[:, sl(b0, 64, dil)]), reads=[b_k], writes=[kbb])
                            rr = dil - 1 - r
                            h0 = rr + dil * (HS // dil - 64)
                            P.add('pool', lambda e, kbt=kbt, h0=h0, dil=dil: e.tensor_copy(kbt[:, 64:128], kh[:, sl(h0, 64, dil)]), reads=[b_kh], writes=[kbb])
                            kB = kbt[:, :]
                            kB_reads = [kbb]
                            mB = mk[:, 3, :]
                        if prevV is None:
                            vA, vAb = vt.next()
                            if i == 0:
                                P.add('pool', lambda e, vA=vA: e.memset(vA[0:64, :], 0.0), writes=[vAb])
                                src = self.swa_V[sl(r, 64, dil), h * 128:(h + 1) * 128]
                                P.dma(vA[64:128, :], src, reads=[self.dbuf(('qk', l))], writes=[vAb])
                            else:
                                raise AssertionError
                        else:
                            vA, vAb = prevV
                        vB, vBb = vt.next()
                        if not lastq:
                            t0 = r + dil * (128 * i + 64)
                            P.dma(vB[:, :], self.swa_V[sl(t0, 128, dil), h * 128:(h + 1) * 128], reads=[self.dbuf(('qk', l))], writes=[vBb])
                        else:
                            t0 = r + dil * (128 * i + 64)
                            P.dma(vB[0:64, :], self.swa_V[sl(t0, 64, dil), h * 128:(h + 1) * 128], reads=[self.dbuf(('qk', l))], writes=[vBb])
                            rr = dil - 1 - r
                            h0 = rr + dil * (HS // dil - 64)
                            P.dma(vB[64:128, :], self.h_swa_V[sl(h0, 64, dil), h * 128:(h + 1) * 128], reads=[self.dbuf('halo')], writes=[vBb])
                        prevV = (vB, vBb)
                        ps, pb = psS.next()
                        for (kk, kx, mx, rds) in ((0, kA, mA, [b_k]), (1, kB, mB, kB_reads)):
                            P.add('pe', lambda e, ps=ps, kk=kk, kx=kx, qv=qv: e.matmul(ps[:, kk * 128:(kk + 1) * 128], kx, qv, start=True, stop=False),
                                  reads=rds + [b_q], writes=[pb])
                            P.add('pe', lambda e, ps=ps, kk=kk, mx=mx: e.matmul(ps[:, kk * 128:(kk + 1) * 128], self.ident_b, mx, start=False, stop=True),
                                  reads=[b_mk, self.b_cb], writes=[pb])
                        pt, ptb = pts.next()
                        P.add('act', lambda e, pt=pt, ps=ps: e.activation(out=pt[:, :, :].rearrange("p a n -> p (a n)"), in_=ps[:, 0:256], func=AF.Exp),
                              reads=[pb], writes=[ptb])
                        po, pob = psO.next()
                        for (kk, vx, vxb) in ((0, vA, vAb), (1, vB, vBb)):
                            P.add('pe', lambda e, po=po, kk=kk, vx=vx, pt=pt: e.matmul(po[:, 0:128], vx[:, :], pt[:, kk, :], start=(kk == 0), stop=(kk == 1)),
                                  reads=[vxb, ptb], writes=[pob])
                        for kk in (0, 1):
                            P.add('pe', lambda e, po=po, kk=kk, pt=pt: e.matmul(po[:, 128:256], self.ones_b, pt[:, kk, :], start=(kk == 0), stop=(kk == 1)),
                                  reads=[self.b_cb, ptb], writes=[pob])
                        q0 = r + dil * 128 * i
                        an = accn[:, sl(q0, 128, dil)]
                        ad = accd[:, sl(q0, 128, dil)]
                        if first:
                            P.add('dve', lambda e, an=an, po=po: e.tensor_copy(an, po[:, 0:128]), reads=[pob], writes=[b_acc])
                            P.add('act', lambda e, ad=ad, po=po: e.copy(ad, po[:, 128:256]), reads=[pob], writes=[b_acc])
                        else:
                            P.add('dve', lambda e, an=an, po=po: e.tensor_tensor(out=an, in0=an, in1=po[:, 0:128], op=ALU.add), reads=[pob, b_acc], writes=[b_acc])
                            P.add('dve', lambda e, ad=ad, po=po: e.tensor_tensor(out=ad, in0=ad, in1=po[:, 128:256], op=ALU.add), reads=[pob, b_acc], writes=[b_acc])
                first = False
            CW = 2048 if T >= 2048 else T
            for c0 in range(0, T, CW):
                P.add('dve', lambda e, c0=c0, CW=CW: e.reciprocal(accd[:, c0:c0 + CW], accd[:, c0:c0 + CW]), reads=[b_acc], writes=[b_acc])
                y, yb = yst.next()
                P.add('pool', lambda e, c0=c0, CW=CW, y=y: e.tensor_tensor(out=y[:, 0:CW], in0=accn[:, c0:c0 + CW], in1=accd[:, c0:c0 + CW], op=ALU.mult),
                      reads=[b_acc], writes=[yb])
                P.dma(self.yT[h * 128:(h + 1) * 128, c0:c0 + CW], y[:, 0:CW], reads=[yb], writes=[self.dbuf(('yT', l))])

    NA_TYPES = {'F0': (0, [0, 1, 2, 3]), 'F1': (4, [-1, 0, 1, 2]), 'INT': (8, [-2, -1, 0, 1, 2]), 'L1': (13, [-2, -1, 0, 1, 2]),
                'L0': (18, [-3, -2, -1, 0, 1, 2])}

    def phase_NA(self, l):
        P = self.P
        T, NT = self.T, self.NT
        J = NT
        P.begin_phase()
        qT = P.sbuf("N_qT", [128, T], BF16)
        kT = P.sbuf("N_kT", [128, T + 256], BF16)
        V = P.sbuf("N_V", [128, J + 2, 128], BF16)
        bias_f = P.sbuf("N_bf", [128, 24, 128], F32)
        bias = P.sbuf("N_b", [128, 24, 128], BF16)
        yb_ = P.sbuf("N_y", [128, T], BF16)
        b_q, b_k, b_v, b_bf, b_b, b_y = Buf(), Buf(), Buf(), Buf(), Buf(), Buf()
        pts = Pool([(P.sbuf(f"N_pt{i}", [128, 6, 128], BF16), Buf()) for i in range(3)])
        rd = Pool([(P.sbuf(f"N_rd{i}", [128, 128], F32), Buf()) for i in range(3)])
        psS = Pool([((self.psb[i], self.psb[i + 1]), (self.psbuf[i], self.psbuf[i + 1])) for i in (0, 2)])
        psO = Pool([(self.psb[i], self.psbuf[i]) for i in (4, 5, 6)])
        for h in range(4):
            P.dma(qT[:], self.na_qT[h], reads=[self.dbuf(('qk', l))], writes=[b_q])
            P.dma(kT[:, 0:T], self.na_kT[h], reads=[self.dbuf(('qk', l))], writes=[b_k])
            P.dma(kT[:, T:T + 128], self.h_na_kT[h, :, 128:256], reads=[self.dbuf('halo')], writes=[b_k])
            P.dma(kT[:, T + 128:T + 256], self.h_na_kT[h, :, 0:128], reads=[self.dbuf('halo')], writes=[b_k])
            P.dma(V[:, 0:J, :], self.na_V[:, h * 128:(h + 1) * 128].rearrange("(j p) d -> p j d", p=128), reads=[self.dbuf(('qk', l))], writes=[b_v])
            P.dma(V[:, J, :], self.h_na_V[128:256, h * 128:(h + 1) * 128], reads=[self.dbuf('halo')], writes=[b_v])
            P.dma(V[:, J + 1, :], self.h_na_V[0:128, h * 128:(h + 1) * 128], reads=[self.dbuf('halo')], writes=[b_v])
            P.dma(bias_f[:], self.na_bias_in[l, h], writes=[b_bf])
            P.add('dve', lambda e: e.tensor_copy(bias[:], bias_f[:]), reads=[b_bf], writes=[b_b])
            for j in range(J):
                ty = 'F0' if j == 0 else 'F1' if j == 1 else 'L0' if j == J - 1 else 'L1' if j == J - 2 else 'INT'
                base, offs = self.NA_TYPES[ty]
                (psA, psB), (pbA, pbB) = psS.next()
                qv = qT[:, j * 128:(j + 1) * 128]
                for oi, o in enumerate(offs):
                    kt = j + o
                    ps, pb = (psA, pbA) if oi < 4 else (psB, pbB)
                    col = (oi % 4) * 128
                    P.add('pe', lambda e, ps=ps, col=col, kt=kt, qv=qv: e.matmul(ps[:, col:col + 128], kT[:, kt * 128:(kt + 1) * 128], qv, start=True, stop=False),
                          reads=[b_k, b_q], writes=[pb])
                    P.add('pe', lambda e, ps=ps, col=col, bi=base + oi: e.matmul(ps[:, col:col + 128], self.ident_b, bias[:, bi, :], start=False, stop=True),
                          reads=[b_b, self.b_cb], writes=[pb])
                pt, ptb = pts.next()
                n0 = min(4, len(offs))
                P.add('act', lambda e, pt=pt, psA=psA, n0=n0: e.activation(out=pt[:, 0:n0, :].rearrange("p a n -> p (a n)"), in_=psA[:, 0:n0 * 128], func=AF.Exp),
                      reads=[pbA], writes=[ptb])
                if len(offs) > 4:
                    n1 = len(offs) - 4
                    P.add('act', lambda e, pt=pt, psB=psB, n1=n1: e.activation(out=pt[:, 4:4 + n1, :].rearrange("p a n -> p (a n)"), in_=psB[:, 0:n1 * 128], func=AF.Exp),
                          reads=[pbB], writes=[ptb])
                po, pob = psO.next()
                no = len(offs)
                for oi, o in enumerate(offs):
                    kt = j + o
                    P.add('pe', lambda e, po=po, oi=oi, kt=kt, pt=pt, no=no: e.matmul(po[:, 0:128], V[:, kt, :], pt[:, oi, :], start=(oi == 0), stop=(oi == no - 1)),
                          reads=[b_v, ptb], writes=[pob])
                for oi, o in enumerate(offs):
                    P.add('pe', lambda e, po=po, oi=oi, pt=pt, no=no: e.matmul(po[:, 128:256], self.ones_b, pt[:, oi, :], start=(oi == 0), stop=(oi == no - 1)),
                          reads=[self.b_cb, ptb], writes=[pob])
                r, rb = rd.next()
                P.add('dve', lambda e, r=r, po=po: e.reciprocal(r[:, :], po[:, 128:256]), reads=[pob], writes=[rb])
                P.add('dve', lambda e, r=r, po=po, j=j: e.tensor_tensor(out=yb_[:, j * 128:(j + 1) * 128], in0=po[:, 0:128], in1=r[:, :], op=ALU.mult),
                      reads=[pob, rb], writes=[b_y])
            P.dma(self.yT[1024 + h * 128:1024 + (h + 1) * 128, :], yb_[:], reads=[b_y], writes=[self.dbuf(('yT', l))])

    def zero_y(self, l, rows):
        P = self.P
        P.begin_phase()
        z = P.sbuf("zy", [128, self.T], BF16)
        bz = Buf()
        P.add('pool', lambda e: e.memset(z[:], 0.0), writes=[bz])
        for r0 in rows:
            P.dma(self.yT[r0:r0 + 128, :], z[:], reads=[bz], writes=[self.dbuf(('yT', l))])


    def compute_lb(self):
        P = self.P
        L = self.depth
        P.begin_phase()
        for (src, dstkind, W) in ((self.hg_lb_in, 'tok', 512), (self.hg_lbT_in, 'fm', 4)):
            lg = P.sbuf(f"lb_lg{W}", [128, L, 2, W], F32)
            sm = P.sbuf(f"lb_sm{W}", [128, 2, W], F32)
            out = P.sbuf(f"lb_out{W}", [128, L, 2, 2, W], F32)
            b = Buf()
            P.dma(lg[:], src, writes=[b])
            P.add('act', lambda e, lg=lg: e.activation(out=lg[:], in_=lg[:], func=AF.Exp), reads=[b], writes=[b])
            P.add('dve', lambda e, lg=lg, sm=sm: e.tensor_copy(sm[:], lg[:, 0]), reads=[b], writes=[b])
            for l in range(1, L):
                P.add('dve', lambda e, lg=lg, sm=sm, l=l: e.tensor_tensor(out=sm[:], in0=sm[:], in1=lg[:, l], op=ALU.add), reads=[b], writes=[b])
            P.add('dve', lambda e, sm=sm: e.reciprocal(sm[:], sm[:]), reads=[b], writes=[b])
            P.add('pool', lambda e, out=out: e.memset(out[:, 0, :, 0, :], 0.0), writes=[b])
            for l in range(1, L):
                P.add('dve', lambda e, lg=lg, sm=sm, l=l: e.tensor_tensor(out=lg[:, l], in0=lg[:, l], in1=sm[:], op=ALU.mult), reads=[b], writes=[b])
                P.add('dve', lambda e, lg=lg, out=out, l=l: e.tensor_tensor(out=out[:, l, :, 0, :], in0=out[:, l - 1, :, 0, :], in1=lg[:, l], op=ALU.add),
                      reads=[b], writes=[b])
            for l in range(L):
                P.add('dve', lambda e, out=out, l=l: e.tensor_scalar(out=out[:, l, :, 1, :], in0=out[:, l, :, 0, :], scalar1=-1.0, scalar2=1.0,
                                                                     op0=ALU.mult, op1=ALU.add), reads=[b], writes=[b])
            if dstkind == 'tok':
                for l in range(L):
                    P.dma(self.lb_d[l].rearrange("d a p w -> p d a w"), out[:, l], reads=[b], writes=[self.dbuf('lb')])
            else:
                for l in range(L):
                    P.dma(self.lbT_d[l], out[:, l], reads=[b], writes=[self.dbuf('lb')])

    def phase_HG(self, l):
        P = self.P
        T, NT = self.T, self.NT
        P.begin_phase()
        lbt = P.sbuf("H_lbt", [128, 2, 2, 512], F32)
        lbf = P.sbuf("H_lbf", [128, 2, 2, 4], F32)
        gain = P.sbuf("H_gain", [128, 512], F32)
        b_lb = Buf()
        P.dma(lbt[:], self.lb_d[l].rearrange("d a p w -> p d a w"), reads=[self.dbuf('lb')], writes=[b_lb])
        P.dma(lbf[:], self.lbT_d[l], reads=[self.dbuf('lb')], writes=[b_lb])
        P.dma(gain[:], self.hg_gain_in[l], writes=[b_lb])
        S = P.sbuf("H_S", [128, 4, 128], F32)
        Sb = P.sbuf("H_Sb", [128, 4, 128], BF16)
        b_S = [Buf() for _ in range(4)]
        b_Sb = [Buf() for _ in range(4)]
        up = Pool([(P.sbuf(f"H_u{i}", [128, 3, 512], F32), Buf()) for i in range(2)])
        fp_ = Pool([(P.sbuf(f"H_f{i}", [128, 8, 128], F32), Buf()) for i in range(2)])
        sc1 = P.sbuf("H_sc1", [128, 512], F32)
        ff = P.sbuf("H_ff", [128, 512], F32)
        b_sc1, b_ff = Buf(), Buf()
        ktok = Pool([(P.sbuf(f"H_kt{i}", [128, 512], F32), Buf()) for i in range(2)])
        logf = Pool([(P.sbuf(f"H_lf{i}", [128, 512], F32), Buf()) for i in range(2)])
        vbp = Pool([(P.sbuf(f"H_vb{i}", [128, 512], BF16), Buf()) for i in range(2)])
        qkp = Pool([(P.sbuf(f"H_qk{i}", [128, 8, 128], F32), Buf()) for i in range(2)])
        exq = Pool([(P.sbuf(f"H_ex{i}", [128, 4, 128], F32), Buf()) for i in range(3)])
        Qb2 = [P.sbuf(f"H_Qb{i}", [128, 4, 128], BF16) for i in range(3)]
        b_Qb2 = [Buf() for _ in range(3)]
        QK = Pool([(P.sbuf(f"H_QK{i}", [128, 2, 128], BF16), Buf()) for i in range(3)])
        Kd2 = [P.sbuf(f"H_Kd{i}", [128, 4, 128], BF16) for i in range(3)]
        kdf = P.sbuf("H_kdf", [128, 128], F32)
        b_kdf = Buf()
        b_Kd2 = [Buf() for _ in range(3)]
        aTm = Pool([(P.sbuf(f"H_aT{i}", [128, 128], BF16), Buf()) for i in range(3)])
        ost = Pool([(P.sbuf(f"H_o{i}", [128, 512], F32), Buf()) for i in range(2)])
        o1t = Pool([(P.sbuf(f"H_o1{i}", [128, 512], F32), Buf()) for i in range(2)])
        sq = P.sbuf("H_sq", [128, 512], F32)
        ss = P.sbuf("H_ss", [128, 4], F32)
        b_sq, b_ss = Buf(), Buf()
        gg = P.sbuf("H_gg", [128, 512], F32)
        b_gg = Buf()
        yb = Pool([(P.sbuf(f"H_y{i}", [128, 512], BF16), Buf()) for i in range(2)])
        yst = Pool([(P.sbuf(f"H_ys{i}", [128, 4, 128], BF16), Buf()) for i in range(2)])
        for i in range(3):
            P.add('pool', lambda e, i=i: e.memset(Qb2[i][:], 0.0), writes=[b_Qb2[i]])
        psO = Pool([(self.psb[i], self.psbuf[i]) for i in (4,)])
        slotb = {5: [self.psbuf[5]] * 4, 6: [self.psbuf[6]] * 4}
        hres = []
        for par in range(2):
            row = []
            for h in range(4):
                r = dict(ex=(P.sbuf(f"H_rex{par}{h}", [128, 3, 128], F32), Buf()),
                         qb=(P.sbuf(f"H_rqb{par}{h}", [128, 4, 128], BF16), Buf()),
                         qk=(P.sbuf(f"H_rqk{par}{h}", [128, 2, 128], BF16), Buf()),
                         kd=(P.sbuf(f"H_rkd{par}{h}", [128, 4, 128], BF16), Buf()),
                         kdf=(P.sbuf(f"H_rkf{par}{h}", [128, 128], F32), Buf()),
                         at=(P.sbuf(f"H_rat{par}{h}", [128, 128], BF16), Buf()))
                P.add('pool', lambda e, t=r['qb'][0]: e.memset(t[:], 0.0), writes=[r['qb'][1]])
                row.append(r)
            hres.append(row)
        UTv = self.UT
        kq = 0
        tqi = 0
        for d in (0, 1):
            cbase = 2 if d == 0 else 6
            Mrem, M1, M2, msk = (self.cf[:, cbase + k, :] for k in range(4))
            fcol = C_HG_F1 if d == 0 else C_HG_F2
            if d == 0:
                P.add('pool', lambda e: e.memset(S[:], 0.0), writes=b_S)
                P.add('pool', lambda e: e.memset(Sb[:], 0.0), writes=b_Sb)
            else:
                self.exchange_state(l, 'hg', S, b_S, Sb, b_Sb)
            tiles = range(NT) if d == 0 else range(NT - 1, -1, -1)
            for tt in tiles:
                r0 = tt * 128
                u, ub = up.next()
                P.dma(u[:, 0, :], self.Utok[r0:r0 + 128, fcol:fcol + 512], reads=[self.dbuf(('Utok', l))], writes=[ub])
                P.dma(u[:, 1, :], self.Utok[r0:r0 + 128, C_HG_I:C_HG_I + 512], reads=[self.dbuf(('Utok', l))], writes=[ub])
                if d == 1:
                    P.dma(u[:, 2, :], self.Utok[r0:r0 + 128, C_HG_G:C_HG_G + 512], reads=[self.dbuf(('Utok', l))], writes=[ub])
                f, fb = fp_.next()
                P.dma(f[:, 0:4, :], UTv[C_HG_Q:C_HG_Q + 512, r0:r0 + 128].rearrange("(h p) t -> p h t", p=128), reads=[self.dbuf(('UT', l))], writes=[fb])
                P.dma(f[:, 4:8, :], UTv[fcol:fcol + 512, r0:r0 + 128].rearrange("(h p) t -> p h t", p=128), reads=[self.dbuf(('UT', l))], writes=[fb])
                kt, ktb = ktok.next()
                lf, lfb = logf.next()
                vb, vbb = vbp.next()
                P.add('act', lambda e, u=u: e.activation(out=sc1[:], in_=u[:, 0, :], func=AF.Sigmoid), reads=[ub], writes=[b_sc1])
                P.add('pool', lambda e, d=d: e.tensor_tensor(out=sc1[:], in0=sc1[:], in1=lbt[:, d, 1, :], op=ALU.mult), reads=[b_sc1, b_lb], writes=[b_sc1])
                P.add('dve', lambda e, d=d: e.tensor_tensor(out=ff[:], in0=sc1[:], in1=lbt[:, d, 0, :], op=ALU.add), reads=[b_sc1, b_lb], writes=[b_ff])
                P.add('pool', lambda e, d=d, kt=kt: e.tensor_tensor(out=kt[:], in0=lbt[:, d, 1, :], in1=sc1[:], op=ALU.subtract), reads=[b_sc1, b_lb], writes=[ktb])
                P.add('act', lambda e, lf=lf: e.activation(out=lf[:], in_=ff[:], func=AF.Ln), reads=[b_ff], writes=[lfb])
                P.add('act', lambda e, vb=vb, u=u: e.copy(vb[:], u[:, 1, :]), reads=[ub], writes=[vbb])
                qk, qkb = qkp.next()
                P.add('act', lambda e, qk=qk, f=f: e.activation(out=qk[:, 0:4, :], in_=f[:, 0:4, :], func=AF.Silu), reads=[fb], writes=[qkb])
                P.add('act', lambda e, qk=qk, f=f: e.activation(out=qk[:, 4:8, :], in_=f[:, 4:8, :], func=AF.Sigmoid, scale=-1.0), reads=[fb], writes=[qkb])
                for h in range(4):
                    P.add('pool', lambda e, qk=qk, h=h, d=d: e.tensor_scalar(out=qk[:, 4 + h, :], in0=qk[:, 4 + h, :], scalar1=lbf[:, d, 1, h:h + 1], scalar2=None,
                                                                              op0=ALU.mult), reads=[qkb, b_lb], writes=[qkb])
                par = tqi % 2
                tqi += 1
                HS4 = [slice(h * 128, (h + 1) * 128) for h in range(4)]
                R = [hres[par][h] for h in range(4)]
                for h in range(4):
                    pe_, peb = self.psb[h], self.psbuf[h]
                    P.add('pe', lambda e, pe_=pe_, lf=lf, hs=HS4[h], Mrem=Mrem: e.matmul(pe_[:, 0:128], Mrem, lf[:, hs], start=True, stop=True),
                          reads=[lfb, self.b_cf], writes=[peb])
                    P.add('pe', lambda e, pe_=pe_, lf=lf, hs=HS4[h], M1=M1: e.matmul(pe_[:, 128:256], lf[:, hs], M1, start=True, stop=True),
                          reads=[lfb, self.b_cf], writes=[peb])
                for h in range(4):
                    pe_, peb = self.psb[h], self.psbuf[h]
                    ex, exb = R[h]['ex']
                    P.add('act', lambda e, ex=ex, pe_=pe_: e.activation(out=ex[:, 0:2, :].rearrange("p a n -> p (a n)"), in_=pe_[:, 0:256], func=AF.Exp),
                          reads=[peb], writes=[exb])
                    P.add('act', lambda e, ex=ex, pe_=pe_: e.activation(out=ex[:, 2, :], in_=pe_[:, 128:256], func=AF.Exp, scale=-1.0),
                          reads=[peb], writes=[exb])
                for h in range(4):
                    ex, exb = R[h]['ex']
                    qb, qbb = R[h]['qb']
                    qbf = qb[:]
                    qb_out = bass.AP(qbf.tensor, qbf.offset, [list(qbf.ap[0]), [160, 4], [1, 32]])
                    qT_h = qk[:, h, :]
                    P.add('dve', lambda e, qb_out=qb_out, qT_h=qT_h, ex=ex: e.tensor_tensor(
                        out=qb_out, in0=qT_h.rearrange("p (c n) -> p c n", c=4), in1=ex[:, 1, :].rearrange("p (c n) -> p c n", c=4), op=ALU.mult),
                        reads=[qkb, exb], writes=[qbb])
                    QKt, QKb = R[h]['qk']
                    P.add('pool', lambda e, QKt=QKt, qT_h=qT_h, ex=ex: e.tensor_tensor(out=QKt[:, 0, :], in0=qT_h, in1=ex[:, 1, :], op=ALU.mult),
                          reads=[qkb, exb], writes=[QKb])
                    P.add('dve', lambda e, QKt=QKt, qk=qk, h=h, ex=ex: e.tensor_tensor(out=QKt[:, 1, :], in0=qk[:, 4 + h, :], in1=ex[:, 2, :], op=ALU.mult),
                          reads=[qkb, exb], writes=[QKb])
                    kd, kdb_ = R[h]['kd']
                    kf, kfb = R[h]['kdf']
                    P.add('pool', lambda e, kt=kt, hs=HS4[h], ex=ex, kf=kf: e.tensor_tensor(out=kf[:], in0=kt[:, hs], in1=ex[:, 0, :], op=ALU.mult),
                          reads=[ktb, exb], writes=[kfb])
                    P.add('dve', lambda e, kd=kd, kf=kf: e.tensor_tensor(out=kd[:], in0=kf[:].unsqueeze(1).to_broadcast([128, 4, 128]), in1=self.cf[:, 16:20, :], op=ALU.mult),
                          reads=[kfb, self.b_cf], writes=[kdb_])
                for h in range(4):
                    pe_, peb = self.psb[h], self.psbuf[h]
                    QKt, QKb = R[h]['qk']
                    P.add('pe', lambda e, pe_=pe_, QKt=QKt: e.matmul(pe_[:, 384:512], QKt[:, 1, :], QKt[:, 0, :], start=True, stop=True),
                          reads=[QKb], writes=[peb])
                for h in range(4):
                    pe_, peb = self.psb[h], self.psbuf[h]
                    at, atb = R[h]['at']
                    P.add('dve', lambda e, at=at, pe_=pe_, msk=msk: e.tensor_tensor(out=at[:], in0=pe_[:, 384:512], in1=msk, op=ALU.mult),
                          reads=[peb, self.b_cf], writes=[atb])
                corder = (0, 1, 2, 3) if d == 0 else (3, 2, 1, 0)
                for ci, c in enumerate(corder):
                    bank = 5 + (ci % 2)
                    pS = self.psb[bank]
                    for h in range(4):
                        hs = HS4[h]
                        qb, qbb = R[h]['qb']
                        kd, kdb_ = R[h]['kd']
                        at, atb = R[h]['at']
                        P.add('pe', lambda e, qb=qb, c=c, h=h, ci=ci: e.matmul(self.psb[h][:, 0:128], qb[:, c, :], Sb[:, h, :], start=(ci == 0), stop=False),
                              reads=[qbb, b_Sb[h]], writes=[self.psbuf[h]])
                        if ci == 3:
                            P.add('pe', lambda e, h=h, hs=hs, at=at, vb=vb: e.matmul(self.psb[h][:, 0:128], at[:], vb[:, hs], start=False, stop=True),
                                  reads=[atb, vbb], writes=[self.psbuf[h]])
                        P.add('pe', lambda e, pS=pS, kd=kd, c=c, vb=vb, hs=hs: e.matmul(pS[:, hs], kd[:, c, :], vb[:, hs], start=True, stop=True),
                              reads=[kdb_, vbb], writes=[slotb[bank][h]])
                    for h in range(4):
                        hs = HS4[h]
                        ex, exb = R[h]['ex']
                        deccol = (c * 32 + 31) if d == 0 else (c * 32)
                        P.add('dve', lambda e, pS=pS, h=h, hs=hs, ex=ex, deccol=deccol: e.scalar_tensor_tensor(
                            out=S[:, h, :], in0=S[:, h, :], scalar=ex[:, 1, deccol:deccol + 1], in1=pS[:, hs], op0=ALU.mult, op1=ALU.add),
                            reads=[slotb[bank][h], exb, b_S[h]], writes=[b_S[h]])
                    for h in range(4):
                        P.add('act', lambda e, h=h: e.copy(Sb[:, h, :], S[:, h, :]), reads=[b_S[h]], writes=[b_Sb[h]])
                if d == 0:
                    o, ob = ost.next()
                    for h in range(4):
                        P.add('act', lambda e, o=o, h=h: e.copy(o[:, h * 128:(h + 1) * 128], self.psb[h][:, 0:128]), reads=[self.psbuf[h]], writes=[ob])
                    P.dma(self.o1_d[r0:r0 + 128, :], o[:], reads=[ob], writes=[self.dbuf(('o1', l))])
                else:
                    o1, o1b = o1t.next()
                    P.dma(o1[:], self.o1_d[r0:r0 + 128, :], reads=[self.dbuf(('o1', l))], writes=[o1b])
                    o, ob = ost.next()
                    for h in range(4):
                        P.add('dve', lambda e, o=o, h=h, o1=o1: e.tensor_tensor(out=o[:, h * 128:(h + 1) * 128], in0=self.psb[h][:, 0:128], in1=o1[:, h * 128:(h + 1) * 128], op=ALU.add),
                              reads=[self.psbuf[h], o1b], writes=[ob])
                    self.norm_gate_emit(l, o, ob, u[:, 2, :], ub, gain, b_lb, sq, b_sq, ss, b_ss, gg, b_gg, yb, yst, 512, r0)

    def norm_gate_emit(self, l, o, ob, gt, gtb, gain, b_gain, sq, b_sq, ss, b_ss, gg, b_gg, yb, yst, yrow0, r0):
        P = self.P
        P.add('pool', lambda e: e.tensor_tensor(out=sq[:], in0=o[:], in1=o[:], op=ALU.mult), reads=[ob], writes=[b_sq])
        P.add('dve', lambda e: e.tensor_reduce(out=ss[:, 0:4], in_=sq[:].rearrange("p (h d) -> p h d", h=4), axis=AX.X, op=ALU.add),
              reads=[b_sq], writes=[b_ss])
        P.add('dve', lambda e: e.tensor_scalar(out=ss[:, 0:4], in0=ss[:, 0:4], scalar1=1.0 / 128, scalar2=EPS, op0=ALU.mult, op1=ALU.add),
              reads=[b_ss], writes=[b_ss])
        P.add('act', lambda e: e.activation(out=ss[:, 0:4], in_=ss[:, 0:4], func=AF.Sqrt), reads=[b_ss], writes=[b_ss])
        P.add('dve', lambda e: e.reciprocal(ss[:, 0:4], ss[:, 0:4]), reads=[b_ss], writes=[b_ss])
        P.add('act', lambda e: e.activation(out=gg[:], in_=gt, func=AF.Silu), reads=[gtb], writes=[b_gg])
        P.add('pool', lambda e: e.tensor_tensor(out=gg[:], in0=gg[:], in1=gain[:], op=ALU.mult), reads=[b_gg, b_gain], writes=[b_gg])
        P.add('dve', lambda e: e.tensor_tensor(out=o[:].rearrange("p (h d) -> p h d", h=4), in0=o[:].rearrange("p (h d) -> p h d", h=4),
                                               in1=ss[:, 0:4].unsqueeze(2).to_broadcast([128, 4, 128]), op=ALU.mult), reads=[ob, b_ss], writes=[ob])
        y, ybb = yb.next()
        P.add('dve', lambda e, y=y: e.tensor_tensor(out=y[:], in0=o[:], in1=gg[:], op=ALU.mult), reads=[ob, b_gg], writes=[ybb])
        ps, pb = self.psb[7], self.psbuf[7]
        psv = ps[:, 0:256].bitcast(BF16)
        for h in range(4):
            P.add('pe', lambda e, h=h, y=y, psv=psv: e.transpose(psv[:, h * 128:(h + 1) * 128], y[:, h * 128:(h + 1) * 128], self.ident_b),
                  reads=[ybb, self.b_cb], writes=[pb])
        st, stb = yst.next()
        P.add('act', lambda e, st=st, psv=psv: e.copy(st[:, :, :].rearrange("p h t -> p (h t)"), psv[:, :]), reads=[pb], writes=[stb])
        P.dma(self.yT[yrow0:yrow0 + 512, r0:r0 + 128].rearrange("(h p) t -> p h t", p=128), st[:, :, :], reads=[stb], writes=[self.dbuf(('yT', l))])


    def exchange_halos(self, l):
        P = self.P
        T, HS = self.T, self.HS
        P.begin_phase()
        bpub, bg = Buf(), Buf()
        src = [self.dbuf(('qk', l)), self.dbuf(('UT', l))]
        P.dma(self.pubA[0].rearrange("p (h t) -> p h t", h=4), self.swa_kT[:, :, T - HS:T].rearrange("h p t -> p h t"), reads=src, writes=[bpub])
        P.dma(self.pubA[1].rearrange("p (a n) -> p a n", n=512), self.swa_V[T - HS:T, :].rearrange("(a p) n -> p a n", p=128), reads=src, writes=[bpub])
        P.dma(self.pubA[2][:, 0:1024].rearrange("p (h t) -> p h t", h=4), self.na_kT[:, :, T - 256:T].rearrange("h p t -> p h t"), reads=src, writes=[bpub])
        P.dma(self.pubA[2][:, 1024:2048].rearrange("p (a n) -> p a n", n=512), self.na_V[T - 256:T, :].rearrange("(a p) n -> p a n", p=128), reads=src, writes=[bpub])
        P.dma(self.pubB.rearrange("p (c t) -> p c t", t=2), self.UT[C_DN_QKV:C_DN_QKV + 1536, T - 2:T].rearrange("(c p) t -> p c t", p=128), reads=src, writes=[bpub])
        for i in range(3):
            P.add('pool', lambda e, i=i: e.collective_compute("AllGather", ALU.bypass, replica_groups=self.RG, ins=[self.pubA[i]], outs=[self.gathA[i]]),
                  reads=[bpub], writes=[bg])
        P.add('pool', lambda e: e.collective_compute("AllGather", ALU.bypass, replica_groups=self.RG, ins=[self.pubB], outs=[self.gathB]),
              reads=[bpub], writes=[bg])
        fl = P.sbuf("X_fl", [128, 2], F32)
        b_fl = Buf()
        P.dma(fl[:], self.flag_in, writes=[b_fl])
        hb = self.dbuf('halo')
        WM = max(self.XWs)
        p0 = P.sbuf("X_p0", [128, WM], BF16)
        p1 = P.sbuf("X_p1", [128, WM], BF16)
        b_p = Buf()
        for i, w in enumerate(self.XWs):
            P.dma(p0[:, 0:w], self.gathA[i][0:128, :], reads=[bg], writes=[b_p])
            P.dma(p1[:, 0:w], self.gathA[i][128:256, :], reads=[bg], writes=[b_p])
            P.add('dve', lambda e, w=w: e.tensor_scalar(out=p0[:, 0:w], in0=p0[:, 0:w], scalar1=fl[:, 0:1], scalar2=None, op0=ALU.mult), reads=[b_p, b_fl], writes=[b_p])
            P.add('dve', lambda e, w=w: e.scalar_tensor_tensor(out=p0[:, 0:w], in0=p1[:, 0:w], scalar=fl[:, 1:2], in1=p0[:, 0:w], op0=ALU.mult, op1=ALU.add),
                  reads=[b_p, b_fl], writes=[b_p])
            if i == 0:
                P.dma(self.h_swa_kT.rearrange("h p t -> p h t"), p0[:, 0:w].rearrange("p (h t) -> p h t", h=4), reads=[b_p], writes=[hb])
            elif i == 1:
                P.dma(self.h_swa_V.rearrange("(a p) n -> p a n", p=128), p0[:, 0:w].rearrange("p (a n) -> p a n", n=512), reads=[b_p], writes=[hb])
            else:
                P.dma(self.h_na_kT.rearrange("h p t -> p h t"), p0[:, 0:1024].rearrange("p (h t) -> p h t", h=4), reads=[b_p], writes=[hb])
                P.dma(self.h_na_V.rearrange("(a p) n -> p a n", p=128), p0[:, 1024:2048].rearrange("p (a n) -> p a n", n=512), reads=[b_p], writes=[hb])
        q0 = P.sbuf("X_q0", [128, 12, 2], F32)
        q1 = P.sbuf("X_q1", [128, 12, 2], F32)
        q2 = P.sbuf("X_q2", [128, 12, 2], F32)
        b_q = Buf()
        P.dma(q0[:], self.gathB[0:128, :].rearrange("p (c t) -> p c t", t=2), reads=[bg], writes=[b_q])
        P.dma(q1[:], self.gathB[128:256, :].rearrange("p (c t) -> p c t", t=2), reads=[bg], writes=[b_q])
        P.add('dve', lambda e: e.tensor_scalar(out=q0[:], in0=q0[:], scalar1=fl[:, 0:1], scalar2=None, op0=ALU.mult), reads=[b_q, b_fl], writes=[b_q])
        P.add('dve', lambda e: e.scalar_tensor_tensor(out=q0[:], in0=q1[:], scalar=fl[:, 1:2], in1=q0[:], op0=ALU.mult, op1=ALU.add), reads=[b_q, b_fl], writes=[b_q])
        P.add('dve', lambda e: e.tensor_copy(q2[:, :, 0:1], q0[:, :, 1:2]), reads=[b_q], writes=[b_q])
        P.add('dve', lambda e: e.tensor_copy(q2[:, :, 1:2], q0[:, :, 0:1]), reads=[b_q], writes=[b_q])
        P.dma(self.h_dn_x.rearrange("(c p) t -> p c t", p=128), q2[:], reads=[b_q], writes=[hb])

    def exchange_state(self, l, kind, S, b_S, Sb, b_Sb):
        P = self.P
        if not self.couple:
            P.add('pool', lambda e: e.memset(S[:], 0.0), writes=b_S)
            P.add('pool', lambda e: e.memset(Sb[:], 0.0), writes=b_Sb)
            return
        bpub, bg = self.dbuf(('pubS', kind, l)), self.dbuf(('gathS', kind, l))
        P.dma(self.pubS.rearrange("p (h n) -> p h n", h=4), S[:], reads=b_S, writes=[self.dbuf('pubS_any')])
        P.add('pool', lambda e: e.collective_compute("AllGather", ALU.bypass, replica_groups=self.RG, ins=[self.pubS], outs=[self.gathS]),
              reads=[self.dbuf('pubS_any')], writes=[self.dbuf('gathS_any')])
        t1 = P.sbuf(f"XS_{kind}", [128, 4, 128], F32)
        fl = P.sbuf(f"XSf_{kind}", [128, 2], F32)
        bt, bf = Buf(), Buf()
        P.dma(fl[:], self.flag_in, writes=[bf])
        P.dma(S[:], self.gathS[0:128, :].rearrange("p (h n) -> p h n", h=4), reads=[self.dbuf('gathS_any')], writes=b_S)
        P.dma(t1[:], self.gathS[128:256, :].rearrange("p (h n) -> p h n", h=4), reads=[self.dbuf('gathS_any')], writes=[bt])
        P.add('dve', lambda e: e.tensor_scalar(out=S[:], in0=S[:], scalar1=fl[:, 0:1], scalar2=None, op0=ALU.mult), reads=b_S + [bf], writes=b_S)
        P.add('dve', lambda e: e.scalar_tensor_tensor(out=S[:], in0=t1[:], scalar=fl[:, 1:2], in1=S[:], op0=ALU.mult, op1=ALU.add), reads=b_S + [bf, bt], writes=b_S)
        P.add('act', lambda e: e.copy(Sb[:], S[:]), reads=b_S, writes=b_Sb)

    def phase_DNconv(self, l):
        P = self.P
        T, NT = self.T, self.NT
        P.begin_phase()
        cw = P.sbuf("DC_w", [128, 12, 5], F32)
        b_cw = Buf()
        P.dma(cw[:], self.dn_conv_in[l], writes=[b_cw])
        xp = Pool([(P.sbuf(f"DC_x{i}", [128, T + 4], F32), Buf()) for i in range(2)])
        acc = Pool([(P.sbuf(f"DC_a{i}", [128, T], F32), Buf()) for i in range(2)])
        sb = Pool([(P.sbuf(f"DC_s{i}", [128, T], BF16), Buf()) for i in range(2)])
        st = Pool([(P.sbuf(f"DC_t{i}", [128, 512], BF16), Buf()) for i in range(3)])
        pss = Pool([(self.psb[i], self.psbuf[i]) for i in range(4)])
        for ct in range(12):
            x, xb = xp.next()
            r0 = C_DN_QKV + ct * 128
            P.add('pool', lambda e, x=x: e.memset(x[:, 0:2], 0.0), writes=[xb])
            P.dma(x[:, 2:2 + T], self.UT[r0:r0 + 128, :], reads=[self.dbuf(('UT', l))], writes=[xb])
            P.dma(x[:, 2 + T:4 + T], self.h_dn_x[ct * 128:(ct + 1) * 128, :], reads=[self.dbuf('halo')], writes=[xb])
            a, ab = acc.next()
            eng = 'dve'
            P.add(eng, lambda e, a=a, x=x, ct=ct: e.tensor_scalar(out=a[:], in0=x[:, 0:T], scalar1=cw[:, ct, 0:1], scalar2=None, op0=ALU.mult),
                  reads=[xb, b_cw], writes=[ab])
            for j in range(1, 5):
                P.add(eng, lambda e, a=a, x=x, ct=ct, j=j: e.scalar_tensor_tensor(out=a[:], in0=x[:, j:j + T], scalar=cw[:, ct, j:j + 1], in1=a[:],
                                                                                  op0=ALU.mult, op1=ALU.add), reads=[xb, b_cw, ab], writes=[ab])
            sbt, sbb = sb.next()
            P.add('act', lambda e, sbt=sbt, a=a: e.activation(out=sbt[:], in_=a[:], func=AF.Silu), reads=[ab], writes=[sbb])
            for tg in range(NT // 4):
                ps, pb = pss.next()
                psv = ps[:, 0:256].bitcast(BF16)
                for k in range(4):
                    tt = tg * 4 + k
                    P.add('pe', lambda e, psv=psv, k=k, tt=tt, sbt=sbt: e.transpose(psv[:, k * 128:(k + 1) * 128], sbt[:, tt * 128:(tt + 1) * 128], self.ident_b),
                          reads=[sbb, self.b_cb], writes=[pb])
                s_, s_b = st.next()
                P.add('act' if tg % 2 else 'dve', (lambda e, s_=s_, psv=psv: e.copy(s_[:], psv[:, :])) if tg % 2 else (lambda e, s_=s_, psv=psv: e.tensor_copy(s_[:], psv[:, :])),
                      reads=[pb], writes=[s_b])
                P.dma(self.dnc_d[tg * 512:(tg + 1) * 512, ct * 128:(ct + 1) * 128].rearrange("(k p) c -> p k c", p=128),
                      s_[:].rearrange("p (k c) -> p k c", k=4), reads=[s_b], writes=[self.dbuf(('dnc', l))])

    def phase_DN(self, l):
        P = self.P
        T, NT = self.T, self.NT
        self.phase_DNconv(l)
        P.begin_phase()
        par = P.sbuf("D_par", [128, 2, 2, 4], F32)
        gain = P.sbuf("D_gain", [128, 512], F32)
        b_par = Buf()
        P.dma(par[:], self.dn_par_in[l], writes=[b_par])
        P.dma(gain[:], self.dn_gain_in[l], writes=[b_par])
        P.add('act', lambda e: e.activation(out=par[:, 0], in_=par[:, 0], func=AF.Exp), reads=[b_par], writes=[b_par])
        P.add('dve', lambda e: e.tensor_scalar(out=par[:, 0], in0=par[:, 0], scalar1=-1.0, scalar2=None, op0=ALU.mult), reads=[b_par], writes=[b_par])
        S = P.sbuf("D_S", [128, 4, 128], F32)
        Sb = P.sbuf("D_Sb", [128, 4, 128], BF16)
        b_S = [Buf() for _ in range(4)]
        b_Sb = [Buf() for _ in range(4)]
        cin = Pool([(P.sbuf(f"D_c{i}", [128, 1536], BF16), Buf()) for i in range(2)])
        zin = Pool([(P.sbuf(f"D_z{i}", [128, 512], F32), Buf()) for i in range(2)])
        abin = Pool([(P.sbuf(f"D_ab{i}", [128, 16], F32), Buf()) for i in range(2)])
        sm = Pool([(P.sbuf(f"D_sm{i}", [128, 12, 4], F32), Buf()) for i in range(2)])
        sqk = P.sbuf("D_sqk", [128, 1024], F32)
        b_sqk = Buf()
        tokb = Pool([(P.sbuf(f"D_tb{i}", [128, 3, 512], BF16), Buf()) for i in range(2)])
        kdec = Pool([(P.sbuf(f"D_kd{i}", [128, 512], BF16), Buf()) for i in range(2)])
        vb4 = Pool([(P.sbuf(f"D_vb4{i}", [128, 4, 128], F32), Buf()) for i in range(3)])
        vbe = Pool([(P.sbuf(f"D_vbe{i}", [128, 512], F32), Buf()) for i in range(2)])
        fm = [P.sbuf(f"D_fm{i}", [128, 3, 4, 128], BF16) for i in range(3)]
        b_fm = [Buf() for _ in range(3)]
        qTp = Pool([(P.sbuf(f"D_qT{i}", [128, 128], BF16), Buf()) for i in range(3)])
        gbc = Pool([(P.sbuf(f"D_gbc{i}", [128, 2, 128], F32), Buf()) for i in range(3)])
        Wm = Pool([(P.sbuf(f"D_W{i}", [128, 3, 128], F32), Buf()) for i in range(3)])
        Nm = Pool([(P.sbuf(f"D_N{i}", [128, 2, 128], F32), Buf()) for i in range(3)])
        FT = Pool([(P.sbuf(f"D_FT{i}", [128, 128], F32), Buf()) for i in range(3)])
        Yb = Pool([(P.sbuf(f"D_Yb{i}", [128, 128], BF16), Buf()) for i in range(3)])
        aTb = Pool([(P.sbuf(f"D_aT{i}", [128, 128], BF16), Buf()) for i in range(3)])
        last4 = Pool([(P.sbuf(f"D_l4{i}", [128, 4], F32), Buf()) for i in range(3)])
        rhsc = Pool([(P.sbuf(f"D_rc{i}", [128, 128], BF16), Buf()) for i in range(4)])
        vnew = Pool([(P.sbuf(f"D_vn{i}", [128, 128], BF16), Buf()) for i in range(4)])
        ost = Pool([(P.sbuf(f"D_o{i}", [128, 512], F32), Buf()) for i in range(2)])
        o1t = Pool([(P.sbuf(f"D_o1{i}", [128, 512], F32), Buf()) for i in range(2)])
        sq = P.sbuf("D_sq", [128, 512], F32)
        ss = P.sbuf("D_ss", [128, 4], F32)
        b_sq, b_ss = Buf(), Buf()
        gg = P.sbuf("D_gg", [128, 512], F32)
        b_gg = Buf()
        yb = Pool([(P.sbuf(f"D_y{i}", [128, 512], BF16), Buf()) for i in range(2)])
        yst = Pool([(P.sbuf(f"D_ys{i}", [128, 4, 128], BF16), Buf()) for i in range(2)])
        for i in range(3):
            P.add('pool', lambda e, i=i: e.memset(fm[i][:], 0.0), writes=[b_fm[i]])
        psA = Pool([(self.psb[i], self.psbuf[i]) for i in (0, 1)])
        psW = Pool([(self.psb[i], self.psbuf[i]) for i in (2, 3)])
        psN = Pool([(self.psb[i], self.psbuf[i]) for i in (4,)])
        psO = Pool([(self.psb[i], self.psbuf[i]) for i in (5,)])
        psS = Pool([(self.psb[i], self.psbuf[i]) for i in (6, 7)])
        ind4 = self.cf[:, 16:20, :]
        kq = 0
        for d in (0, 1):
            cb = 2 if d == 0 else 6
            Mrem, M1 = self.cf[:, cb, :], self.cf[:, cb + 1, :]
            mk_ij_s = self.cf[:, 12 if d == 0 else 10, :]
            mk_ji_s = self.cf[:, 10 if d == 0 else 12, :]
            mk_ji_i = self.cf[:, 11 if d == 0 else 13, :]
            if d == 0:
                P.add('pool', lambda e: e.memset(S[:], 0.0), writes=b_S)
                P.add('pool', lambda e: e.memset(Sb[:], 0.0), writes=b_Sb)
            else:
                self.exchange_state(l, 'dn', S, b_S, Sb, b_Sb)
            tiles = range(NT) if d == 0 else range(NT - 1, -1, -1)
            for tt in tiles:
                r0 = tt * 128
                c_, cb_ = cin.next()
                P.dma(c_[:], self.dnc_d[r0:r0 + 128, :], reads=[self.dbuf(('dnc', l))], writes=[cb_])
                ab, abb = abin.next()
                P.dma(ab[:], self.Utok[r0:r0 + 128, C_DN_AB:C_DN_AB + 16], reads=[self.dbuf(('Utok', l))], writes=[abb])
                if d == 1:
                    z, zb = zin.next()
                    P.dma(z[:], self.Utok[r0:r0 + 128, C_DN_Z:C_DN_Z + 512], reads=[self.dbuf(('Utok', l))], writes=[zb])
                m, mb = sm.next()
                P.add('pool', lambda e, c_=c_: e.tensor_tensor(out=sqk[:], in0=c_[:, 0:1024], in1=c_[:, 0:1024], op=ALU.mult), reads=[cb_], writes=[b_sqk])
                P.add('dve', lambda e, m=m: e.tensor_reduce(out=m[:, 0:2, :].rearrange("p a h -> p (a h)"), in_=sqk[:].rearrange("p (a d) -> p a d", a=8),
                                                            axis=AX.X, op=ALU.add), reads=[b_sqk], writes=[mb])
                P.add('dve', lambda e, m=m: e.tensor_scalar(out=m[:, 0:2, :], in0=m[:, 0:2, :], scalar1=EPS, scalar2=None, op0=ALU.add), reads=[mb], writes=[mb])
                P.add('act', lambda e, m=m: e.activation(out=m[:, 0:2, :], in_=m[:, 0:2, :], func=AF.Sqrt), reads=[mb], writes=[mb])
                P.add('dve', lambda e, m=m: e.reciprocal(m[:, 0:2, :], m[:, 0:2, :]), reads=[mb], writes=[mb])
                P.add('dve', lambda e, m=m: e.tensor_scalar(out=m[:, 0, :], in0=m[:, 0, :], scalar1=128.0 ** -0.5, scalar2=None, op0=ALU.mult), reads=[mb], writes=[mb])
                P.add('dve', lambda e, m=m, ab=ab, d=d: e.tensor_tensor(out=m[:, 2, :], in0=ab[:, 4 * d:4 * d + 4], in1=par[:, 1, d, :], op=ALU.add),
                      reads=[abb, b_par], writes=[mb])
                P.add('act', lambda e, m=m: e.activation(out=m[:, 2, :], in_=m[:, 2, :], func=AF.Exp), reads=[mb], writes=[mb])
                P.add('act', lambda e, m=m: e.activation(out=m[:, 2, :], in_=m[:, 2, :], func=AF.Ln, bias=1.0), reads=[mb], writes=[mb])
                P.add('dve', lambda e, m=m, d=d: e.tensor_tensor(out=m[:, 2, :], in0=m[:, 2, :], in1=par[:, 0, d, :], op=ALU.mult), reads=[mb, b_par], writes=[mb])
                P.add('act', lambda e, m=m, ab=ab, d=d: e.activation(out=m[:, 3, :], in_=ab[:, 8 + 4 * d:12 + 4 * d], func=AF.Sigmoid), reads=[abb], writes=[mb])
                P.add('act', lambda e, m=m: e.activation(out=m[:, 4, :], in_=m[:, 3, :], func=AF.Ln), reads=[mb], writes=[mb])
                pa, pab = psA.next()
                P.add('pe', lambda e, pa=pa, m=m, M1=M1: e.matmul(pa[:, 0:4], M1, m[:, 2, :], start=True, stop=True), reads=[mb, self.b_cf], writes=[pab])
                P.add('pe', lambda e, pa=pa, m=m, Mrem=Mrem: e.matmul(pa[:, 4:8], Mrem, m[:, 2, :], start=True, stop=True), reads=[mb, self.b_cf], writes=[pab])
                P.add('dve', lambda e, pa=pa, m=m: e.tensor_copy(m[:, 5:7, :].rearrange("p a h -> p (a h)"), pa[:, 0:8]), reads=[pab], writes=[mb])
                P.add('act', lambda e, m=m: e.activation(out=m[:, 7:9, :], in_=m[:, 5:7, :], func=AF.Exp), reads=[mb], writes=[mb])
                P.add('dve', lambda e, m=m: e.tensor_tensor(out=m[:, 9, :], in0=m[:, 5, :], in1=m[:, 4, :], op=ALU.add), reads=[mb], writes=[mb])
                P.add('dve', lambda e, m=m: e.tensor_scalar(out=m[:, 10, :], in0=m[:, 5, :], scalar1=-1.0, scalar2=None, op0=ALU.mult), reads=[mb], writes=[mb])
                P.add('dve', lambda e, m=m: e.scalar_tensor_tensor(out=m[:, 11, :], in0=m[:, 3, :], scalar=-1.0, in1=m[:, 7, :], op0=ALU.mult, op1=ALU.mult),
                      reads=[mb], writes=[mb])
                tb, tbb = tokb.next()
                kdt, kdb = kdec.next()
                ve, veb = vbe.next()
                c3 = c_[:].rearrange("p (a h d) -> p a h d", a=3, h=4)
                bc = lambda k, m=m: m[:, k, :].unsqueeze(2).to_broadcast([128, 4, 128])
                b0, b1, b3, b7, b8 = bc(0), bc(1), bc(3), bc(7), bc(8)
                v4h = lambda ap: ap.rearrange("p (h d) -> p h d", h=4)
                P.add('dve', lambda e, o_=v4h(tb[:, 0, :]), i0=c3[:, 1], i1=b1: e.tensor_tensor(out=o_, in0=i0, in1=i1, op=ALU.mult), reads=[cb_, mb], writes=[tbb])
                P.add('pool', lambda e, o_=v4h(tb[:, 1, :]), i0=c3[:, 0], i1=b0: e.tensor_tensor(out=o_, in0=i0, in1=i1, op=ALU.mult), reads=[cb_, mb], writes=[tbb])
                P.add('pool', lambda e, o_=v4h(tb[:, 2, :]), i0=v4h(tb[:, 1, :]), i1=b7: e.tensor_tensor(out=o_, in0=i0, in1=i1, op=ALU.mult), reads=[tbb, mb], writes=[tbb])
                P.add('dve', lambda e, o_=v4h(kdt[:]), i0=v4h(tb[:, 0, :]), i1=b8: e.tensor_tensor(out=o_, in0=i0, in1=i1, op=ALU.mult), reads=[tbb, mb], writes=[kdb])
                P.add('pool', lambda e, o_=v4h(ve[:]), i0=c3[:, 2], i1=b3: e.tensor_tensor(out=o_, in0=i0, in1=i1, op=ALU.mult), reads=[cb_, mb], writes=[veb])
                po, pob = psO.next()
                for h in range(4):
                    hs = slice(h * 128, (h + 1) * 128)
                    k3 = kq % 3
                    kq += 1
                    f = fm[k3]
                    fbuf = b_fm[k3]
                    pt, ptb = psA.next()
                    ptv = pt[:, 0:256].bitcast(BF16)
                    for k in range(3):
                        P.add('pe', lambda e, ptv=ptv, k=k, tb=tb, hs=hs: e.transpose(ptv[:, k * 128:(k + 1) * 128], tb[:, k, hs], self.ident_b),
                              reads=[tbb, self.b_cb], writes=[ptb])
                    ff_ = f[:]
                    base = ff_.offset
                    pstr = list(ff_.ap[0])
                    k4_out = bass.AP(ff_.tensor, base + 512, [pstr, [160, 4], [1, 32]])
                    q4_out = bass.AP(ff_.tensor, base + 1024, [pstr, [160, 4], [1, 32]])
                    qt, qtb = qTp.next()
                    P.add('act', lambda e, f=f, ptv=ptv: e.copy(f[:, 0, 0, :], ptv[:, 0:128]), reads=[ptb], writes=[fbuf])
                    P.add('dve', lambda e, k4_out=k4_out, ptv=ptv: e.tensor_copy(k4_out, ptv[:, 0:128].rearrange("p (c n) -> p c n", c=4)), reads=[ptb], writes=[fbuf])
                    P.add('act', lambda e, qt=qt, ptv=ptv: e.copy(qt[:], ptv[:, 128:256]), reads=[ptb], writes=[qtb])
                    P.add('dve', lambda e, q4_out=q4_out, ptv=ptv: e.tensor_copy(q4_out, ptv[:, 256:384].rearrange("p (c n) -> p c n", c=4)), reads=[ptb], writes=[fbuf])
                    kT = f[:, 0, 0, :]
                    gb, gbb = gbc.next()
                    P.add('pool', lambda e, gb=gb, m=m, h=h: e.tensor_scalar(out=gb[:, 0, :], in0=self.ones_f, scalar1=m[:, 2, h:h + 1], scalar2=None, op0=ALU.mult),
                          reads=[mb, self.b_cf], writes=[gbb])
                    P.add('pool', lambda e, gb=gb, m=m, h=h: e.tensor_scalar(out=gb[:, 1, :], in0=self.ones_f, scalar1=m[:, 4, h:h + 1], scalar2=None, op0=ALU.mult),
                          reads=[mb, self.b_cf], writes=[gbb])
                    pw, pwb = psW.next()
                    P.add('pe', lambda e, pw=pw, gb=gb, M1=M1: e.matmul(pw[:, 0:128], gb[:, 0, :], M1, start=True, stop=False), reads=[gbb, self.b_cf], writes=[pwb])
                    P.add('pe', lambda e, pw=pw, mk=mk_ij_s: e.matmul(pw[:, 0:128], self.ident_f, mk, start=False, stop=True), reads=[self.b_cf], writes=[pwb])
                    P.add('pe', lambda e, pw=pw, gb=gb, M1=M1: e.matmul(pw[:, 128:256], gb[:, 0, :], M1, start=True, stop=False), reads=[gbb, self.b_cf], writes=[pwb])
                    P.add('pe', lambda e, pw=pw, gb=gb: e.matmul(pw[:, 128:256], gb[:, 1, :], self.ident_f, start=False, stop=False), reads=[gbb, self.b_cf], writes=[pwb])
                    P.add('pe', lambda e, pw=pw, mk=mk_ji_s: e.matmul(pw[:, 128:256], self.cf[:, 20, :], mk, start=False, stop=True), reads=[self.b_cf], writes=[pwb])
                    P.add('pe', lambda e, pw=pw, gb=gb, M1=M1: e.matmul(pw[:, 256:384], gb[:, 0, :], M1, start=True, stop=False), reads=[gbb, self.b_cf], writes=[pwb])
                    P.add('pe', lambda e, pw=pw, mk=mk_ji_i: e.matmul(pw[:, 256:384], self.cf[:, 20, :], mk, start=False, stop=True), reads=[self.b_cf], writes=[pwb])
                    P.add('pe', lambda e, pw=pw, kT=kT: e.matmul(pw[:, 384:512], kT, kT, start=True, stop=True), reads=[fbuf], writes=[pwb])
                    W, Wb = Wm.next()
                    P.add('act', lambda e, W=W, pw=pw, m=m, h=h: e.activation(out=W[:, 0, :], in_=pw[:, 0:128], func=AF.Exp, scale=-1.0, bias=m[:, 9, h:h + 1]),
                          reads=[pwb, mb], writes=[Wb])
                    P.add('act', lambda e, W=W, pw=pw, m=m, h=h: e.activation(out=W[:, 1:3, :].rearrange("p a n -> p (a n)"), in_=pw[:, 128:384], func=AF.Exp,
                                                                              bias=m[:, 10, h:h + 1]), reads=[pwb, mb], writes=[Wb])
                    Nt, Nb = Nm.next()
                    P.add('dve', lambda e, Nt=Nt, pw=pw, W=W: e.tensor_tensor(out=Nt[:, 0, :], in0=pw[:, 384:512], in1=W[:, 0, :], op=ALU.mult), reads=[pwb, Wb], writes=[Nb])
                    P.add('pool', lambda e, Nt=Nt: e.tensor_tensor(out=Nt[:, 0, :], in0=Nt[:, 0, :], in1=self.ident_f, op=ALU.add), reads=[Nb, self.b_cf], writes=[Nb])
                    P.add('dve', lambda e, Nt=Nt, pw=pw, W=W: e.tensor_tensor(out=Nt[:, 1, :], in0=pw[:, 384:512], in1=W[:, 1, :], op=ALU.mult), reads=[pwb, Wb], writes=[Nb])
                    P.add('pool', lambda e, Nt=Nt: e.tensor_tensor(out=Nt[:, 1, :], in0=self.ident_f, in1=Nt[:, 1, :], op=ALU.subtract), reads=[Nb, self.b_cf], writes=[Nb])
                    pq, pqb = psA.next()
                    P.add('pe', lambda e, pq=pq, kT=kT, qt=qt: e.matmul(pq[:, 0:128], kT, qt[:], start=True, stop=True), reads=[fbuf, qtb], writes=[pqb])
                    at, atb = aTb.next()
                    P.add('dve', lambda e, at=at, pq=pq, W=W: e.tensor_tensor(out=at[:], in0=pq[:, 0:128], in1=W[:, 2, :], op=ALU.mult), reads=[pqb, Wb], writes=[atb])
                    P.add('pe', lambda e, pq=pq, gb=gb: e.matmul(pq[:, 128:132], gb[:, 0, :], self.cf[:, 16:20, 0], start=True, stop=True), reads=[gbb, self.b_cf], writes=[pqb])
                    l4, l4b = last4.next()
                    P.add('act', lambda e, l4=l4, pq=pq: e.activation(out=l4[:], in_=pq[:, 128:132], func=AF.Exp), reads=[pqb], writes=[l4b])
                    for it in range(4):
                        pn, pnb = psN.next()
                        P.add('pe', lambda e, pn=pn, Nt=Nt: e.matmul(pn[:, 0:128], Nt[:, 1, :], Nt[:, 0, :], start=True, stop=True), reads=[Nb], writes=[pnb])
                        ft, ftb = FT.next()
                        P.add('dve', lambda e, ft=ft, pn=pn: e.tensor_tensor(out=ft[:], in0=self.ident_f, in1=pn[:, 0:128], op=ALU.subtract), reads=[pnb, self.b_cf], writes=[ftb])
                        P.add('pe', lambda e, pn=pn, ft=ft, Nt=Nt: e.matmul(pn[:, 128:256], ft[:], Nt[:, 1, :], start=True, stop=True), reads=[ftb, Nb], writes=[pnb])
                        P.add('dve', lambda e, Nt=Nt, pn=pn: e.tensor_tensor(out=Nt[:, 1, :], in0=Nt[:, 1, :], in1=pn[:, 128:256], op=ALU.add), reads=[pnb, Nb], writes=[Nb])
                    ybf, ybb_ = Yb.next()
                    P.add('act', lambda e, ybf=ybf, Nt=Nt: e.copy(ybf[:], Nt[:, 1, :]), reads=[Nb], writes=[ybb_])
                    v4, v4b = vb4.next()
                    P.add('pool', lambda e, v4=v4, ve=ve, hs=hs: e.tensor_tensor(out=v4[:], in0=ve[:, hs].unsqueeze(1).to_broadcast([128, 4, 128]), in1=ind4, op=ALU.mult),
                          reads=[veb, self.b_cf], writes=[v4b])
                    corder = (0, 1, 2, 3) if d == 0 else (3, 2, 1, 0)
                    for ci, c in enumerate(corder):
                        pS, pSb = psS.next()
                        P.add('pe', lambda e, pS=pS, f=f, c=c, h=h: e.matmul(pS[:, 0:128], f[:, 1, c, :], Sb[:, h, :], start=True, stop=True),
                              reads=[fbuf, b_Sb[h]], writes=[pSb])
                        P.add('pe', lambda e, po=po, hs=hs, f=f, c=c, h=h, ci=ci: e.matmul(po[:, hs], f[:, 2, c, :], Sb[:, h, :], start=(ci == 0), stop=False),
                              reads=[fbuf, b_Sb[h]], writes=[pob])
                        rc, rcb = rhsc.next()
                        P.add('dve', lambda e, rc=rc, pS=pS, m=m, h=h, v4=v4, c=c: e.scalar_tensor_tensor(
                            out=rc[:], in0=pS[:, 0:128], scalar=m[:, 11, h:h + 1], in1=v4[:, c, :], op0=ALU.mult, op1=ALU.add), reads=[pSb, mb, v4b], writes=[rcb])
                        P.add('pe', lambda e, pS=pS, ybf=ybf, rc=rc: e.matmul(pS[:, 128:256], ybf[:], rc[:], start=True, stop=True), reads=[ybb_, rcb], writes=[pSb])
                        vn, vnb = vnew.next()
                        P.add('act', lambda e, vn=vn, pS=pS: e.copy(vn[:], pS[:, 128:256]), reads=[pSb], writes=[vnb])
                        P.add('pe', lambda e, po=po, hs=hs, at=at, vn=vn, ci=ci: e.matmul(po[:, hs], at[:], vn[:], start=False, stop=(ci == 3)),
                              reads=[atb, vnb], writes=[pob])
                        P.add('pe', lambda e, pS=pS, kdt=kdt, hs=hs, vn=vn: e.matmul(pS[:, 256:384], kdt[:, hs], vn[:], start=True, stop=True), reads=[kdb, vnb], writes=[pSb])
                        P.add('dve', lambda e, pS=pS, h=h, l4=l4, c=c: e.scalar_tensor_tensor(
                            out=S[:, h, :], in0=S[:, h, :], scalar=l4[:, c:c + 1], in1=pS[:, 256:384], op0=ALU.mult, op1=ALU.add),
                            reads=[pSb, l4b, b_S[h]], writes=[b_S[h]])
                        P.add('act', lambda e, h=h: e.copy(Sb[:, h, :], S[:, h, :]), reads=[b_S[h]], writes=[b_Sb[h]])
                if d == 0:
                    o, ob = ost.next()
                    P.add('act', lambda e, o=o, po=po: e.copy(o[:], po[:]), reads=[pob], writes=[ob])
                    P.dma(self.o1dn_d[r0:r0 + 128, :], o[:], reads=[ob], writes=[self.dbuf(('o1dn', l))])
                else:
                    o1, o1b = o1t.next()
                    P.dma(o1[:], self.o1dn_d[r0:r0 + 128, :], reads=[self.dbuf(('o1dn', l))], writes=[o1b])
                    o, ob = ost.next()
                    P.add('dve', lambda e, o=o, po=po, o1=o1: e.tensor_tensor(out=o[:], in0=po[:], in1=o1[:], op=ALU.add), reads=[pob, o1b], writes=[ob])
                    self.norm_gate_emit(l, o, ob, z[:], zb, gain, b_par, sq, b_sq, ss, b_ss, gg, b_gg, yb, yst, 1536, r0)

    def fake_mixer(self, l):
        P = self.P
        P.begin_phase()
        T = self.T
        st = Pool([(P.sbuf(f"fk{i}", [128, T], F32), Buf()) for i in range(2)])
        sb = Pool([(P.sbuf(f"fkb{i}", [128, T], BF16), Buf()) for i in range(2)])
        rows = [C_HG_Q + i * 128 for i in range(12)] + [C_DN_QKV + i * 128 for i in range(4)]
        for i, r0 in enumerate(rows):
            s, sbf = st.next()
            b, bbf = sb.next()
            P.dma(s[:], self.UT[r0:r0 + 128, :], reads=[self.dbuf(('UT', l))], writes=[sbf])
            P.add('dve', lambda e, s=s, b=b: e.tensor_copy(b[:], s[:]), reads=[sbf], writes=[bbf])
            P.dma(self.yT[i * 128:(i + 1) * 128, :], b[:], reads=[bbf], writes=[self.dbuf(('yT', l))])

    def dump_debug(self):
        P = self.P
        if 'UT' in self.debug:
            P.barrier()
            P.dma(self.dbg_UT, self.UT, reads=[self.dbuf(('UT', 0))])
            P.dma(self.dbg_Utok, self.Utok, reads=[self.dbuf(('Utok', 0))])

    def build(self):
        self.setup()
        self.compute_mod()
        self.convert_weights()
        if not self.couple:
            self.zero_halos()
        if 'hg' in self.mixers:
            self.compute_lb()
        for l in range(self.depth):
            x_src = self.xT_in if l == 0 else self.xs
            x_dst = self.yT_out if l == self.depth - 1 else self.xs
            self.phase_A(l, x_src)
            if 'UT' in self.debug and l == 0:
                self.dump_debug()
            if 'fake' in self.mixers:
                self.fake_mixer(l)
            else:
                self.phase_B1(l)
                if self.couple:
                    self.exchange_halos(l)
                zr = []
                if 'swa' in self.mixers:
                    self.phase_SWA(l)
                else:
                    zr += [i * 128 for i in range(0, 4)]
                if 'hg' in self.mixers:
                    self.phase_HG(l)
                else:
                    zr += [i * 128 for i in range(4, 8)]
                if 'na' in self.mixers:
                    self.phase_NA(l)
                else:
                    zr += [i * 128 for i in range(8, 12)]
                if 'dn' in self.mixers:
                    self.phase_DN(l)
                else:
                    zr += [i * 128 for i in range(12, 16)]
                if zr:
                    self.zero_y(l, zr)
            self.phase_C(l, x_src, x_dst)
        self.P.finalize()
        return self.nc


def make_consts():
    c = np.zeros((128, NCONST, 128), np.float32)
    c[:, 0, :] = np.eye(128)
    c[:, 1, :] = 1.0
    t = np.arange(128)[:, None]
    i = np.arange(128)[None, :]
    same = (t // 32) == (i // 32)
    c[:, 2, :] = same & (t > i)
    c[:, 3, :] = same & (t <= i)
    c[:, 5, :] = same & (t <= i)
    c[:, 6, :] = same & (t < i)
    c[:, 7, :] = same & (t >= i)
    c[:, 9, :] = same & (t >= i)
    for cc in range(4):
        c[:, 16 + cc, :] = ((np.arange(128) // 32) == cc)[:, None]
    c[:, 20, :] = -np.eye(128)
    same = (t // 32) == (i // 32)
    BIG = 30000.0
    c[:, 10, :] = np.where(same & (t < i), 0.0, BIG)
    c[:, 11, :] = np.where(same & (t <= i), 0.0, BIG)
    c[:, 12, :] = np.where(same & (t > i), 0.0, BIG)
    c[:, 13, :] = np.where(same & (t >= i), 0.0, BIG)
    c[:, 14, :] = same
    c[:, 15, 0:64] = (t < 64)
    c[:, 15, :] = 0.0
    c[:, 15, 0] = (np.arange(128) < 64)
    c[:, 15, 1] = (np.arange(128) >= 64)
    return c


def arrange_vec(v):
    n = v.shape[-1] // 128
    return np.ascontiguousarray(np.swapaxes(v.reshape(*v.shape[:-1], n, 128), -1, -2))


NEG = -30000.0


def rope_tables(pos):
    T = len(pos)
    half = 16
    inv_freq = np.power(np.float32(500000.0), -np.arange(half, dtype=np.float32) / np.float32(half)).astype(np.float32)
    ang = pos.astype(np.float32)[:, None] * inv_freq[None, :]
    cs = np.stack([np.cos(ang), np.sin(ang)], axis=1).astype(np.float32)
    return np.ascontiguousarray(cs.reshape(T // 128, 128, 2, 16).transpose(1, 0, 2, 3))


def swa_masks(coupled):
    p = np.arange(128)[:, None]
    c = np.arange(128)[None, :]
    m = np.full((128, 4, 128), NEG, np.float32)
    ai = np.abs(p - 64 - c) <= 64
    m[:, 0, :] = np.where(ai & (p >= 64), 0.0, NEG)
    m[:, 1, :] = np.where(ai, 0.0, NEG)
    bi = np.abs(p + 64 - c) <= 64
    m[:, 2, :] = np.where(bi, 0.0, NEG)
    bl = np.where(p < 64, bi, ((p - 64) + c >= 127) & bool(coupled))
    m[:, 3, :] = np.where(bl, 0.0, NEG)
    return m


def na_bias_mats(rpb, tok_own, tok_partner, Ls, T):
    rows = Ls // 64
    J = T // 128
    jm = J // 2
    specs = [(0, [('o', 0), ('o', 1), ('o', 2), ('o', 3)]),
             (1, [('o', 0), ('o', 1), ('o', 2), ('o', 3)]),
             (jm, [('o', jm - 2), ('o', jm - 1), ('o', jm), ('o', jm + 1), ('o', jm + 2)]),
             (J - 2, [('o', J - 4), ('o', J - 3), ('o', J - 2), ('o', J - 1), ('h', 0)]),
             (J - 1, [('o', J - 4), ('o', J - 3), ('o', J - 2), ('o', J - 1), ('h', 0), ('h', 1)])]
    out = np.full((4, 128, 24, 128), NEG, np.float32)
    idx = 0
    for (j, keys) in specs:
        tq = tok_own[j * 128:(j + 1) * 128]
        rq, cq = tq // 64, tq % 64
        r0 = np.clip(rq - 4, 0, rows - 8)
        c0 = np.clip(cq - 8, 0, 64 - 16)
        for (kind, kt) in keys:
            if kind == 'o':
                tk = tok_own[kt * 128:(kt + 1) * 128]
            elif tok_partner is not None:
                pt = J - 1 - kt
                tk = tok_partner[pt * 128:(pt + 1) * 128]
            else:
                tk = None
            if tk is not None:
                rk, ck = tk // 64, tk % 64
                valid = ((rk[:, None] >= r0[None, :]) & (rk[:, None] < r0[None, :] + 8) &
                         (ck[:, None] >= c0[None, :]) & (ck[:, None] < c0[None, :] + 16))
                ro = np.clip(rk[:, None] - rq[None, :] + 7, 0, 14)
                co = np.clip(ck[:, None] - cq[None, :], -15, 15) + 15
                for h in range(4):
                    g = rpb[h][ro, co]
                    out[h, :, idx, :] = np.where(valid, g, NEG)
            idx += 1
    assert idx == 24
    return out


def rep128(v):
    return np.ascontiguousarray(np.broadcast_to(v[None, :], (128, v.shape[0])))


def extra_inputs(inp, L, T, tok, rev, sel):
    lg = np.asarray(inp['hgrn_lb_logits'])[:L]
    if rev:
        lg = lg[:, ::-1]
    hg_lb = np.ascontiguousarray(np.broadcast_to(lg[None], (128, L, 2, 512))).astype(np.float32)
    hg_lbT = np.ascontiguousarray(lg.reshape(L, 2, 4, 128).transpose(3, 0, 1, 2)).astype(np.float32)
    hg_gain = np.stack([rep128(np.tile(np.asarray(inp['hgrn_norm_g'])[l], 4)) for l in range(L)]).astype(np.float32)
    flag = np.ascontiguousarray(np.broadcast_to(np.asarray(sel, np.float32)[None, :], (128, 2)))
    return dict(hg_lb=hg_lb, hg_lbT=hg_lbT, hg_gain=hg_gain, flag=flag)


def dn_inputs(inp, L, rev):
    cw = np.asarray(inp['dn_conv_w'])[:L]
    if rev:
        cw = cw[:, ::-1]
    dn_conv = np.ascontiguousarray(cw.reshape(L, 5, 12, 128).transpose(0, 3, 2, 1)).astype(np.float32)
    al = np.asarray(inp['dn_a_log'])[:L]
    db = np.asarray(inp['dn_dt_bias'])[:L]
    if rev:
        al, db = al[:, ::-1], db[:, ::-1]
    par = np.stack([al, db], axis=1)
    dn_par = np.ascontiguousarray(np.broadcast_to(par[:, None], (L, 128, 2, 2, 4))).astype(np.float32)
    dn_gain = np.stack([rep128(np.tile(np.asarray(inp['dn_norm_g'])[l], 4)) for l in range(L)]).astype(np.float32)
    return dict(dn_conv=dn_conv, dn_par=dn_par, dn_gain=dn_gain)


W_PERM_CACHE = {}


def permute_w_in(w_in):
    idx = np.arange(INW)
    idx[C_HG_F1:C_HG_F1 + 512], idx[C_HG_F2:C_HG_F2 + 512] = np.arange(C_HG_F2, C_HG_F2 + 512), np.arange(C_HG_F1, C_HG_F1 + 512)
    for base in (C_DN_AB, C_DN_AB + 8):
        idx[base:base + 4], idx[base + 4:base + 8] = np.arange(base + 4, base + 8), np.arange(base, base + 4)
    return np.ascontiguousarray(w_in[:, :, idx])


def run_model(seqs, T, depth, inp, n_cores, mixers=('swa', 'hg', 'na', 'dn'), trace=False):
    L = depth
    roles = []
    for si, (x, c) in enumerate(seqs):
        Ls = x.shape[0]
        if Ls == 2 * T:
            if len(roles) % 2:
                roles.append(None)
            t0 = np.arange(T)
            t1 = 2 * T - 1 - np.arange(T)
            roles.append(dict(si=si, tok=t0, ptok=t1, rev=False, sel=(0.0, 1.0), Ls=Ls))
            roles.append(dict(si=si, tok=t1, ptok=t0, rev=True, sel=(1.0, 0.0), Ls=Ls))
        else:
            assert Ls == T
            roles.append(dict(si=si, tok=np.arange(T), ptok=None, rev=False, sel=(0.0, 0.0), Ls=Ls))
    assert len(roles) <= n_cores
    real = [r for r in roles if r is not None]
    k = 0
    while len(roles) < n_cores:
        roles.append(dict(real[k % len(real)], dup=True) if True else None)
        k += 1
    roles = [r if r is not None else dict(real[0], dup=True) for r in roles]
    import os
    b = Builder(T, depth=L, mixers=mixers, couple=(os.environ.get('COUPLE', '1') == '1'), n_cores=n_cores)
    nc = b.build()
    w_in = np.asarray(inp['w_in'])[:L]
    w_in_rev = permute_w_in(w_in) if any(r['rev'] for r in roles) else None
    shared = dict(ada_w=np.asarray(inp['ada_w'])[:L], ada_b=arrange_vec(np.asarray(inp['ada_b'])[:L]),
                  ng=np.stack([arrange_vec(np.asarray(inp['norm1_g'])[:L]), arrange_vec(np.asarray(inp['norm2_g'])[:L])], axis=1),
                  w_out=np.asarray(inp['w_out'])[:L], w1=np.asarray(inp['w_mlp_in'])[:L], w2=np.asarray(inp['w_mlp_out'])[:L],
                  consts=make_consts(),
                  gains=np.stack([np.stack([rep128(np.tile(np.asarray(inp[kk])[l], 4)) for kk in ('swa_q_norm', 'swa_k_norm', 'na_q_norm', 'na_k_norm')])
                                  for l in range(L)]).astype(np.float32))
    in_maps = []
    cache = {}
    for r in roles:
        key = (r['si'], r['rev'])
        if key in cache:
            in_maps.append(cache[key])
            continue
        x, c = seqs[r['si']]
        m = dict(shared)
        m['xT'] = np.ascontiguousarray(np.asarray(x)[r['tok']].T)
        m['cvec'] = arrange_vec(np.asarray(c))
        m['w_in'] = w_in_rev if r['rev'] else w_in
        m['rope'] = rope_tables(r['tok'])
        m['swa_mask'] = swa_masks(r['ptok'] is not None)
        m['na_bias'] = np.stack([na_bias_mats(np.asarray(inp['na_rpb'])[l], r['tok'], r['ptok'], r['Ls'], T) for l in range(L)])
        m.update(extra_inputs(inp, L, T, r['tok'], r['rev'], r['sel']))
        m.update(dn_inputs(inp, L, r['rev']))
        cache[key] = m
        in_maps.append(m)
    res = run_bass_kernel_spmd(nc, in_maps, core_ids=list(range(n_cores)), **({'trace': True} if trace else {}))
    outs = [np.zeros((x.shape[0], D), np.float32) for (x, c) in seqs]
    for ci, r in enumerate(roles):
        if r.get('dup'):
            continue
        y = np.asarray(res.results[ci]['yT']).T
        outs[r['si']][r['tok']] = y
    return outs, res


def kernel(x_prompt, x_sample, c_prompt, c_sample, **w):
    x_prompt, x_sample = np.asarray(x_prompt), np.asarray(x_sample)
    c_prompt, c_sample = np.asarray(c_prompt), np.asarray(c_sample)
    T = 8192
    seqs = [(x_sample[0], c_sample[0]), (x_sample[1], c_sample[1]), (x_prompt[0], c_prompt[0]), (x_prompt[1], c_prompt[1])]
    outs, _ = run_model(seqs, T, DEPTH, w, 8)
    y_sample = np.stack([outs[0], outs[1]]).astype(np.float32)
    y_prompt = np.stack([outs[2], outs[3]]).astype(np.float32)
    return (y_prompt, y_sample)
```
